# Optimizing a Trainium2 kernel written in Bass

```python
import math
import jax
import jax.numpy as jnp
from jax import lax
import numpy as np

D_MODEL = 1024
BATCH = 8
SEQ = 2048
DEPTH = 2

CTX_LEN = 256
GRID_W = 64
EPS = 1e-6
N_MOD = 6

SSD_HEADS = 16
SSD_HEAD_DIM = 64
SSD_WIDTH = SSD_HEADS * SSD_HEAD_DIM
SSD_GROUPS = 4
SSD_STATE = 128
SSD_CONV = 3
SSD_CHUNK = 128
SSD_BC_WIDTH = SSD_GROUPS * SSD_STATE
SSD_CONV_CH = SSD_WIDTH + 2 * SSD_BC_WIDTH
SSD_DT_MIN = 1e-3
SSD_DT_MAX = 1e-1

ATT_HEADS = 8
ATT_HEAD_DIM = 64
ATT_V_DIM = 2 * ATT_HEAD_DIM
ATT_QK_WIDTH = ATT_HEADS * 2 * ATT_HEAD_DIM
ATT_WIDTH = ATT_HEADS * ATT_V_DIM
ATT_SCALE = ATT_HEAD_DIM ** -0.5
Q_BLOCK = 128
ROPE_BASE = 10000.0
ROPE_PAIRS = ATT_HEAD_DIM // 4

N_BRANCH = 2
D_FF = 2816
FFN_CONV = 3

IN_SIZES = (SSD_WIDTH, SSD_CONV_CH, 2 * SSD_HEADS, ATT_QK_WIDTH, ATT_QK_WIDTH, ATT_WIDTH, N_BRANCH * D_MODEL)
IN_WIDTH = SSD_WIDTH + SSD_CONV_CH + 2 * SSD_HEADS + 2 * ATT_QK_WIDTH + ATT_WIDTH + N_BRANCH * D_MODEL

kernel_name = 'hybrid_ssd_diffattn_dit_block'


def rmsnorm(x, w):
    xf = x.astype(jnp.float32)
    y = xf * lax.rsqrt(jnp.mean(xf * xf, axis=-1, keepdims=True) + EPS)
    return (y * w.astype(jnp.float32)).astype(x.dtype)


def modulate(h, shift, scale):
    return h * (1.0 + scale) + shift


def dwconv_centred(x, w, b):
    k_w = w.shape[0]
    pad = k_w // 2
    n = x.shape[1]
    xp = jnp.pad(x, ((0, 0), (pad, pad), (0, 0)))
    y = xp[:, 0:n] * w[0]
    for k in range(1, k_w):
        y = y + xp[:, k:k + n] * w[k]
    return y + b


def split_in(u):
    points = []
    acc = 0
    for s in IN_SIZES[:-1]:
        acc += s
        points.append(acc)
    return jnp.split(u, points, axis=-1)


def segsum(a):
    t = a.shape[-1]
    cs = jnp.cumsum(a, axis=-1)
    diff = cs[..., :, None] - cs[..., None, :]
    mask = jnp.tril(jnp.ones((t, t), dtype=bool))
    return jnp.where(mask, diff, -jnp.inf)


def ssd_scan(xs, dt, a_head, b_mat, c_mat, h0):
    bsz, n, n_heads, p = xs.shape
    g, s_dim = b_mat.shape[-2:]
    r = n_heads // g
    nc = n // SSD_CHUNK
    xd = (xs * dt[..., None]).reshape(bsz, nc, SSD_CHUNK, g, r, p)
    a = (dt * a_head).reshape(bsz, nc, SSD_CHUNK, g, r).transpose(0, 3, 4, 1, 2)
    a_cs = jnp.cumsum(a, axis=-1)
    bc = b_mat.reshape(bsz, nc, SSD_CHUNK, g, s_dim)
    cc = c_mat.reshape(bsz, nc, SSD_CHUNK, g, s_dim)
    cb = jnp.einsum('bclgn,bcsgn->bgcls', cc, bc)
    decay_in = jnp.exp(segsum(a))
    y_diag = jnp.einsum('bgcls,bgrcls,bcsgrp->bclgrp', cb, decay_in, xd)
    decay_to_end = jnp.exp(a_cs[..., -1:] - a_cs)
    states = jnp.einsum('bclgn,bgrcl,bclgrp->bcgrpn', bc, decay_to_end, xd)
    states = jnp.concatenate([h0.reshape(bsz, 1, g, r, p, s_dim).astype(states.dtype), states], axis=1)
    chunk_tot = jnp.pad(a_cs[..., -1], ((0, 0), (0, 0), (0, 0), (1, 0)))
    decay_chunk = jnp.exp(segsum(chunk_tot))
    states = jnp.einsum('bgrzc,bcgrpn->bzgrpn', decay_chunk, states)
    h_in, h_last = states[:, :-1], states[:, -1]
    y_off = jnp.einsum('bclgn,bcgrpn,bgrcl->bclgrp', cc, h_in, jnp.exp(a_cs))
    y = (y_diag + y_off).reshape(bsz, n, n_heads, p)
    return y, h_last.reshape(bsz, n_heads, p, s_dim)


def ssd_branch(xbc_l, dt_raw_l, z_l, xbc_c, dt_raw_c, z_c, conv_w, conv_b, a_log, dt_bias, d_skip, norm_w, with_ctx):
    a_dir = -jnp.exp(a_log.astype(jnp.float32))
    dt_b = dt_bias.astype(jnp.float32)

    def prep(xbc, dt_raw):
        bsz, n, _ = xbc.shape
        xbc = jax.nn.silu(dwconv_centred(xbc, conv_w, conv_b))
        xs, b_mat, c_mat = jnp.split(xbc, [SSD_WIDTH, SSD_WIDTH + SSD_BC_WIDTH], axis=-1)
        xs = xs.reshape(bsz, n, SSD_HEADS, SSD_HEAD_DIM)
        b_mat = b_mat.reshape(bsz, n, SSD_GROUPS, SSD_STATE)
        c_mat = c_mat.reshape(bsz, n, SSD_GROUPS, SSD_STATE)
        dt = jax.nn.softplus(dt_raw.astype(jnp.float32).reshape(bsz, n, 2, SSD_HEADS) + dt_b)
        return xs, b_mat, c_mat, dt

    def flip(t):
        return jnp.flip(t, axis=1)

    xs_c, b_c, c_c, dt_c = prep(xbc_c, dt_raw_c)
    xs_l, b_l, c_l, dt_l = prep(xbc_l, dt_raw_l)
    bsz = xs_l.shape[0]
    h0 = jnp.zeros((bsz, SSD_HEADS, SSD_HEAD_DIM, SSD_STATE), jnp.float32)
    yf_c, hf_c = ssd_scan(xs_c, dt_c[:, :, 0], a_dir[0], b_c, c_c, h0)
    yb_c, hb_c = ssd_scan(flip(xs_c), flip(dt_c[:, :, 1]), a_dir[1], flip(b_c), flip(c_c), h0)
    yf_l, _ = ssd_scan(xs_l, dt_l[:, :, 0], a_dir[0], b_l, c_l, hf_c)
    yb_l, _ = ssd_scan(flip(xs_l), flip(dt_l[:, :, 1]), a_dir[1], flip(b_l), flip(c_l), hb_c)

    def readout(xs, yf, yb_rev, z):
        bsz_, n, _, _ = xs.shape
        y = yf + flip(yb_rev) + d_skip.astype(jnp.float32)[:, None] * xs
        y = y.reshape(bsz_, n, SSD_WIDTH).astype(z.dtype) * jax.nn.silu(z)
        return rmsnorm(y, norm_w)

    y_l = readout(xs_l, yf_l, yb_l, z_l)
    y_c = readout(xs_c, yf_c, yb_c, z_c) if with_ctx else None
    return y_l, y_c


def rope_2d_tables(n_tokens):
    rows = n_tokens // GRID_W
    inv_freq = ROPE_BASE ** (-jnp.arange(ROPE_PAIRS, dtype=jnp.float32) / ROPE_PAIRS)
    row_pos = jnp.arange(rows, dtype=jnp.float32)
    col_pos = jnp.arange(GRID_W, dtype=jnp.float32)
    ang_r = jnp.broadcast_to(row_pos[:, None, None] * inv_freq, (rows, GRID_W, ROPE_PAIRS))
    ang_c = jnp.broadcast_to(col_pos[None, :, None] * inv_freq, (rows, GRID_W, ROPE_PAIRS))
    ang = jnp.stack([ang_r, ang_c], axis=2).reshape(n_tokens, 2, ROPE_PAIRS)
    return jnp.cos(ang), jnp.sin(ang)


def apply_rope_2d(x, cos, sin):
    xs = x.astype(jnp.float32).reshape(*x.shape[:-1], 2, 2, ROPE_PAIRS)
    x1, x2 = xs[..., 0, :], xs[..., 1, :]
    c = cos[None, :, None, None]
    s = sin[None, :, None, None]
    out = jnp.stack([x1 * c - x2 * s, x2 * c + x1 * s], axis=-2)
    return out.reshape(x.shape).astype(x.dtype)


def diff_softmax_attend(q, k, v, lam):
    s = jnp.einsum('bqhcd,bkhcd->bhcqk', q, k).astype(jnp.float32) * ATT_SCALE
    p = jax.nn.softmax(s, axis=-1)
    p_diff = p[:, :, 0] - lam * p[:, :, 1]
    return jnp.einsum('bhqk,bkhe->bqhe', p_diff.astype(v.dtype), v)


def diff_attn_branch(q_l, k_l, v_l, q_c, k_c, v_c, cos, sin, lambdas, subln_w, layer_idx, with_ctx):
    bsz, n, _ = q_l.shape
    n_ctx = q_c.shape[1]
    lam_init = 0.8 - 0.6 * math.exp(-0.3 * layer_idx)
    lf = lambdas.astype(jnp.float32)
    lam = jnp.exp(jnp.sum(lf[0] * lf[1])) - jnp.exp(jnp.sum(lf[2] * lf[3])) + lam_init
    qk_shape = (ATT_HEADS, 2, ATT_HEAD_DIM)
    q_l = apply_rope_2d(q_l.reshape(bsz, n, *qk_shape), cos, sin)
    k_l = apply_rope_2d(k_l.reshape(bsz, n, *qk_shape), cos, sin)
    v_l = v_l.reshape(bsz, n, ATT_HEADS, ATT_V_DIM)
    q_c = q_c.reshape(bsz, n_ctx, *qk_shape)
    k_c = k_c.reshape(bsz, n_ctx, *qk_shape)
    v_c = v_c.reshape(bsz, n_ctx, ATT_HEADS, ATT_V_DIM)
    k_all = jnp.concatenate([k_c, k_l], axis=1)
    v_all = jnp.concatenate([v_c, v_l], axis=1)
    nb = n // Q_BLOCK
    q_blocks = q_l.reshape(bsz, nb, Q_BLOCK, *qk_shape).swapaxes(0, 1)
    o_l = lax.map(lambda qb: diff_softmax_attend(qb, k_all, v_all, lam), q_blocks)
    o_l = o_l.swapaxes(0, 1).reshape(bsz, n, ATT_HEADS, ATT_V_DIM)

    def readout(o):
        return (rmsnorm(o, subln_w) * (1.0 - lam_init)).reshape(o.shape[0], o.shape[1], ATT_WIDTH)

    y_l = readout(o_l)
    y_c = readout(diff_softmax_attend(q_c, k_c, v_c, lam)) if with_ctx else None
    return y_l, y_c


def branch_merge(y_s, y_a, gates, w_br_s, w_br_a, w_o):
    g_s, g_a = jnp.split(jax.nn.sigmoid(gates), N_BRANCH, axis=-1)
    return (g_s * (y_s @ w_br_s) + g_a * (y_a @ w_br_a)) @ w_o


def conv_ffn(h, w_u, conv_w, conv_b, w_d):
    u = dwconv_centred(h @ w_u, conv_w, conv_b)
    a, v = jnp.split(u, 2, axis=-1)
    return (jax.nn.silu(a) * v) @ w_d


def setup_inputs(seed: int = 0) -> dict:
    key = jax.random.key(seed)
    ks = jax.random.split(key, 26)
    f32 = jnp.float32

    def nrm(k, shape, scale):
        return jax.random.normal(k, shape, f32) * scale

    x = nrm(ks[0], (BATCH, SEQ, D_MODEL), 1.0)
    c = nrm(ks[1], (BATCH, D_MODEL), 1.0)
    ctx = nrm(ks[2], (BATCH, CTX_LEN, D_MODEL), 1.0)
    c_ctx = nrm(ks[3], (D_MODEL,), 1.0)
    w_mod = nrm(ks[4], (DEPTH, D_MODEL, N_MOD * D_MODEL), 0.5 * D_MODEL ** -0.5)
    b_mod = nrm(ks[5], (DEPTH, N_MOD * D_MODEL), 0.02)
    norm1_w = 1.0 + nrm(ks[6], (DEPTH, D_MODEL), 0.02)
    w_in = nrm(ks[7], (DEPTH, D_MODEL, IN_WIDTH), D_MODEL ** -0.5)
    ssd_conv_w = nrm(ks[8], (DEPTH, SSD_CONV, SSD_CONV_CH), SSD_CONV ** -0.5)
    ssd_conv_b = nrm(ks[9], (DEPTH, SSD_CONV_CH), 0.02)
    ssd_a_log = jnp.log(jax.random.uniform(ks[10], (DEPTH, 2, SSD_HEADS), f32, 1.0, 16.0))
    u_dt = jax.random.uniform(ks[11], (DEPTH, 2, SSD_HEADS), f32)
    dt0 = jnp.exp(u_dt * (math.log(SSD_DT_MAX) - math.log(SSD_DT_MIN)) + math.log(SSD_DT_MIN))
    ssd_dt_bias = dt0 + jnp.log(-jnp.expm1(-dt0))
    ssd_d = 1.0 + nrm(ks[12], (DEPTH, SSD_HEADS), 0.1)
    ssd_norm_w = 1.0 + nrm(ks[13], (DEPTH, SSD_WIDTH), 0.02)
    diff_lambda = nrm(ks[14], (DEPTH, 4, ATT_HEAD_DIM), 0.1)
    att_subln_w = 1.0 + nrm(ks[15], (DEPTH, ATT_V_DIM), 0.02)
    w_br_ssd = nrm(ks[16], (DEPTH, SSD_WIDTH, D_MODEL), SSD_WIDTH ** -0.5)
    w_br_att = nrm(ks[17], (DEPTH, ATT_WIDTH, D_MODEL), ATT_WIDTH ** -0.5)
    w_out = nrm(ks[18], (DEPTH, D_MODEL, D_MODEL), D_MODEL ** -0.5)
    norm2_w = 1.0 + nrm(ks[19], (DEPTH, D_MODEL), 0.02)
    w_up = nrm(ks[20], (DEPTH, D_MODEL, 2 * D_FF), D_MODEL ** -0.5)
    ffn_conv_w = nrm(ks[21], (DEPTH, FFN_CONV, 2 * D_FF), FFN_CONV ** -0.5)
    ffn_conv_b = nrm(ks[22], (DEPTH, 2 * D_FF), 0.02)
    w_down = nrm(ks[23], (DEPTH, D_FF, D_MODEL), D_FF ** -0.5)
    final_norm_w = 1.0 + nrm(ks[24], (D_MODEL,), 0.02)
    return {'x': x, 'c': c, 'ctx': ctx, 'c_ctx': c_ctx, 'w_mod': w_mod, 'b_mod': b_mod,
            'norm1_w': norm1_w, 'w_in': w_in, 'ssd_conv_w': ssd_conv_w, 'ssd_conv_b': ssd_conv_b,
            'ssd_a_log': ssd_a_log, 'ssd_dt_bias': ssd_dt_bias, 'ssd_d': ssd_d, 'ssd_norm_w': ssd_norm_w,
            'diff_lambda': diff_lambda, 'att_subln_w': att_subln_w, 'w_br_ssd': w_br_ssd,
            'w_br_att': w_br_att, 'w_out': w_out, 'norm2_w': norm2_w, 'w_up': w_up,
            'ffn_conv_w': ffn_conv_w, 'ffn_conv_b': ffn_conv_b, 'w_down': w_down,
            'final_norm_w': final_norm_w}


def reference(x, c, ctx, c_ctx, w_mod, b_mod, norm1_w, w_in, ssd_conv_w, ssd_conv_b, ssd_a_log,
              ssd_dt_bias, ssd_d, ssd_norm_w, diff_lambda, att_subln_w, w_br_ssd, w_br_att, w_out,
              norm2_w, w_up, ffn_conv_w, ffn_conv_b, w_down, final_norm_w):
    n = x.shape[1]
    cos, sin = rope_2d_tables(n)
    xl, xc = x, ctx
    for i in range(DEPTH):
        with_ctx = i < DEPTH - 1
        mod_l = (jax.nn.silu(c) @ w_mod[i] + b_mod[i])[:, None, :]
        mod_c = (jax.nn.silu(c_ctx) @ w_mod[i] + b_mod[i])[None, None, :]
        sh1_l, sc1_l, g1_l, sh2_l, sc2_l, g2_l = jnp.split(mod_l, N_MOD, axis=-1)
        sh1_c, sc1_c, g1_c, sh2_c, sc2_c, g2_c = jnp.split(mod_c, N_MOD, axis=-1)
        u_l = modulate(rmsnorm(xl, norm1_w[i]), sh1_l, sc1_l) @ w_in[i]
        u_c = modulate(rmsnorm(xc, norm1_w[i]), sh1_c, sc1_c) @ w_in[i]
        z_l, xbc_l, dt_l, q_l, k_l, v_l, gate_l = split_in(u_l)
        z_c, xbc_c, dt_c, q_c, k_c, v_c, gate_c = split_in(u_c)
        ys_l, ys_c = ssd_branch(xbc_l, dt_l, z_l, xbc_c, dt_c, z_c, ssd_conv_w[i], ssd_conv_b[i],
                                ssd_a_log[i], ssd_dt_bias[i], ssd_d[i], ssd_norm_w[i], with_ctx)
        ya_l, ya_c = diff_attn_branch(q_l, k_l, v_l, q_c, k_c, v_c, cos, sin, diff_lambda[i],
                                      att_subln_w[i], i, with_ctx)
        xl = xl + g1_l * branch_merge(ys_l, ya_l, gate_l, w_br_ssd[i], w_br_att[i], w_out[i])
        h_l = modulate(rmsnorm(xl, norm2_w[i]), sh2_l, sc2_l)
        xl = xl + g2_l * conv_ffn(h_l, w_up[i], ffn_conv_w[i], ffn_conv_b[i], w_down[i])
        if with_ctx:
            xc = xc + g1_c * branch_merge(ys_c, ya_c, gate_c, w_br_ssd[i], w_br_att[i], w_out[i])
            h_c = modulate(rmsnorm(xc, norm2_w[i]), sh2_c, sc2_c)
            xc = xc + g2_c * conv_ffn(h_c, w_up[i], ffn_conv_w[i], ffn_conv_b[i], w_down[i])
    return rmsnorm(xl, final_norm_w)
```

```python
import math
import os
XF = os.environ.get('KX', '')
from contextlib import ExitStack

import numpy as np
import concourse.bass as bass
import concourse.mybir as mybir
from concourse.bass_utils import run_bass_kernel_spmd

F32 = mybir.dt.float32
BF16 = mybir.dt.bfloat16
AF = mybir.ActivationFunctionType
ALU = mybir.AluOpType
AX = mybir.AxisListType

D = 1024
SEQ = 2048
CTX = 256
NTOK = SEQ + CTX
NT = NTOK // 128
DEPTH = 2
EPS = 1e-6
NH_SSD = 16
D_FF = 2816
NFF = D_FF // 128
IN_W = 8224
OFF_Z, OFF_XBC, OFF_DT, OFF_Q, OFF_K, OFF_V, OFF_G = 0, 1024, 3072, 3104, 4128, 5152, 6176
ATT_SCALE = 64 ** -0.5
BLOCKS = [(0, 256)] + [(256 + 512 * j, 512) for j in range(4)]

ENGS = ("pe", "act", "dve", "pool", "sp")


def _base(k):
    return k if isinstance(k, str) else k[0]


class Prog:
    NDMA = 8

    def __init__(self, nc, es):
        self.nc = nc
        self.lists = {e: [] for e in ENGS}
        self.cnt = {e: 0 for e in ENGS}
        self.sem = {e: es.enter_context(nc.semaphore("s_" + e)) for e in ENGS}
        self.waited = {e: {} for e in ENGS}
        self.last_w = {}
        self.readers = {}
        self.touch = {}
        self.inherit = {}
        self.seen = set()
        self.dsem = {}
        self.dcnt = {}
        self.drot = {q: 0 for q in ("sp", "act", "pool")}
        for q in ("sp", "act", "pool"):
            for i in range(self.NDMA):
                nm = "d_%s%d" % (q, i)
                self.dsem[nm] = es.enter_context(nc.semaphore(nm))
                self.dcnt[nm] = 0
        self.out_waits = {}
        self.nops = 0
        self.ps_last = {}

    def _semh(self, name):
        return self.sem[name] if name in self.sem else self.dsem[name]

    @staticmethod
    def _merge(d, e, s):
        if d.get(e, 0) < s:
            d[e] = s

    def _deps(self, reads, writes, eng=None):
        deps = {}
        for k in list(reads) + list(writes):
            if _base(k) == "ps":
                for e, s in self.ps_last.get(k, {}).items():
                    if e != eng:
                        self._merge(deps, e, s)
        for k in list(reads) + list(writes):
            if k not in self.seen:
                for e, s in self.inherit.get(_base(k), {}).items():
                    self._merge(deps, e, s)
        for k in reads:
            w = self.last_w.get(k)
            if w is not None:
                self._merge(deps, *w)
        for k in writes:
            w = self.last_w.get(k)
            if w is not None:
                self._merge(deps, *w)
            for e, s in self.readers.get(k, {}).items():
                self._merge(deps, e, s)
        return deps

    def _waits(self, eng, deps):
        out = []
        for e2, s2 in deps.items():
            if e2 == eng and eng == "pe":
                continue
            if self.waited[eng].get(e2, 0) >= s2:
                continue
            self.waited[eng][e2] = s2
            out.append((e2, s2))
        return out

    def _commit(self, tag, reads, writes):
        e, s = tag
        for k in reads:
            self.seen.add(k)
            self._merge(self.readers.setdefault(k, {}), e, s)
            self._merge(self.touch.setdefault(_base(k), {}), e, s)
        for k in writes:
            self.seen.add(k)
            self.last_w[k] = tag
            self.readers[k] = {}
            self._merge(self.touch.setdefault(_base(k), {}), e, s)
        for k in list(reads) + list(writes):
            if _base(k) == "ps":
                self.ps_last.setdefault(k, {})[e] = s

    def op(self, eng, emit, reads=(), writes=()):
        deps = self._deps(reads, writes, eng)
        waits = self._waits(eng, deps)
        self.cnt[eng] += 1
        self.lists[eng].append((waits, emit, (eng, 1)))
        self._commit((eng, self.cnt[eng]), reads, writes)
        self.nops += 1

    def dma(self, q, out, in_, reads=(), writes=(), is_output=False, **kw):
        deps = self._deps(reads, writes)
        nm = "d_%s%d" % (q, self.drot[q] % self.NDMA)
        self.drot[q] += 1
        if self.dcnt[nm] > 0:
            self._merge(deps, nm, self.dcnt[nm])
        waits = self._waits(q, deps)
        self.dcnt[nm] += 16
        seq = self.dcnt[nm]
        self.lists[q].append((waits, lambda e: e.dma_start(out=out, in_=in_, **kw), (nm, 16)))
        self._commit((nm, seq), reads, writes)
        if is_output:
            self._merge(self.out_waits, nm, seq)
        self.nops += 1

    def finish(self):
        self.lists["sp"].append((list(self.out_waits.items()), None, None))

    def emit(self):
        nc = self.nc
        with nc.Block() as block:
            def run(e, lst):
                for waits, emit, inc in lst:
                    for (nm, v) in waits:
                        e.wait_ge(self._semh(nm), v)
                    if emit is not None:
                        emit(e).then_inc(self._semh(inc[0]), inc[1])

            @block.tensor
            def _(e):
                run(e, self.lists["pe"])

            @block.scalar
            def _(e):
                run(e, self.lists["act"])

            @block.vector
            def _(e):
                run(e, self.lists["dve"])

            @block.gpsimd
            def _(e):
                run(e, self.lists["pool"])

            @block.sync
            def _(e):
                run(e, self.lists["sp"])


class Arena:
    def __init__(self, tensor, cap_bytes, prog):
        self.t = tensor
        self.cap = cap_bytes
        self.P = prog
        self.live = {}
        self.hist = []
        self.uid = 0
        self.peak = 0

    def alloc_at(self, name, shape, dt, off):
        esz = 4 if dt == F32 else 2
        n = int(np.prod(shape[1:]))
        size = (n * esz + 31) // 32 * 32
        assert off % 32 == 0
        assert off + size <= self.cap, "SBUF arena overflow at %s: %d > %d" % (name, off + size, self.cap)
        for k2, (o, s_) in self.live.items():
            assert not (o < off + size and off < o + s_), "overlap %s with live %s" % (name, k2)
        self.peak = max(self.peak, off + size)
        self.uid += 1
        key = "%s#%d" % (name, self.uid)
        inh = {}
        keep = []
        for (nm, o, s_) in self.hist:
            if o < off + size and off < o + s_:
                for e, q in self.P.touch.get(nm, {}).items():
                    Prog._merge(inh, e, q)
                for e, q in self.P.inherit.get(nm, {}).items():
                    Prog._merge(inh, e, q)
                if off <= o and o + s_ <= off + size:
                    continue
            keep.append((nm, o, s_))
        self.hist = keep
        self.P.inherit[key] = inh
        a = self.t[0:shape[0], off // 4:(off + size) // 4]
        if dt != F32:
            a = a.bitcast(dt)
        a = a[:, 0:n]
        if len(shape) == 3:
            a = a.rearrange("p (a b) -> p a b", a=shape[1])
        elif len(shape) == 4:
            a = a.rearrange("p (a b c) -> p a b c", a=shape[1], b=shape[2])
        self.live[key] = (off, size)
        return a, key, size

    def free(self, key):
        o, s_ = self.live.pop(key)
        self.hist.append((key, o, s_))


class Region:
    def __init__(self, arena, lo, hi):
        self.A, self.lo, self.hi = arena, lo, hi
        self.p = lo
        self.keys = []

    def alloc(self, name, shape, dt):
        a, key, size = self.A.alloc_at(name, shape, dt, self.p)
        self.p += size
        assert self.p <= self.hi, "region overflow at %s: %d > %d" % (name, self.p, self.hi)
        self.keys.append(key)
        return a, key

    def free_all(self):
        for k in self.keys:
            self.A.free(k)
        self.keys = []
        self.p = self.lo


def build_program(dbg=None):
    nc = bass.Bass("TRN2", target_bir_lowering=False)

    def din(name, shape):
        return nc.dram_tensor(name, list(shape), F32, kind="ExternalInput").ap()

    x_in = din("x_b", [SEQ, D])
    ctx_in = din("ctx_b", [CTX, D])
    cT_in = din("cT", [128, 16])
    w_mod = din("w_mod", [DEPTH, D, 6 * D])
    b_mod = din("b_mod", [DEPTH, 6 * D])
    bmodT_in = din("bmodT", [DEPTH, 128, 48])
    n1wT_in = din("n1wT", [DEPTH, 128, 8])
    n2wT_in = din("n2wT", [DEPTH, 128, 8])
    ssdnwT_in = din("ssdnwT", [DEPTH, 128, 8])
    sublnT_in = din("sublnT", [DEPTH, 128, 1])
    w_in = din("w_in", [DEPTH, D, IN_W])
    ssdcw_in = din("ssdcw", [DEPTH, 128, 16 * 3])
    ssdcb_in = din("ssdcb", [DEPTH, 128, 16])
    alog_in = din("alog", [DEPTH, 32])
    dtb_in = din("dtb", [DEPTH, 32])
    dskip_in = din("dskip", [DEPTH, 16])
    lam_in = din("lamv", [DEPTH, 256])
    w_brs = din("w_br_ssd", [DEPTH, D, D])
    w_bra = din("w_br_att", [DEPTH, D, D])
    w_out = din("w_out", [DEPTH, D, D])
    w_up = din("w_up", [DEPTH, D, 2 * D_FF])
    ffncw_in = din("ffncw", [DEPTH, 128, 2 * NFF * 3])
    ffncb_in = din("ffncb", [DEPTH, 128, 2 * NFF])
    w_down = din("w_down", [DEPTH, D_FF, D])
    fnw_in = din("fnw", [D])
    cos_in = din("ropecos", [128, SEQ])
    sin_in = din("ropesin", [128, SEQ])
    y_out = nc.dram_tensor("y", [SEQ, D], F32, kind="ExternalOutput").ap()
    xres = nc.dram_tensor("xres", [NTOK, D], F32).ap()
    dbg_out = None
    if dbg is not None:
        dbg_out = nc.dram_tensor("dbg", [128, dbg[1]], F32, kind="ExternalOutput").ap()

    es = ExitStack()
    with es:
        P = Prog(nc, es)
        CAP = 206 * 1024
        arena_t = es.enter_context(nc.sbuf_tensor("arena", [128, CAP // 4], F32))
        psum = es.enter_context(nc.psum_tensor("psum", [128, 4096], F32))
        A = Arena(arena_t, CAP, P)

        def PS(b, n=512, off=0):
            return psum[:, b * 512 + off: b * 512 + off + n]

        def PSK(b):
            return ("ps", b)

        def MM(out, lhsT, rhs, start, stop, r, w, skip=False):
            if skip:
                P.op("pe", lambda e: e.matmul(out, lhsT=lhsT, rhs=rhs, start=start, stop=stop, skip_group_check=True), r, w)
            else:
                P.op("pe", lambda e: e.matmul(out, lhsT=lhsT, rhs=rhs, start=start, stop=stop), r, w)

        def ACT(out, in_, func, r, w, **kw):
            P.op("act", lambda e: e.activation(out=out, in_=in_, func=func, **kw), r, w)

        def TT(eng, out, in0, in1, op, r, w):
            P.op(eng, lambda e: e.tensor_tensor(out=out, in0=in0, in1=in1, op=op), r, w)

        def TS(eng, out, in0, s1, s2, op0, op1, r, w):
            if op1 is None:
                P.op(eng, lambda e: e.tensor_scalar(out=out, in0=in0, scalar1=s1, scalar2=None, op0=op0), r, w)
            else:
                P.op(eng, lambda e: e.tensor_scalar(out=out, in0=in0, scalar1=s1, scalar2=s2, op0=op0, op1=op1), r, w)

        def STT(eng, out, in0, scalar, in1, op0, op1, r, w):
            P.op(eng, lambda e: e.scalar_tensor_tensor(out=out, in0=in0, scalar=scalar, in1=in1, op0=op0, op1=op1), r, w)

        def CP(eng, out, in_, r, w):
            P.op(eng, lambda e: e.tensor_copy(out=out, in_=in_), r, w)

        def MSET(eng, ap, val, w):
            P.op(eng, lambda e: e.memset(ap, val), (), w)

        def RECIP(out, in_, r, w):
            P.op("dve", lambda e: e.reciprocal(out=out, in_=in_), r, w)

        def TR(out, in_, r, w):
            P.op("pe", lambda e: e.transpose(out=out, in_=in_, identity=ident), list(r) + [k_ident], w)

        def WLOAD(dst, src, r, w):
            P.dma("pool", dst, src, r, w)

        def wsrc(wt, l, c0, n):
            return wt[l].rearrange("(kc p) n -> p kc n", p=128)[:, :, c0:c0 + n]

        BASE = 27 * 1024
        SZ = 36864
        R_P = Region(A, 0, BASE)
        R0 = Region(A, BASE, BASE + SZ)
        R1 = Region(A, BASE + SZ, BASE + 2 * SZ)
        R2 = Region(A, BASE + 2 * SZ, BASE + 3 * SZ)
        R3 = Region(A, BASE + 3 * SZ, CAP)
        R23 = Region(A, BASE + 2 * SZ, CAP)

        ident, k_ident = R_P.alloc("ident", [128, 128], BF16)
        identf, k_identf = R_P.alloc("identf", [128, 128], F32)
        tri, k_tri = R_P.alloc("tri", [128, 4, 128], F32)
        ones, k_ones = R_P.alloc("ones", [128, 128], F32)
        MSET("pool", identf, 0.0, [k_identf])
        P.op("pool", lambda e: e.affine_select(out=identf, in_=identf, pattern=[[-1, 128]], compare_op=ALU.not_equal,
                                               fill=1.0, base=0, channel_multiplier=1), [k_identf], [k_identf])
        CP("dve", ident, identf, [k_identf], [k_ident])
        MSET("pool", ones, 1.0, [k_ones])
        MSET("pool", tri, 1.0, [k_tri])
        P.op("pool", lambda e: e.affine_select(out=tri[:, 0, :], in_=tri[:, 0, :], pattern=[[1, 128]], compare_op=ALU.is_ge,
                                               fill=0.0, base=0, channel_multiplier=-1), [k_tri], [k_tri])
        P.op("pool", lambda e: e.affine_select(out=tri[:, 1, :], in_=tri[:, 1, :], pattern=[[-1, 128]], compare_op=ALU.is_ge,
                                               fill=0.0, base=0, channel_multiplier=1), [k_tri], [k_tri])
        P.op("pool", lambda e: e.affine_select(out=tri[:, 2, :], in_=tri[:, 2, :], pattern=[[-1, 128]], compare_op=ALU.is_gt,
                                               fill=0.0, base=0, channel_multiplier=1), [k_tri], [k_tri])
        P.op("pool", lambda e: e.affine_select(out=tri[:, 3, :], in_=tri[:, 3, :], pattern=[[1, 128]], compare_op=ALU.is_gt,
                                               fill=0.0, base=0, channel_multiplier=-1), [k_tri], [k_tri])

        permT, k_perm = R_P.alloc("permT", [128, 128], BF16)
        iv = ident.rearrange("p (a hf f) -> p a hf f", hf=2, f=16)
        pv = permT.rearrange("p (a hf f) -> p a hf f", hf=2, f=16)
        CP("pool", pv[:, :, 0, :], iv[:, :, 1, :], [k_ident], [(k_perm, 0)])
        CP("pool", pv[:, :, 1, :], iv[:, :, 0, :], [k_ident], [(k_perm, 1)])
        cT, k_cT = R_P.alloc("cT", [128, 8, 2], F32)
        scT, k_scT = R_P.alloc("scT", [128, 8, 2], BF16)
        screp, k_screp = R_P.alloc("screp", [128, 8, 2, 128], BF16)
        P.dma("sp", cT, cT_in.rearrange("p (k w) -> p k w", w=2), (), [k_cT])
        ACT(scT, cT, AF.Silu, [k_cT], [k_scT])
        CP("dve", screp, scT.unsqueeze(3).to_broadcast([128, 8, 2, 128]), [k_scT], [k_screp])
        p_mark = (R_P.p, len(R_P.keys))

        dbgbuf = es.enter_context(nc.sbuf_tensor("dbgbuf", [128, 2, 128], F32)) if dbg is not None else None

        def dump(ap, ncols, rkeys, reg=None):
            for i, c0 in enumerate(range(0, ncols, 128)):
                n = min(128, ncols - c0)
                CP("dve", dbgbuf[:, i % 2, 0:n], ap[:, c0:c0 + n], rkeys, [("dbgbuf", i % 2)])
                P.dma("sp", dbg_out[:, c0:c0 + n], dbgbuf[:, i % 2, 0:n], [("dbgbuf", i % 2)], [("dbg", i)], is_output=True)

        def want(stage):
            return dbg is not None and dbg[0] == stage

        def norm_front(xt, kx, n_feat, bufs, i):
            junk, kj, ss, kss, xn, kxn, ev, kev = bufs
            nkc = n_feat // 128
            kxl = list(kx) if isinstance(kx, list) else [kx]
            ACT(junk[:, i % 2, 0:n_feat], xt, AF.Square, kxl, [(kj, i % 2), (kss, i % 2, 0)], accum_out=ss[:, i % 2, 0:1])
            ACT(ss[:, i % 2, 1:2], ss[:, i % 2, 0:1], AF.Ln, [(kss, i % 2, 0)], [(kss, i % 2, 1)], scale=1.0 / n_feat, bias=EPS)
            ACT(ss[:, i % 2, 2:3], ss[:, i % 2, 1:2], AF.Exp, [(kss, i % 2, 1)], [(kss, i % 2, 2)], scale=-0.5)
            TS("dve", xn[:, i % 2, 0:n_feat], xt, ss[:, i % 2, 2:3], None, ALU.mult, None, kxl + [(kss, i % 2, 2)], [(kxn, i % 2)])
            b = 6 + (i % 2)
            pt = PS(b).bitcast(BF16)
            for kc in range(nkc):
                TR(pt[:, kc * 128:(kc + 1) * 128], xn[:, i % 2, kc * 128:(kc + 1) * 128], [(kxn, i % 2)], [PSK(b)])

        def norm_back(t, scale_ap, bias_ap, extra_r, dstT, kdst, bufs, i):
            junk, kj, ss, kss, xn, kxn, ev, kev = bufs
            b = 6 + (i % 2)
            pt = PS(b).bitcast(BF16).rearrange("p (k t) -> p k t", k=8)
            dst = dstT[:, :, t * 128:(t + 1) * 128]
            sc = scale_ap.unsqueeze(2).to_broadcast([128, 8, 128])
            if bias_ap is None:
                TT("dve", dst, pt, sc, ALU.mult, [PSK(b)] + list(extra_r), [(kdst, t)])
            else:
                TT("dve", ev[:, i % 2], pt, sc, ALU.mult, [PSK(b)] + list(extra_r), [(kev, i % 2)])
                TT("dve", dst, ev[:, i % 2], bias_ap.unsqueeze(2).to_broadcast([128, 8, 128]), ALU.add, [(kev, i % 2)] + list(extra_r), [(kdst, t)])

        def norm_bufs(reg):
            junk, kj = reg.alloc("junk", [128, 2, D], BF16)
            ss, kss = reg.alloc("ss", [128, 2, 4], F32)
            xn, kxn = reg.alloc("xn", [128, 2, D], BF16)
            ev, kev = reg.alloc("nev", [128, 2, 8, 128], F32)
            return (junk, kj, ss, kss, xn, kxn, ev, kev)

        for l in range(DEPTH):
            with_ctx = l < DEPTH - 1
            T0 = 0 if with_ctx else 2
            tiles_q = list(range(T0, NT))
            blocks_q = BLOCKS[(0 if with_ctx else 1):]
            lam_init = 0.8 - 0.6 * math.exp(-0.3 * l)

            def xsrc(t, l=l):
                if l == 0:
                    return ctx_in[t * 128:(t + 1) * 128, :] if t < 2 else x_in[(t - 2) * 128:(t - 1) * 128, :]
                return xres[t * 128:(t + 1) * 128, :]

            for k in R_P.keys[p_mark[1]:]:
                A.free(k)
            del R_P.keys[p_mark[1]:]
            R_P.p = p_mark[0]

            bmodT, k_bmodT = R_P.alloc("bmodT", [128, 48], F32)
            modT, k_modT = R_P.alloc("modT", [128, 48, 2], F32)
            scl1, k_scl1 = R_P.alloc("scl1", [128, 8, 2], F32)
            scl2, k_scl2 = R_P.alloc("scl2", [128, 8, 2], F32)
            n1w, k_n1w = R_P.alloc("n1w", [128, 8], F32)
            n2w, k_n2w = R_P.alloc("n2w", [128, 8], F32)
            g1bc, k_g1bc = R_P.alloc("g1bc", [128, 2, D], F32)
            g2bc, k_g2bc = R_P.alloc("g2bc", [128, 2, D], F32)
            P.dma("sp", bmodT, bmodT_in[l], (), [k_bmodT])
            P.dma("sp", n1w, n1wT_in[l], (), [k_n1w])
            P.dma("sp", n2w, n2wT_in[l], (), [k_n2w])
            bmrow, k_bmrow = R3.alloc("bmrow", [128, 2, D], F32)
            P.dma("sp", bmrow[:, 0, :], b_mod[l, 2 * D:3 * D].partition_broadcast(128), (), [(k_bmrow, 0)])
            P.dma("sp", bmrow[:, 1, :], b_mod[l, 5 * D:6 * D].partition_broadcast(128), (), [(k_bmrow, 1)])
            wm = [R3.alloc("wmod%d" % i, [128, 8, 512], BF16) for i in range(2)]
            hT, k_hT = R0.alloc("hT", [128, 8, NTOK], BF16)
            xts = [R3.alloc("xt%d" % i, [128, D], F32) for i in range(3)]
            nbufs = norm_bufs(R3)

            def mod_chunk(cb):
                wt, kw = wm[cb % 2]
                WLOAD(wt, wsrc(w_mod, l, cb * 512, 512), (), [kw])
                which = cb // 2
                if which in (2, 5):
                    gb, kg = (g1bc, k_g1bc) if which == 2 else (g2bc, k_g2bc)
                    hf = cb % 2
                    for w_ in range(2):
                        b = (cb * 2 + w_) % 4
                        for kc in range(8):
                            MM(PS(b), screp[:, kc, w_, :], wt[:, kc, :], kc == 0, kc == 7, [k_screp, kw], [PSK(b)])
                        TT("dve", gb[:, w_, hf * 512:(hf + 1) * 512], PS(b), bmrow[:, 0 if which == 2 else 1, hf * 512:(hf + 1) * 512],
                           ALU.add, [PSK(b), (k_bmrow, 0 if which == 2 else 1)], [(kg, w_, hf)])
                else:
                    for j4 in range(4):
                        j = cb * 4 + j4
                        b = 4 + (j % 2)
                        for kc in range(8):
                            MM(PS(b, 2), wt[:, kc, j4 * 128:(j4 + 1) * 128], scT[:, kc, :], kc == 0, kc == 7, [k_scT, kw], [PSK(b)])
                        TS("dve", modT[:, j, :], PS(b, 2), bmodT[:, j:j + 1], None, ALU.add, None, [PSK(b), k_bmodT], [(k_modT, j)])

            for cb in range(4):
                mod_chunk(cb)
            STT("dve", scl1, modT[:, 8:16, :], 1.0, n1w.unsqueeze(2).to_broadcast([128, 8, 2]), ALU.add, ALU.mult,
                [(k_modT, j) for j in range(8, 16)] + [k_n1w], [k_scl1])

            def n1_front(t):
                xt, kx = xts[t % 3]
                P.dma("sp", xt, xsrc(t), [("xres", t)], [kx])
                norm_front(xt, kx, D, nbufs, t)

            rest = list(range(4, 12))
            n1_front(0)
            for t in range(NT):
                if t + 1 < NT:
                    n1_front(t + 1)
                w_ = 1 if t < 2 else 0
                norm_back(t, scl1[:, :, w_], modT[:, 0:8, w_], [k_scl1] + [(k_modT, j) for j in range(8)], hT, k_hT, nbufs, t)
                if t % 2 == 1 and rest:
                    mod_chunk(rest.pop(0))
            while rest:
                mod_chunk(rest.pop(0))
            STT("dve", scl2, modT[:, 32:40, :], 1.0, n2w.unsqueeze(2).to_broadcast([128, 8, 2]), ALU.add, ALU.mult,
                [(k_modT, j) for j in range(32, 40)] + [k_n2w], [k_scl2])
            k_g1 = [(k_g1bc, w_, hf) for w_ in range(2) for hf in range(2)]
            k_g2 = [(k_g2bc, w_, hf) for w_ in range(2) for hf in range(2)]
            R3.free_all()
            hT_all = [(k_hT, t) for t in range(NT)]
            if want("hT%d" % l):
                dump(hT.rearrange("p a b -> p (a b)"), 8 * NTOK, hT_all, R3)
                break

            ybuf, k_ybuf = R1.alloc("ybuf", [128, NT, D], BF16)
            S = R23
            ssdcw, k_ssdcw = S.alloc("ssdcw", [128, 16, 3], F32)
            ssdcb, k_ssdcb = S.alloc("ssdcb", [128, 16], F32)
            alog, k_alog = S.alloc("alog", [128, 32], F32)
            dtb, k_dtb = S.alloc("dtb", [128, 32], F32)
            dsk, k_dsk = S.alloc("dsk", [128, 16], F32)
            Aneg, k_Aneg = S.alloc("Aneg", [128, 32], F32)
            P.dma("sp", ssdcw, ssdcw_in[l].rearrange("p (c k) -> p c k", k=3), (), [k_ssdcw])
            P.dma("sp", ssdcb, ssdcb_in[l], (), [k_ssdcb])
            P.dma("sp", alog, alog_in[l].partition_broadcast(128), (), [k_alog])
            P.dma("sp", dtb, dtb_in[l].partition_broadcast(128), (), [k_dtb])
            P.dma("sp", dsk, dskip_in[l].partition_broadcast(128), (), [k_dsk])
            ACT(Aneg, alog, AF.Exp, [k_alog], [k_Aneg])
            TS("dve", Aneg, Aneg, -1.0, None, ALU.mult, None, [k_Aneg], [k_Aneg])
            wdt, k_wdt = S.alloc("wdt", [128, 8, 32], BF16)
            WLOAD(wdt, wsrc(w_in, l, OFF_DT, 32), (), [k_wdt])
            dt_all, k_dt = S.alloc("dt_all", [128, NT, 32], F32)
            a_all, k_a = S.alloc("a_all", [128, NT, 32], F32)
            eacs, k_eacs = S.alloc("eacs", [128, NT, 32], F32)
            edte, k_edte = S.alloc("edte", [128, NT, 32], F32)
            etot, k_etot = S.alloc("etot", [128, NT, 32], F32)
            dtdte, k_dtdte = S.alloc("dtdte", [128, NT, 32], F32)
            RT = Region(A, CAP - 2 * 2304, CAP)
            tmpa, k_tmpa = RT.alloc("tmpa", [128, NT, 32], F32)
            tmpb, k_tmpb = RT.alloc("tmpb", [128, NT, 32], F32)

            def pview(b0):
                return psum[:, b0 * 512: b0 * 512 + NT * 32].rearrange("p (t c) -> p t c", c=32)

            def pkeys(b0):
                return [PSK(b0), PSK(b0 + 1)]

            for t in range(NT):
                for kc in range(8):
                    MM(psum[:, t * 32:(t + 1) * 32], hT[:, kc, t * 128:(t + 1) * 128], wdt[:, kc, :], kc == 0, kc == 7,
                       [(k_hT, t), k_wdt], [PSK(0 if t < 16 else 1)])
            TT("dve", tmpa, pview(0), dtb.unsqueeze(1).to_broadcast([128, NT, 32]), ALU.add, pkeys(0) + [k_dtb], [k_tmpa])
            ACT(tmpb, tmpa, AF.Exp, [k_tmpa], [k_tmpb])
            ACT(dt_all, tmpb, AF.Ln, [k_tmpb], [k_dt], bias=1.0)
            TT("dve", a_all, dt_all, Aneg.unsqueeze(1).to_broadcast([128, NT, 32]), ALU.mult, [k_dt, k_Aneg], [k_a])
            RT.free_all()
            for t in range(NT):
                bk = lambda b0: [PSK(b0 + (0 if t < 16 else 1))]
                for d_ in range(2):
                    sl = slice(t * 32 + d_ * 16, t * 32 + d_ * 16 + 16)
                    MM(psum[:, 1024 + sl.start:1024 + sl.stop], tri[:, d_, :], a_all[:, t, d_ * 16:(d_ + 1) * 16], True, True,
                       [k_tri, k_a], bk(2))
                    MM(psum[:, 2048 + sl.start:2048 + sl.stop], tri[:, 2 + d_, :], a_all[:, t, d_ * 16:(d_ + 1) * 16], True, True,
                       [k_tri, k_a], bk(4))
                MM(psum[:, t * 32:(t + 1) * 32], ones, a_all[:, t, :], True, True, [k_ones, k_a], bk(0))
            ACT(eacs, pview(2), AF.Exp, pkeys(2), [k_eacs])
            ACT(edte, pview(4), AF.Exp, pkeys(4), [k_edte])
            ACT(etot, pview(0), AF.Exp, pkeys(0), [k_etot])
            TT("dve", dtdte, dt_all, edte, ALU.mult, [k_dt, k_edte], [k_dtdte])

            if want("dt%d" % l):
                tmp, k_tmp = S.alloc("dd", [128, 5 * NT * 32], F32)
                for i_, (ap_, k_) in enumerate(((dt_all, k_dt), (a_all, k_a), (eacs, k_eacs), (edte, k_edte), (etot, k_etot))):
                    CP("dve", tmp[:, i_ * NT * 32:(i_ + 1) * NT * 32], ap_.rearrange("p a b -> p (a b)"), [k_], [k_tmp])
                P.dma("sp", dbg_out[:, 0:5 * NT * 32], tmp, [k_tmp], ["dbg"], is_output=True)
                break
            wgs = [S.alloc("wg%d" % i, [128, 8, 512], BF16) for i in range(2)]

            def load_wg(g):
                wg, kwg = wgs[g % 2]
                WLOAD(wg[:, :, 0:256], wsrc(w_in, l, OFF_XBC + g * 256, 256), (), [(kwg, 0)])
                WLOAD(wg[:, :, 256:384], wsrc(w_in, l, OFF_XBC + 1024 + g * 128, 128), (), [(kwg, 1)])
                WLOAD(wg[:, :, 384:512], wsrc(w_in, l, OFF_XBC + 1536 + g * 128, 128), (), [(kwg, 2)])

            load_wg(0)
            load_wg(1)
            xbcT, k_xbcT = S.alloc("xbcT", [128, 4, NTOK], BF16)
            xs_tok, k_xs = S.alloc("xs_tok", [128, NT, 256], BF16)
            B_tok, k_Bt = S.alloc("B_tok", [128, NT, 128], BF16)
            CBms = [S.alloc("CBm%d" % i, [128, 2, 128], BF16) for i in range(2)]
            rhsb, k_rhsb = S.alloc("rhsb", [128, 2, 4, 128], F32)
            expEs = [S.alloc("expE%d" % i, [128, 2, 4, 128], BF16) for i in range(2)]
            MTs = [S.alloc("MT%d" % i, [128, 2, 4, 128], BF16) for i in range(2)]
            xds = [S.alloc("xd%d" % i, [128, 2, 4, 64], BF16) for i in range(2)]
            xdds = [S.alloc("xdd%d" % i, [128, 4, 64], BF16) for i in range(4)]
            t1, k_t1 = S.alloc("t1", [128, 4, 64], F32)
            t2, k_t2 = S.alloc("t2", [128, 4, 64], F32)
            t1b, k_t1b = S.alloc("t1b", [128, 4, 64], F32)
            t3s = [S.alloc("t3%d" % i, [128, 4, 64], F32) for i in range(2)]
            Hrun, k_Hrun = S.alloc("Hrun", [128, 2, 2, 256], F32)
            SX = Region(A, S.p, CAP)

            def h4(ap):
                return ap.rearrange("p (h d) -> p h d", h=4)

            for g in range(4):
                wg, kwg = wgs[g % 2]
                Upad, k_U = SX.alloc("Upad", [128, NTOK + 4], F32)
                acc, k_acc = SX.alloc("acc", [128, NTOK], F32)
                MSET("pool", Upad[:, 0:1], 0.0, [(k_U, "p0")])
                MSET("pool", Upad[:, 257:259], 0.0, [(k_U, "p1")])
                MSET("pool", Upad[:, 2307:2308], 0.0, [(k_U, "p2")])
                for cc in range(4):
                    cidx = (g * 2 + cc) if cc < 2 else (8 + g if cc == 2 else 12 + g)
                    kwp = (kwg, 0) if cc < 2 else (kwg, cc - 1)
                    for bi, (t0, n) in enumerate(BLOCKS):
                        b = (cc * 5 + bi) % 6
                        for kc in range(8):
                            MM(PS(b, n), wg[:, kc, cc * 128:(cc + 1) * 128], hT[:, kc, t0:t0 + n], kc == 0, kc == 7,
                               [kwp] + [(k_hT, t) for t in range(t0 // 128, (t0 + n) // 128)], [PSK(b)])
                        off = t0 + 1 if t0 == 0 else t0 + 3
                        ACT(Upad[:, off:off + n], PS(b, n), AF.Copy, [PSK(b)], [(k_U, bi)])
                    ukeys = [(k_U, bi) for bi in range(5)] + [(k_U, "p0"), (k_U, "p1"), (k_U, "p2")]
                    for (o0, n, u0) in ((0, 256, 0), (256, 2048, 258)):
                        ka = (k_acc, o0)
                        TS("dve", acc[:, o0:o0 + n], Upad[:, u0:u0 + n], ssdcw[:, cidx, 0:1], None, ALU.mult, None,
                           ukeys + [k_ssdcw], [ka])
                        STT("dve", acc[:, o0:o0 + n], Upad[:, u0 + 1:u0 + 1 + n], ssdcw[:, cidx, 1:2], acc[:, o0:o0 + n], ALU.mult, ALU.add,
                            ukeys + [k_ssdcw, ka], [ka])
                        STT("dve", acc[:, o0:o0 + n], Upad[:, u0 + 2:u0 + 2 + n], ssdcw[:, cidx, 2:3], acc[:, o0:o0 + n], ALU.mult, ALU.add,
                            ukeys + [k_ssdcw, ka], [ka])
                        ACT(xbcT[:, cc, o0:o0 + n], acc[:, o0:o0 + n], AF.Silu, [ka, k_ssdcb], [(k_xbcT, cc, o0)], bias=ssdcb[:, cidx:cidx + 1])
                SX.free_all()
                if want("xbc%d" % l):
                    dump(xbcT.rearrange("p a b -> p (a b)"), 4 * NTOK, [(k_xbcT, cc, o0) for cc in range(4) for o0 in (0, 256)], SX)
                    break
                if g + 2 < 4 and 'a' not in XF:
                    load_wg(g + 2)
                xk = lambda cc, t: (k_xbcT, cc, 0 if t < 2 else 256)
                for t in range(NT):
                    b = 6 + (t % 2)
                    pt = PS(b).bitcast(BF16)
                    for cc in range(3):
                        TR(pt[:, cc * 128:(cc + 1) * 128], xbcT[:, cc, t * 128:(t + 1) * 128], [xk(cc, t)], [PSK(b)])
                    if 'b' not in XF:
                        CP("dve", xs_tok[:, t, :], pt[:, 0:256], [PSK(b)], [(k_xs, t)])
                    if 'c' not in XF:
                        ACT(B_tok[:, t, :], pt[:, 256:384], AF.Copy, [PSK(b)], [(k_Bt, t)])
                if want("tok%d" % l):
                    dump(xs_tok.rearrange("p a b -> p (a b)"), NT * 256, [(k_xs, t) for t in range(NT)])
                    break
                Hin, k_Hin = SX.alloc("Hin", [128, 2, NT, 256], BF16)
                orders = [list(range(NT)), [1, 0] + list(range(NT - 1, 1, -1))]
                MSET("pool", Hrun[:, :, 0, :], 0.0, [(k_Hrun, d_, 0, h_) for d_ in range(2) for h_ in range(4)])

                def emit_xdd(i, d_):
                    c = orders[d_][i]
                    hs = slice(d_ * 16 + g * 4, d_ * 16 + g * 4 + 4)
                    xdd, k_xdd = xdds[(2 * i + d_) % 4]
                    TT("pool", xdd, h4(xs_tok[:, c, :]), dtdte[:, c, hs].unsqueeze(2).to_broadcast([128, 4, 64]), ALU.mult,
                       [(k_xs, c), k_dtdte], [k_xdd])

                for d_ in range(2):
                    emit_xdd(0, d_)
                for i in range(NT):
                    for d_ in range(2):
                        if i + 1 < NT - 1:
                            emit_xdd(i + 1, d_)
                    for d_ in range(2):
                        c = orders[d_][i]
                        pp = i % 2
                        ACT(Hin[:, d_, c, :], Hrun[:, d_, pp, :], AF.Copy, [(k_Hrun, d_, pp, h_) for h_ in range(4)], [(k_Hin, d_, c)])
                        if i == NT - 1:
                            continue
                        xdd, k_xdd = xdds[(2 * i + d_) % 4]
                        b = 6 + ((2 * i + d_) % 2)
                        MM(PS(b, 256), B_tok[:, c, :], xdd.rearrange("p h d -> p (h d)"), True, True, [(k_Bt, c), k_xdd], [PSK(b)])
                        for h_ in range(4):
                            hh = d_ * 16 + g * 4 + h_
                            STT("dve", Hrun[:, d_, 1 - pp, h_ * 64:(h_ + 1) * 64], Hrun[:, d_, pp, h_ * 64:(h_ + 1) * 64], etot[:, c, hh:hh + 1],
                                PS(b, 64, off=h_ * 64), ALU.mult, ALU.add, [(k_Hrun, d_, pp, h_), k_etot, PSK(b)], [(k_Hrun, d_, 1 - pp, h_)])
                dt_dh = lambda c: dt_all[:, c, :].rearrange("p (d h) -> p d h", d=2)[:, :, g * 4:g * 4 + 4]
                a_dh = lambda c: a_all[:, c, :].rearrange("p (d h) -> p d h", d=2)[:, :, g * 4:g * 4 + 4]

                def prepA1(c):
                    csl = slice(c * 128, (c + 1) * 128)
                    bA = c % 2
                    cbm, k_cbm = CBms[c % 2]
                    MM(PS(bA, 128), xbcT[:, 2, csl], xbcT[:, 3, csl], True, True, [xk(2, c), xk(3, c)], [PSK(bA)])
                    TT("dve", cbm, PS(bA, 128).unsqueeze(1).to_broadcast([128, 2, 128]), tri[:, 0:2, :], ALU.mult, [PSK(bA), k_tri], [k_cbm])
                    TT("pool", rhsb[:, 0], a_dh(c)[:, 0, :].unsqueeze(2).to_broadcast([128, 4, 128]),
                       tri[:, 0, :].unsqueeze(1).to_broadcast([128, 4, 128]), ALU.mult, [k_a, k_tri], [(k_rhsb, 0)])
                    TT("dve", rhsb[:, 1], a_dh(c)[:, 1, :].unsqueeze(2).to_broadcast([128, 4, 128]),
                       tri[:, 1, :].unsqueeze(1).to_broadcast([128, 4, 128]), ALU.mult, [k_a, k_tri], [(k_rhsb, 1)])

                def prepA2(c):
                    ee, k_ee = expEs[c % 2]
                    for d_ in range(2):
                        MM(PS(2 + d_), tri[:, 2 + d_, :], rhsb[:, d_].rearrange("p h l -> p (h l)"), True, True, [k_tri, (k_rhsb, d_)], [PSK(2 + d_)])
                        ACT(ee[:, d_].rearrange("p h l -> p (h l)"), PS(2 + d_), AF.Exp, [PSK(2 + d_)], [(k_ee, d_)])

                def prepB(c):
                    mt, k_mt = MTs[c % 2]
                    xd, k_xd = xds[c % 2]
                    cbm, k_cbm = CBms[c % 2]
                    ee, k_ee = expEs[c % 2]
                    TT("dve", mt, ee, cbm.unsqueeze(2).to_broadcast([128, 2, 4, 128]), ALU.mult, [(k_ee, 0), (k_ee, 1), k_cbm], [k_mt])
                    TT("pool", xd, h4(xs_tok[:, c, :]).unsqueeze(1).to_broadcast([128, 2, 4, 64]),
                       dt_dh(c).unsqueeze(3).to_broadcast([128, 2, 4, 64]), ALU.mult, [(k_xs, c), k_dt], [k_xd])
                    TT("pool", t3s[c % 2][0], h4(xs_tok[:, c, :]), dsk[:, g * 4:g * 4 + 4].unsqueeze(2).to_broadcast([128, 4, 64]), ALU.mult,
                       [(k_xs, c), k_dsk], [t3s[c % 2][1]])

                def finish_pe(c):
                    csl = slice(c * 128, (c + 1) * 128)
                    bY = 4 + (c % 2)
                    bO = 6 + (c % 2)
                    mt, k_mt = MTs[c % 2]
                    xd, k_xd = xds[c % 2]
                    for d_ in range(2):
                        for h_ in range(4):
                            MM(PS(bY, 64, off=h_ * 64), mt[:, d_, h_, :], xd[:, d_, h_, :], d_ == 0 and h_ == 0, d_ == 1, [k_mt, k_xd], [PSK(bY)], skip=True)
                    MM(PS(bY, 256, off=256), xbcT[:, 3, csl], Hin[:, 0, c, :], False, True, [xk(3, c), (k_Hin, 0, c)], [PSK(bY)], skip=True)
                    MM(PS(bO, 256), xbcT[:, 3, csl], Hin[:, 1, c, :], True, True, [xk(3, c), (k_Hin, 1, c)], [PSK(bO)])

                def finish_dve(c):
                    bY = 4 + (c % 2)
                    bO = 6 + (c % 2)
                    t3, k_t3 = t3s[c % 2]
                    TT("dve", t1, h4(PS(bY, 256, off=256)), eacs[:, c, g * 4:g * 4 + 4].unsqueeze(2).to_broadcast([128, 4, 64]), ALU.mult,
                       [PSK(bY), k_eacs], [k_t1])
                    TT("dve", t2, h4(PS(bO, 256)), eacs[:, c, 16 + g * 4:16 + g * 4 + 4].unsqueeze(2).to_broadcast([128, 4, 64]), ALU.mult,
                       [PSK(bO), k_eacs], [k_t2])
                    TT("pool", t2, t2, t3, ALU.add, [k_t2, k_t3], [k_t2])
                    TT("dve", t1b, h4(PS(bY, 256)), t1, ALU.add, [PSK(bY), k_t1], [k_t1b])
                    TT("dve", h4(ybuf[:, c, g * 256:(g + 1) * 256]), t1b, t2, ALU.add, [k_t1b, k_t2], [(k_ybuf, c, g)])

                oc = tiles_q
                no = len(oc)
                prepA1(oc[0])
                prepA2(oc[0])
                prepA1(oc[1])
                prepA2(oc[1])
                prepB(oc[0])
                for ci in range(no):
                    if ci + 2 < no:
                        prepA1(oc[ci + 2])
                    finish_pe(oc[ci])
                    if ci + 2 < no:
                        prepA2(oc[ci + 2])
                    if ci + 1 < no:
                        prepB(oc[ci + 1])
                    finish_dve(oc[ci])
                SX.free_all()
            if want("xbc%d" % l) or want("hin%d" % l) or want("tok%d" % l):
                break
            S.free_all()
            if want("ybuf%d" % l):
                dump(ybuf.rearrange("p a b -> p (a b)"), NT * D, [(k_ybuf, c, g) for c in range(NT) for g in range(4)], R23)
                break

            ysT, k_ysT = R2.alloc("ysT", [128, 8, NTOK], BF16)
            wz, k_wz = R3.alloc("wz", [128, 8, D], BF16)
            ssdnw, k_ssdnw = R3.alloc("ssdnw", [128, 8], F32)
            sz, k_sz = R3.alloc("sz", [128, 2, D], F32)
            yz, k_yz = R3.alloc("yz", [128, 2, D], F32)
            nbufs = norm_bufs(R3)
            P.dma("sp", ssdnw, ssdnwT_in[l], (), [k_ssdnw])
            for hf in range(2):
                WLOAD(wz[:, :, hf * 512:(hf + 1) * 512], wsrc(w_in, l, OFF_Z + hf * 512, 512), (), [(k_wz, hf)])
            def ro_front(i):
                t = tiles_q[i]
                for hf in range(2):
                    b = hf + 2 * (i % 2)
                    for kc in range(8):
                        MM(PS(b), hT[:, kc, t * 128:(t + 1) * 128], wz[:, kc, hf * 512:(hf + 1) * 512], kc == 0, kc == 7,
                           [(k_hT, t), (k_wz, hf)], [PSK(b)])
                    ACT(sz[:, i % 2, hf * 512:(hf + 1) * 512], PS(b), AF.Silu, [PSK(b)], [(k_sz, i % 2, hf)])
                TT("dve", yz[:, i % 2, :], ybuf[:, t, :], sz[:, i % 2, :], ALU.mult,
                   [(k_ybuf, t, g) for g in range(4)] + [(k_sz, i % 2, 0), (k_sz, i % 2, 1)], [(k_yz, i % 2)])
                norm_front(yz[:, i % 2, :], (k_yz, i % 2), D, nbufs, i)

            ro_front(0)
            for i, t in enumerate(tiles_q):
                if i + 1 < len(tiles_q):
                    ro_front(i + 1)
                norm_back(t, ssdnw, None, [k_ssdnw], ysT, k_ysT, nbufs, i)
            R3.free_all()
            R1.free_all()
            ysT_all = [(k_ysT, t) for t in tiles_q]
            if want("ysT%d" % l):
                dump(ysT.rearrange("p a b -> p (a b)"), 8 * NTOK, ysT_all, R3)
                break

            yaT, k_yaT = R1.alloc("yaT", [128, 8, NTOK], BF16)
            T_ = R3
            cosT, k_cos = T_.alloc("cosT", [128, SEQ], F32)
            sinT, k_sin = T_.alloc("sinT", [128, SEQ], F32)
            P.dma("sp", cosT, cos_in, (), [k_cos])
            P.dma("sp", sinT, sin_in, (), [k_sin])
            subln, k_subln = T_.alloc("subln", [128, 1], F32)
            lamv, k_lamv = T_.alloc("lamv", [128, 4, 64], F32)
            lprod, k_lprod = T_.alloc("lprod", [128, 2, 64], F32)
            lsm, k_lsm = T_.alloc("lsm", [128, 4], F32)
            P.dma("sp", subln, sublnT_in[l], (), [k_subln])
            P.dma("sp", lamv, lam_in[l].partition_broadcast(128).rearrange("p (a b) -> p a b", a=4), (), [k_lamv])
            TS("dve", subln, subln, 1.0 - lam_init, None, ALU.mult, None, [k_subln], [k_subln])
            TT("dve", lprod[:, 0, :], lamv[:, 0, :], lamv[:, 1, :], ALU.mult, [k_lamv], [(k_lprod, 0)])
            TT("dve", lprod[:, 1, :], lamv[:, 2, :], lamv[:, 3, :], ALU.mult, [k_lamv], [(k_lprod, 1)])
            P.op("dve", lambda e: e.tensor_reduce(out=lsm[:, 0:2], in_=lprod, axis=AX.X, op=ALU.add), [(k_lprod, 0), (k_lprod, 1)], [(k_lsm, 0)])
            ACT(lsm[:, 2:4], lsm[:, 0:2], AF.Exp, [(k_lsm, 0)], [(k_lsm, 1)])
            TT("dve", lsm[:, 0:1], lsm[:, 3:4], lsm[:, 2:3], ALU.subtract, [(k_lsm, 1)], [(k_lsm, 2)])
            TS("dve", lsm[:, 1:2], lsm[:, 0:1], -lam_init, None, ALU.add, None, [(k_lsm, 2)], [(k_lsm, 3)])
            neglam = lsm[:, 1:2]
            k_neglam = (k_lsm, 3)
            wsl = [T_.alloc("watt%d" % i, [128, 3, 8, 128], BF16) for i in range(2)]
            qraws = [T_.alloc("qraw%d" % i, [128, 512], BF16) for i in range(2)]
            qT, k_qT = T_.alloc("qT", [128, 2, NTOK], BF16)
            MSET("pool", qT[64:128, 0, :], 0.0, [(k_qT, "z0")])
            MSET("pool", qT[0:64, 1, :], 0.0, [(k_qT, "z1")])
            kT, k_kT = T_.alloc("kT", [128, NTOK], BF16)
            v_aug, k_v = T_.alloc("v_aug", [128, NT, 132], BF16)
            ropa = [T_.alloc("ropa%d" % i, [128, 512], F32) for i in range(2)]
            ropb = [T_.alloc("ropb%d" % i, [128, 512], F32) for i in range(2)]
            pTs = [T_.alloc("pT%d" % i, [128, 512], BF16) for i in range(4)]
            o4, k_o4 = T_.alloc("o4", [128, 4, 128], F32)
            on4, k_on4 = T_.alloc("on4", [128, 4, 128], BF16)
            rs4, k_rs4 = T_.alloc("rs4", [128, 4, 8], F32)
            att = {"sb": -1, "sbs": {}}
            MSET("pool", v_aug[:, :, 128:129], 1.0, [(k_v, "ones")])

            def load_watt(h):
                wt, kwt = wsl[h % 2]
                WLOAD(wt[:, 0], wsrc(w_in, l, OFF_Q + h * 128, 128), (), [(kwt, 0)])
                WLOAD(wt[:, 1], wsrc(w_in, l, OFF_K + h * 128, 128), (), [(kwt, 1)])
                WLOAD(wt[:, 2], wsrc(w_in, l, OFF_V + h * 128, 128), (), [(kwt, 2)])

            load_watt(0)
            load_watt(1)
            cnt = {"rope": 0, "post": 0}
            pend = []

            def flush_pend():
                if pend:
                    if pend[1] == 2:
                        post2(pend[0])
                    post3(pend[0], pend[2])
                    pend.clear()

            for h in range(8):
                wt, kwt = wsl[h % 2]
                for (wi, dst, kdst, isq) in ((0, None, None, True), (1, kT, k_kT, False)):
                    for bi, (t0, n) in enumerate(BLOCKS):
                        hk = [(k_hT, t) for t in range(t0 // 128, (t0 + n) // 128)]
                        if bi == 0:
                            if isq and not with_ctx:
                                continue
                            for kc in range(8):
                                MM(PS(0, n), wt[:, wi, kc, :], hT[:, kc, t0:t0 + n], kc == 0, kc == 7, [(kwt, wi)] + hk, [PSK(0)])
                            if isq:
                                ACT(qT[0:64, 0, t0:t0 + n], PS(0, n)[0:64, :], AF.Copy, [PSK(0), (k_qT, "z0")], [(k_qT, 0, bi)])
                                ACT(qT[64:128, 1, t0:t0 + n], PS(0, n)[64:128, :], AF.Copy, [PSK(0), (k_qT, "z1")], [(k_qT, 1, bi)])
                            else:
                                ACT(dst[:, t0:t0 + n], PS(0, n), AF.Copy, [PSK(0)], [(kdst, bi)])
                            continue
                        r_i = cnt["rope"] % 2
                        cnt["rope"] += 1
                        bA, bB = 2 * r_i, 2 * r_i + 1
                        for kc in range(8):
                            MM(PS(bA), wt[:, wi, kc, :], hT[:, kc, t0:t0 + n], kc == 0, kc == 7, [(kwt, wi)] + hk, [PSK(bA)])
                        qr, k_qr = qraws[r_i]
                        ACT(qr, PS(bA), AF.Copy, [PSK(bA)], [k_qr])
                        MM(PS(bB), permT, qr, True, True, [(k_perm, 0), (k_perm, 1), k_qr], [PSK(bB)])
                        ra, k_ra = ropa[r_i]
                        rb_, k_rb_ = ropb[r_i]
                        c0 = t0 - CTX
                        TT("dve", ra, PS(bA), cosT[:, c0:c0 + n], ALU.mult, [PSK(bA), k_cos], [k_ra])
                        TT("dve", rb_, PS(bB), sinT[:, c0:c0 + n], ALU.mult, [PSK(bB), k_sin], [k_rb_])
                        if isq:
                            TT("pool", qT[0:64, 0, t0:t0 + n], ra[0:64, :], rb_[0:64, :], ALU.add, [k_ra, k_rb_, (k_qT, "z0")], [(k_qT, 0, bi)])
                            TT("pool", qT[64:128, 1, t0:t0 + n], ra[64:128, :], rb_[64:128, :], ALU.add, [k_ra, k_rb_, (k_qT, "z1")], [(k_qT, 1, bi)])
                        else:
                            TT("pool", dst[:, t0:t0 + n], ra, rb_, ALU.add, [k_ra, k_rb_], [(kdst, bi)])
                for t4 in range(0, NT, 4):
                    nt4 = min(4, NT - t4)
                    b = 4 + ((t4 // 4) % 2)
                    for ti in range(nt4):
                        t = t4 + ti
                        for kc in range(8):
                            MM(PS(b, 128, off=ti * 128), hT[:, kc, t * 128:(t + 1) * 128], wt[:, 2, kc, :], kc == 0, kc == 7,
                               [(k_hT, t), (kwt, 2)], [PSK(b)])
                    CP("dve", v_aug[:, t4:t4 + nt4, 0:128], PS(b, nt4 * 128).rearrange("p (t e) -> p t e", e=128), [PSK(b)],
                       [(k_v, t) for t in range(t4, t4 + nt4)])
                flush_pend()
                if h + 2 < 8:
                    load_watt(h + 2)
                qblocks = ([(0, 256, [0, 1], [0])] if with_ctx else []) + [(256 + 512 * j, 512, list(range(NT)), [1 + j]) for j in range(4)]
                LOOK = 3

                def acc_of(nq, comp, j, n=129):
                    idx = comp * nq + j
                    return PS(4 + idx // 3, n, off=(idx % 3) * 132), PSK(4 + idx // 3), idx

                def emit_S(qb, i):
                    q0, qn, ktiles, qbi = qb
                    ki, comp = i // 2, i % 2
                    kt = ktiles[ki]
                    sb = att["sb"] = (att["sb"] + 1) % 4
                    att["sbs"][(q0, i)] = sb
                    MM(PS(sb, qn), kT[:, kt * 128:(kt + 1) * 128], qT[:, comp, q0:q0 + qn], True, True,
                       [(k_kT, 0 if kt < 2 else 1 + (kt - 2) // 4), (k_qT, "z%d" % (1 - comp))] + [(k_qT, comp, b_) for b_ in qbi], [PSK(sb)])

                def emit_EP(qb, i):
                    q0, qn, ktiles, qbi = qb
                    nq = qn // 128
                    ki, comp = i // 2, i % 2
                    kt = ktiles[ki]
                    sb = att["sbs"].pop((q0, i))
                    pT_, k_pT = pTs[sb]
                    ACT(pT_[:, 0:qn], PS(sb, qn), AF.Exp, [PSK(sb)], [k_pT], scale=ATT_SCALE)
                    for j in range(nq):
                        a_, ka_, idx = acc_of(nq, comp, j)
                        MM(a_, pT_[:, j * 128:(j + 1) * 128], v_aug[:, kt, 0:129], ki == 0 and idx % 3 == 0, ki == len(ktiles) - 1,
                           [k_pT, (k_v, kt), (k_v, "ones")], [ka_], skip=True)

                def post1(qb):
                    q0, qn, ktiles, qbi = qb
                    nq = qn // 128
                    A0 = [acc_of(nq, 0, j) for j in range(nq)]
                    A1 = [acc_of(nq, 1, j) for j in range(nq)]
                    J = range(nq)
                    for j in J:
                        RECIP(rs4[:, j, 0:1], A0[j][0][:, 128:129], [A0[j][1]], [(k_rs4, j, 0)])
                        RECIP(rs4[:, j, 1:2], A1[j][0][:, 128:129], [A1[j][1]], [(k_rs4, j, 1)])
                    for j in J:
                        TT("dve", rs4[:, j, 2:3], rs4[:, j, 1:2], neglam, ALU.mult, [(k_rs4, j, 1), k_neglam], [(k_rs4, j, 2)])
                    for j in J:
                        TS("dve", o4[:, j, :], A0[j][0][:, 0:128], rs4[:, j, 0:1], None, ALU.mult, None, [A0[j][1], (k_rs4, j, 0)], [(k_o4, j)])
                    for j in J:
                        STT("dve", o4[:, j, :], A1[j][0][:, 0:128], rs4[:, j, 2:3], o4[:, j, :], ALU.mult, ALU.add,
                            [A1[j][1], (k_rs4, j, 2), (k_o4, j)], [(k_o4, j)])

                def post2(qb):
                    J = range(qb[1] // 128)
                    for j in J:
                        ACT(on4[:, j, :], o4[:, j, :], AF.Square, [(k_o4, j)], [(k_on4, j), (k_rs4, j, 3)], accum_out=rs4[:, j, 3:4])
                    for j in J:
                        ACT(rs4[:, j, 4:5], rs4[:, j, 3:4], AF.Ln, [(k_rs4, j, 3)], [(k_rs4, j, 4)], scale=1.0 / 128, bias=EPS)
                    for j in J:
                        ACT(rs4[:, j, 5:6], rs4[:, j, 4:5], AF.Exp, [(k_rs4, j, 4)], [(k_rs4, j, 5)], scale=-0.5)
                    for j in J:
                        TS("dve", on4[:, j, :], o4[:, j, :], rs4[:, j, 5:6], None, ALU.mult, None, [(k_o4, j), (k_rs4, j, 5)], [(k_on4, j)])

                def post3(qb, h):
                    q0 = qb[0]
                    J = range(qb[1] // 128)
                    pt = PS(7).bitcast(BF16)
                    for j in J:
                        TR(pt[:, j * 128:(j + 1) * 128], on4[:, j, :], [(k_on4, j)], [PSK(7)])
                    for j in J:
                        tq = q0 // 128 + j
                        ACT(yaT[:, h, tq * 128:(tq + 1) * 128], pt[:, j * 128:(j + 1) * 128], AF.Identity, [PSK(7), k_subln],
                            [(k_yaT, tq, h)], scale=subln[:, 0:1])

                for bi_, qb in enumerate(qblocks):
                    nst = 2 * len(qb[2])
                    if bi_ == 0:
                        for i in range(min(LOOK, nst)):
                            emit_S(qb, i)
                    for i in range(nst):
                        if i + LOOK < nst:
                            emit_S(qb, i + LOOK)
                        emit_EP(qb, i)
                        if pend and pend[1] == 2 and i >= 3:
                            post2(pend[0])
                            pend[1] = 3
                        if pend and pend[1] == 3 and i >= 8:
                            post3(pend[0], pend[2])
                            pend.clear()
                    flush_pend()
                    if bi_ + 1 < len(qblocks):
                        nb = qblocks[bi_ + 1]
                        for i in range(min(LOOK, 2 * len(nb[2]))):
                            emit_S(nb, i)
                    post1(qb)
                    pend.extend([qb, 2, h])
            flush_pend()
            T_.free_all()
            yaT_all = [(k_yaT, t, h) for t in tiles_q for h in range(8)]
            if want("yaT%d" % l):
                dump(yaT.rearrange("p a b -> p (a b)"), 8 * NTOK, yaT_all)
                break

            mT, k_mT = R3.alloc("mT", [128, 8, NTOK], BF16)
            wms = [R3.alloc("wmrg%d" % i, [128, 4, 8, 128], BF16) for i in range(2)]
            sgs = [R3.alloc("sg%d" % i, [128, 2, 512], F32) for i in range(2)]
            tms = [R3.alloc("tm%d" % i, [128, 2, 512], F32) for i in range(2)]

            def load_wm(j):
                wt, kwt = wms[j % 2]
                WLOAD(wt[:, 0], wsrc(w_brs, l, j * 128, 128), (), [(kwt, 0)])
                WLOAD(wt[:, 1], wsrc(w_bra, l, j * 128, 128), (), [(kwt, 1)])
                WLOAD(wt[:, 2], wsrc(w_in, l, OFF_G + j * 128, 128), (), [(kwt, 2)])
                WLOAD(wt[:, 3], wsrc(w_in, l, OFF_G + D + j * 128, 128), (), [(kwt, 3)])

            load_wm(0)
            load_wm(1)
            it = 0
            for j in range(8):
                wt, kwt = wms[j % 2]
                for (t0, n) in blocks_q:
                    tl = list(range(t0 // 128, (t0 + n) // 128))
                    b0 = 4 * (it % 2)
                    sg, k_sg = sgs[it % 2]
                    tm, k_tm = tms[it % 2]
                    it += 1
                    srcs = ((ysT, [(k_ysT, t) for t in tl]), (yaT, [(k_yaT, t, h_) for t in tl for h_ in range(8)]),
                            (hT, [(k_hT, t) for t in tl]), (hT, [(k_hT, t) for t in tl]))
                    for wi in range(4):
                        src, sk = srcs[wi]
                        for kc in range(8):
                            MM(PS(b0 + wi, n), wt[:, wi, kc, :], src[:, kc, t0:t0 + n], kc == 0, kc == 7, [(kwt, wi)] + sk, [PSK(b0 + wi)])
                    for wi in range(2):
                        ACT(sg[:, wi, 0:n], PS(b0 + 2 + wi, n), AF.Sigmoid, [PSK(b0 + 2 + wi)], [(k_sg, wi)])
                    for wi in range(2):
                        TT("dve", tm[:, wi, 0:n], PS(b0 + wi, n), sg[:, wi, 0:n], ALU.mult, [PSK(b0 + wi), (k_sg, wi)], [(k_tm, wi)])
                    TT("pool", mT[:, j, t0:t0 + n], tm[:, 0, 0:n], tm[:, 1, 0:n], ALU.add, [(k_tm, 0), (k_tm, 1)], [(k_mT, t, j) for t in tl])
                if j + 2 < 8:
                    load_wm(j + 2)
            for k in R3.keys[1:]:
                A.free(k)
            del R3.keys[1:]
            R0.free_all()
            R1.free_all()
            R2.free_all()

            h2T, k_h2T = R0.alloc("h2T", [128, 8, NTOK], BF16)
            wo, k_wo = R1.alloc("wo", [128, 8, D], BF16)
            for hf in range(2):
                WLOAD(wo[:, :, hf * 512:(hf + 1) * 512], wsrc(w_out, l, hf * 512, 512), (), [(k_wo, hf)])
            xts = [R2.alloc("xt%d" % i, [128, D], F32) for i in range(2)]
            xns = [R2.alloc("xnew%d" % i, [128, D], F32) for i in range(2)]
            nbufs = norm_bufs(R2)
            def op_front(i):
                t = tiles_q[i]
                w_ = 1 if t < 2 else 0
                xt, kx = xts[i % 2]
                xnw, kxn_ = xns[i % 2]
                P.dma("sp", xt, xsrc(t), [("xres", t)], [kx])
                for hf in range(2):
                    b = hf + 2 * (i % 2)
                    for kc in range(8):
                        MM(PS(b), mT[:, kc, t * 128:(t + 1) * 128], wo[:, kc, hf * 512:(hf + 1) * 512], kc == 0, kc == 7,
                           [(k_mT, t, kc), (k_wo, hf)], [PSK(b)])
                    TT("dve", xnw[:, hf * 512:(hf + 1) * 512], PS(b), g1bc[:, w_, hf * 512:(hf + 1) * 512], ALU.mult,
                       [PSK(b), (k_g1bc, w_, hf)], [(kxn_, hf)])
                kxh = [(kxn_, 0), (kxn_, 1)]
                TT("pool", xnw, xnw, xt, ALU.add, kxh + [kx], kxh)
                P.dma("sp", xres[t * 128:(t + 1) * 128, :], xnw, kxh, [("xres", t)])
                norm_front(xnw, kxh, D, nbufs, i)

            op_front(0)
            for i, t in enumerate(tiles_q):
                if i + 1 < len(tiles_q):
                    op_front(i + 1)
                w_ = 1 if t < 2 else 0
                norm_back(t, scl2[:, :, w_], modT[:, 24:32, w_], [k_scl2] + [(k_modT, j) for j in range(24, 32)], h2T, k_h2T, nbufs, i)
            R1.free_all()
            R2.free_all()
            R3.free_all()
            if want("xm%d" % l):
                P.dma("sp", dbg_out.rearrange("p (t f) -> p t f", t=NT), xres.rearrange("(t p) f -> p t f", p=128),
                      [("xres", t) for t in range(NT)], ["dbg"], is_output=True)
                break
            if want("h2T%d" % l):
                dump(h2T.rearrange("p a b -> p (a b)"), 8 * NTOK, [(k_h2T, t) for t in tiles_q])
                break

            GT0 = BASE + SZ
            RG = Region(A, GT0, GT0 + NFF * NTOK * 2)
            RF = Region(A, GT0 + NFF * NTOK * 2, CAP)
            gT, k_gT = RG.alloc("gT", [128, NFF, NTOK], BF16)
            ffncw, k_fcw = RF.alloc("ffncw", [128, 2 * NFF, 3], F32)
            ffncb, k_fcb = RF.alloc("ffncb", [128, 2 * NFF], F32)
            P.dma("sp", ffncw, ffncw_in[l].rearrange("p (c k) -> p c k", k=3), (), [k_fcw])
            P.dma("sp", ffncb, ffncb_in[l], (), [k_fcb])
            Upads = [RF.alloc("Upad%d" % i, [128, NTOK + 4], F32) for i in range(2)]
            acc_, k_acc = RF.alloc("acc", [128, NTOK], F32)
            sa, k_sa = RF.alloc("sa", [128, NTOK], BF16)
            NWU = 2
            wus = [RF.alloc("wup%d" % i, [128, 8, 256], BF16) for i in range(NWU)]
            for Upad, k_U in Upads:
                MSET("pool", Upad[:, 0:1], 0.0, [(k_U, "p0")])
                MSET("pool", Upad[:, 257:259], 0.0, [(k_U, "p1")])
                MSET("pool", Upad[:, 2307:2308], 0.0, [(k_U, "p2")])

            def load_wu(j):
                wt, kwt = wus[j % NWU]
                WLOAD(wt[:, :, 0:128], wsrc(w_up, l, j * 128, 128), (), [(kwt, 0)])
                WLOAD(wt[:, :, 128:256], wsrc(w_up, l, D_FF + j * 128, 128), (), [(kwt, 1)])

            for j in range(NWU):
                load_wu(j)
            ranges = ([(0, 256, 0)] if with_ctx else []) + [(256, 2048, 258)]
            bi_q = list(range(0 if with_ctx else 1, 5))
            it = 0
            for j in range(NFF):
                wt, kwt = wus[j % NWU]
                for part in range(2):
                    cidx = j + part * NFF
                    Upad, k_U = Upads[part]
                    for bi in bi_q:
                        t0, n = BLOCKS[bi]
                        b = it % 6
                        it += 1
                        for kc in range(8):
                            MM(PS(b, n), wt[:, kc, part * 128:(part + 1) * 128], h2T[:, kc, t0:t0 + n], kc == 0, kc == 7,
                               [(kwt, part)] + [(k_h2T, t) for t in range(t0 // 128, (t0 + n) // 128)], [PSK(b)])
                        off = t0 + 1 if t0 == 0 else t0 + 3
                        ACT(Upad[:, off:off + n], PS(b, n), AF.Copy, [PSK(b)], [(k_U, bi)])
                        ACT(acc_[:, t0:t0 + n], PS(b, n), AF.Copy, [PSK(b), k_fcw], [(k_acc, bi)], scale=ffncw[:, cidx, 1:2])
                    ukeys = [(k_U, bi) for bi in bi_q] + [(k_U, "p0"), (k_U, "p1"), (k_U, "p2")]
                    for (o0, n, u0) in ranges:
                        ka = [(k_acc, 0)] if o0 == 0 else [(k_acc, bi) for bi in range(1, 5)]
                        STT("dve", acc_[:, o0:o0 + n], Upad[:, u0:u0 + n], ffncw[:, cidx, 0:1], acc_[:, o0:o0 + n], ALU.mult, ALU.add,
                            ukeys + [k_fcw] + ka, ka)
                        STT("dve", acc_[:, o0:o0 + n], Upad[:, u0 + 2:u0 + 2 + n], ffncw[:, cidx, 2:3], acc_[:, o0:o0 + n], ALU.mult, ALU.add,
                            ukeys + [k_fcw] + ka, ka)
                        if part == 0:
                            ACT(sa[:, o0:o0 + n], acc_[:, o0:o0 + n], AF.Silu, ka + [k_fcb], [(k_sa, o0)], bias=ffncb[:, cidx:cidx + 1])
                        else:
                            STT("dve", gT[:, j, o0:o0 + n], acc_[:, o0:o0 + n], ffncb[:, cidx:cidx + 1], sa[:, o0:o0 + n], ALU.add, ALU.mult,
                                ka + [k_fcb, (k_sa, o0)], [(k_gT, j, o0)])
                if j + NWU < NFF:
                    load_wu(j + NWU)
            RF.free_all()
            R0.free_all()
            if want("gT%d" % l):
                dump(gT.rearrange("p a b -> p (a b)"), NFF * NTOK, [(k_gT, j, o0) for j in range(NFF) for (o0, _, _) in ranges])
                break

            last = l == DEPTH - 1
            wds = [R0.alloc("wd0", [128, NFF, 512], BF16), RF.alloc("wd1", [128, NFF, 512], BF16)]
            for hf in range(2):
                for j0 in range(0, NFF, 6):
                    j1 = min(NFF, j0 + 6)
                    WLOAD(wds[hf][0][:, j0:j1, :], w_down[l].rearrange("(j p) n -> p j n", p=128)[:, j0:j1, hf * 512:(hf + 1) * 512],
                          (), [(wds[hf][1], j0)])
            xts = [RF.alloc("xt%d" % i, [128, D], F32) for i in range(2)]
            xns = [RF.alloc("xnew%d" % i, [128, D], F32) for i in range(2)]
            if last:
                fnw, k_fnw = R0.alloc("fnw", [128, D], F32)
                fjunk, k_fjunk = R0.alloc("fjunk", [128, 2, D], BF16)
                fss, k_fss = R0.alloc("fss", [128, 2, 4], F32)
                P.dma("sp", fnw, fnw_in.partition_broadcast(128), (), [k_fnw])
            for i, t in enumerate(tiles_q):
                w_ = 1 if t < 2 else 0
                xt, kx = xts[i % 2]
                xnw, kxn_ = xns[i % 2]
                P.dma("sp", xt, xres[t * 128:(t + 1) * 128, :], [("xres", t)], [kx])
                gk = [(k_gT, j, 0 if t < 2 else 256) for j in range(NFF)]
                for hf in range(2):
                    b = hf + 2 * (i % 2)
                    wd, kwd = wds[hf]
                    for j in range(NFF):
                        MM(PS(b), gT[:, j, t * 128:(t + 1) * 128], wd[:, j, :], j == 0, j == NFF - 1,
                           [(k_gT, j, 0 if t < 2 else 256), (kwd, (j // 6) * 6)], [PSK(b)])
                    TT("dve", xnw[:, hf * 512:(hf + 1) * 512], PS(b), g2bc[:, w_, hf * 512:(hf + 1) * 512], ALU.mult,
                       [PSK(b), (k_g2bc, w_, hf)], [(kxn_, hf)])
                kxh = [(kxn_, 0), (kxn_, 1)]
                TT("pool", xnw, xnw, xt, ALU.add, kxh + [kx], kxh)
                if not last:
                    P.dma("sp", xres[t * 128:(t + 1) * 128, :], xnw, kxh, [("xres", t)])
                else:
                    pi = i % 2
                    ACT(fjunk[:, pi, :], xnw, AF.Square, kxh, [(k_fjunk, pi), (k_fss, pi, 0)], accum_out=fss[:, pi, 0:1])
                    ACT(fss[:, pi, 1:2], fss[:, pi, 0:1], AF.Ln, [(k_fss, pi, 0)], [(k_fss, pi, 1)], scale=1.0 / D, bias=EPS)
                    ACT(fss[:, pi, 2:3], fss[:, pi, 1:2], AF.Exp, [(k_fss, pi, 1)], [(k_fss, pi, 2)], scale=-0.5)
                    STT("dve", xt, xnw, fss[:, pi, 2:3], fnw, ALU.mult, ALU.mult, kxh + [(k_fss, pi, 2), k_fnw, kx], [kx])
                    P.dma("sp", y_out[(t - 2) * 128:(t - 1) * 128, :], xt, [kx], [("y", t)], is_output=True)
            RF.free_all()
            R0.free_all()
            RG.free_all()
            if want("xf%d" % l):
                P.dma("sp", dbg_out.rearrange("p (t f) -> p t f", t=NT), xres.rearrange("(t p) f -> p t f", p=128),
                      [("xres", t) for t in range(NT)], ["dbg"], is_output=True)
                break

        P.finish()
        P.emit()
        print("ops", P.nops, "arena peak", A.peak)
    return nc


def rope_tables():
    inv_freq = (np.float32(10000.0) ** (-np.arange(16, dtype=np.float32) / np.float32(16))).astype(np.float32)
    t = np.arange(SEQ)
    row = (t // 64).astype(np.float32)
    col = (t % 64).astype(np.float32)
    cosT = np.zeros((128, SEQ), np.float32)
    sinT = np.zeros((128, SEQ), np.float32)
    for p in range(128):
        d = p % 64
        axis, half, f = d // 32, (d % 32) // 16, d % 16
        ang = ((row if axis == 0 else col) * inv_freq[f]).astype(np.float32)
        cosT[p] = np.cos(ang)
        sinT[p] = np.sin(ang) * (-1.0 if half == 0 else 1.0)
    return cosT, sinT


def make_in_maps(inp):
    f = lambda a: np.ascontiguousarray(np.asarray(a, dtype=np.float32))
    cosT, sinT = rope_tables()
    shared = {
        "w_mod": f(inp["w_mod"]), "b_mod": f(inp["b_mod"]),
        "bmodT": f(np.asarray(inp["b_mod"]).reshape(DEPTH, 48, 128).transpose(0, 2, 1)),
        "n1wT": f(np.asarray(inp["norm1_w"]).reshape(DEPTH, 8, 128).transpose(0, 2, 1)),
        "n2wT": f(np.asarray(inp["norm2_w"]).reshape(DEPTH, 8, 128).transpose(0, 2, 1)),
        "ssdnwT": f(np.asarray(inp["ssd_norm_w"]).reshape(DEPTH, 8, 128).transpose(0, 2, 1)),
        "sublnT": f(np.asarray(inp["att_subln_w"]).reshape(DEPTH, 128, 1)),
        "w_in": f(inp["w_in"]),
        "ssdcw": f(np.asarray(inp["ssd_conv_w"]).transpose(0, 2, 1).reshape(DEPTH, 16, 128, 3).transpose(0, 2, 1, 3).reshape(DEPTH, 128, 48)),
        "ssdcb": f(np.asarray(inp["ssd_conv_b"]).reshape(DEPTH, 16, 128).transpose(0, 2, 1)),
        "alog": f(np.asarray(inp["ssd_a_log"]).reshape(DEPTH, 32)),
        "dtb": f(np.asarray(inp["ssd_dt_bias"]).reshape(DEPTH, 32)),
        "dskip": f(inp["ssd_d"]),
        "lamv": f(np.asarray(inp["diff_lambda"]).reshape(DEPTH, 256)),
        "w_br_ssd": f(inp["w_br_ssd"]), "w_br_att": f(inp["w_br_att"]), "w_out": f(inp["w_out"]),
        "w_up": f(inp["w_up"]),
        "ffncw": f(np.asarray(inp["ffn_conv_w"]).transpose(0, 2, 1).reshape(DEPTH, 2 * NFF, 128, 3).transpose(0, 2, 1, 3).reshape(DEPTH, 128, 2 * NFF * 3)),
        "ffncb": f(np.asarray(inp["ffn_conv_b"]).reshape(DEPTH, 2 * NFF, 128).transpose(0, 2, 1)),
        "w_down": f(inp["w_down"]),
        "fnw": f(inp["final_norm_w"]),
        "ropecos": cosT, "ropesin": sinT,
    }
    x = np.asarray(inp["x"], dtype=np.float32)
    ctx = np.asarray(inp["ctx"], dtype=np.float32)
    c = np.asarray(inp["c"], dtype=np.float32)
    c_ctx = np.asarray(inp["c_ctx"], dtype=np.float32)
    maps = []
    for b in range(8):
        cT = np.stack([c[b].reshape(8, 128).T, c_ctx.reshape(8, 128).T], axis=-1).reshape(128, 16)
        m = dict(shared)
        m["x_b"] = f(x[b])
        m["ctx_b"] = f(ctx[b])
        m["cT"] = f(cT)
        maps.append(m)
    return maps


def kernel(**inputs):
    nc = build_program()
    res = run_bass_kernel_spmd(nc, make_in_maps(inputs), core_ids=list(range(8)))
    return np.stack([np.asarray(r["y"], dtype=np.float32) for r in res.results], axis=0)
```

```python
import math
import os
XF = os.environ.get('KX', '')
from contextlib import ExitStack

import numpy as np
import concourse.bass as bass
import concourse.mybir as mybir
from concourse.bass_utils import run_bass_kernel_spmd

F32 = mybir.dt.float32
BF16 = mybir.dt.bfloat16
AF = mybir.ActivationFunctionType
ALU = mybir.AluOpType
AX = mybir.AxisListType

D = 1024
SEQ = 2048
CTX = 256
NTOK = SEQ + CTX
NT = NTOK // 128
DEPTH = 2
EPS = 1e-6
NH_SSD = 16
D_FF = 2816
NFF = D_FF // 128
IN_W = 8224
OFF_Z, OFF_XBC, OFF_DT, OFF_Q, OFF_K, OFF_V, OFF_G = 0, 1024, 3072, 3104, 4128, 5152, 6176
ATT_SCALE = 64 ** -0.5
BLOCKS = [(0, 256)] + [(256 + 512 * j, 512) for j in range(4)]

ENGS = ("pe", "act", "dve", "pool", "sp")


def _base(k):
    return k if isinstance(k, str) else k[0]


class Prog:
    NDMA = 8

    def __init__(self, nc, es):
        self.nc = nc
        self.lists = {e: [] for e in ENGS}
        self.cnt = {e: 0 for e in ENGS}
        self.sem = {e: es.enter_context(nc.semaphore("s_" + e)) for e in ENGS}
        self.waited = {e: {} for e in ENGS}
        self.last_w = {}
        self.readers = {}
        self.touch = {}
        self.inherit = {}
        self.seen = set()
        self.dsem = {}
        self.dcnt = {}
        self.drot = {q: 0 for q in ("sp", "act", "pool")}
        for q in ("sp", "act", "pool"):
            for i in range(self.NDMA):
                nm = "d_%s%d" % (q, i)
                self.dsem[nm] = es.enter_context(nc.semaphore(nm))
                self.dcnt[nm] = 0
        self.out_waits = {}
        self.nops = 0
        self.ps_last = {}

    def _semh(self, name):
        return self.sem[name] if name in self.sem else self.dsem[name]

    @staticmethod
    def _merge(d, e, s):
        if d.get(e, 0) < s:
            d[e] = s

    def _deps(self, reads, writes, eng=None):
        deps = {}
        for k in list(reads) + list(writes):
            if _base(k) == "ps":
                for e, s in self.ps_last.get(k, {}).items():
                    if e != eng:
                        self._merge(deps, e, s)
        for k in list(reads) + list(writes):
            if k not in self.seen:
                for e, s in self.inherit.get(_base(k), {}).items():
                    self._merge(deps, e, s)
        for k in reads:
            w = self.last_w.get(k)
            if w is not None:
                self._merge(deps, *w)
        for k in writes:
            w = self.last_w.get(k)
            if w is not None:
                self._merge(deps, *w)
            for e, s in self.readers.get(k, {}).items():
                self._merge(deps, e, s)
        return deps

    def _waits(self, eng, deps):
        out = []
        for e2, s2 in deps.items():
            if e2 == eng and eng == "pe":
                continue
            if self.waited[eng].get(e2, 0) >= s2:
                continue
            self.waited[eng][e2] = s2
            out.append((e2, s2))
        return out

    def _commit(self, tag, reads, writes):
        e, s = tag
        for k in reads:
            self.seen.add(k)
            self._merge(self.readers.setdefault(k, {}), e, s)
            self._merge(self.touch.setdefault(_base(k), {}), e, s)
        for k in writes:
            self.seen.add(k)
            self.last_w[k] = tag
            self.readers[k] = {}
            self._merge(self.touch.setdefault(_base(k), {}), e, s)
        for k in list(reads) + list(writes):
            if _base(k) == "ps":
                self.ps_last.setdefault(k, {})[e] = s

    def op(self, eng, emit, reads=(), writes=()):
        deps = self._deps(reads, writes, eng)
        waits = self._waits(eng, deps)
        self.cnt[eng] += 1
        self.lists[eng].append((waits, emit, (eng, 1)))
        self._commit((eng, self.cnt[eng]), reads, writes)
        self.nops += 1

    def dma(self, q, out, in_, reads=(), writes=(), is_output=False, **kw):
        deps = self._deps(reads, writes)
        nm = "d_%s%d" % (q, self.drot[q] % self.NDMA)
        self.drot[q] += 1
        if self.dcnt[nm] > 0:
            self._merge(deps, nm, self.dcnt[nm])
        waits = self._waits(q, deps)
        self.dcnt[nm] += 16
        seq = self.dcnt[nm]
        self.lists[q].append((waits, lambda e: e.dma_start(out=out, in_=in_, **kw), (nm, 16)))
        self._commit((nm, seq), reads, writes)
        if is_output:
            self._merge(self.out_waits, nm, seq)
        self.nops += 1

    def finish(self):
        self.lists["sp"].append((list(self.out_waits.items()), None, None))

    def emit(self):
        nc = self.nc
        with nc.Block() as block:
            def run(e, lst):
                for waits, emit, inc in lst:
                    for (nm, v) in waits:
                        e.wait_ge(self._semh(nm), v)
                    if emit is not None:
                        emit(e).then_inc(self._semh(inc[0]), inc[1])

            @block.tensor
            def _(e):
                run(e, self.lists["pe"])

            @block.scalar
            def _(e):
                run(e, self.lists["act"])

            @block.vector
            def _(e):
                run(e, self.lists["dve"])

            @block.gpsimd
            def _(e):
                run(e, self.lists["pool"])

            @block.sync
            def _(e):
                run(e, self.lists["sp"])


class Arena:
    def __init__(self, tensor, cap_bytes, prog):
        self.t = tensor
        self.cap = cap_bytes
        self.P = prog
        self.live = {}
        self.hist = []
        self.uid = 0
        self.peak = 0

    def alloc_at(self, name, shape, dt, off):
        esz = 4 if dt == F32 else 2
        n = int(np.prod(shape[1:]))
        size = (n * esz + 31) // 32 * 32
        assert off % 32 == 0
        assert off + size <= self.cap, "SBUF arena overflow at %s: %d > %d" % (name, off + size, self.cap)
        for k2, (o, s_) in self.live.items():
            assert not (o < off + size and off < o + s_), "overlap %s with live %s" % (name, k2)
        self.peak = max(self.peak, off + size)
        self.uid += 1
        key = "%s#%d" % (name, self.uid)
        inh = {}
        keep = []
        for (nm, o, s_) in self.hist:
            if o < off + size and off < o + s_:
                for e, q in self.P.touch.get(nm, {}).items():
                    Prog._merge(inh, e, q)
                for e, q in self.P.inherit.get(nm, {}).items():
                    Prog._merge(inh, e, q)
                if off <= o and o + s_ <= off + size:
                    continue
            keep.append((nm, o, s_))
        self.hist = keep
        self.P.inherit[key] = inh
        a = self.t[0:shape[0], off // 4:(off + size) // 4]
        if dt != F32:
            a = a.bitcast(dt)
        a = a[:, 0:n]
        if len(shape) == 3:
            a = a.rearrange("p (a b) -> p a b", a=shape[1])
        elif len(shape) == 4:
            a = a.rearrange("p (a b c) -> p a b c", a=shape[1], b=shape[2])
        self.live[key] = (off, size)
        return a, key, size

    def free(self, key):
        o, s_ = self.live.pop(key)
        self.hist.append((key, o, s_))


class Region:
    def __init__(self, arena, lo, hi):
        self.A, self.lo, self.hi = arena, lo, hi
        self.p = lo
        self.keys = []

    def alloc(self, name, shape, dt):
        a, key, size = self.A.alloc_at(name, shape, dt, self.p)
        self.p += size
        assert self.p <= self.hi, "region overflow at %s: %d > %d" % (name, self.p, self.hi)
        self.keys.append(key)
        return a, key

    def free_all(self):
        for k in self.keys:
            self.A.free(k)
        self.keys = []
        self.p = self.lo


def build_program(dbg=None):
    nc = bass.Bass("TRN2", target_bir_lowering=False)

    def din(name, shape):
        return nc.dram_tensor(name, list(shape), F32, kind="ExternalInput").ap()

    x_in = din("x_b", [SEQ, D])
    ctx_in = din("ctx_b", [CTX, D])
    cT_in = din("cT", [128, 16])
    w_mod = din("w_mod", [DEPTH, D, 6 * D])
    b_mod = din("b_mod", [DEPTH, 6 * D])
    bmodT_in = din("bmodT", [DEPTH, 128, 48])
    n1wT_in = din("n1wT", [DEPTH, 128, 8])
    n2wT_in = din("n2wT", [DEPTH, 128, 8])
    ssdnwT_in = din("ssdnwT", [DEPTH, 128, 8])
    sublnT_in = din("sublnT", [DEPTH, 128, 1])
    w_in = din("w_in", [DEPTH, D, IN_W])
    ssdcw_in = din("ssdcw", [DEPTH, 128, 16 * 3])
    ssdcb_in = din("ssdcb", [DEPTH, 128, 16])
    alog_in = din("alog", [DEPTH, 32])
    dtb_in = din("dtb", [DEPTH, 32])
    dskip_in = din("dskip", [DEPTH, 16])
    lam_in = din("lamv", [DEPTH, 256])
    w_brs = din("w_br_ssd", [DEPTH, D, D])
    w_bra = din("w_br_att", [DEPTH, D, D])
    w_out = din("w_out", [DEPTH, D, D])
    w_up = din("w_up", [DEPTH, D, 2 * D_FF])
    ffncw_in = din("ffncw", [DEPTH, 128, 2 * NFF * 3])
    ffncb_in = din("ffncb", [DEPTH, 128, 2 * NFF])
    w_down = din("w_down", [DEPTH, D_FF, D])
    fnw_in = din("fnw", [D])
    cos_in = din("ropecos", [128, SEQ])
    sin_in = din("ropesin", [128, SEQ])
    y_out = nc.dram_tensor("y", [SEQ, D], F32, kind="ExternalOutput").ap()
    xres = nc.dram_tensor("xres", [NTOK, D], F32).ap()
    dbg_out = None
    if dbg is not None:
        dbg_out = nc.dram_tensor("dbg", [128, dbg[1]], F32, kind="ExternalOutput").ap()

    es = ExitStack()
    with es:
        P = Prog(nc, es)
        CAP = 206 * 1024
        arena_t = es.enter_context(nc.sbuf_tensor("arena", [128, CAP // 4], F32))
        psum = es.enter_context(nc.psum_tensor("psum", [128, 4096], F32))
        A = Arena(arena_t, CAP, P)

        def PS(b, n=512, off=0):
            return psum[:, b * 512 + off: b * 512 + off + n]

        def PSK(b):
            return ("ps", b)

        def MM(out, lhsT, rhs, start, stop, r, w, skip=False):
            if skip:
                P.op("pe", lambda e: e.matmul(out, lhsT=lhsT, rhs=rhs, start=start, stop=stop, skip_group_check=True), r, w)
            else:
                P.op("pe", lambda e: e.matmul(out, lhsT=lhsT, rhs=rhs, start=start, stop=stop), r, w)

        def ACT(out, in_, func, r, w, **kw):
            P.op("act", lambda e: e.activation(out=out, in_=in_, func=func, **kw), r, w)

        def TT(eng, out, in0, in1, op, r, w):
            P.op(eng, lambda e: e.tensor_tensor(out=out, in0=in0, in1=in1, op=op), r, w)

        def TS(eng, out, in0, s1, s2, op0, op1, r, w):
            if op1 is None:
                P.op(eng, lambda e: e.tensor_scalar(out=out, in0=in0, scalar1=s1, scalar2=None, op0=op0), r, w)
            else:
                P.op(eng, lambda e: e.tensor_scalar(out=out, in0=in0, scalar1=s1, scalar2=s2, op0=op0, op1=op1), r, w)

        def STT(eng, out, in0, scalar, in1, op0, op1, r, w):
            P.op(eng, lambda e: e.scalar_tensor_tensor(out=out, in0=in0, scalar=scalar, in1=in1, op0=op0, op1=op1), r, w)

        def CP(eng, out, in_, r, w):
            P.op(eng, lambda e: e.tensor_copy(out=out, in_=in_), r, w)

        def MSET(eng, ap, val, w):
            P.op(eng, lambda e: e.memset(ap, val), (), w)

        def RECIP(out, in_, r, w):
            P.op("dve", lambda e: e.reciprocal(out=out, in_=in_), r, w)

        def TR(out, in_, r, w):
            P.op("pe", lambda e: e.transpose(out=out, in_=in_, identity=ident), list(r) + [k_ident], w)

        def WLOAD(dst, src, r, w):
            P.dma("pool", dst, src, r, w)

        def wsrc(wt, l, c0, n):
            return wt[l].rearrange("(kc p) n -> p kc n", p=128)[:, :, c0:c0 + n]

        BASE = 27 * 1024
        SZ = 36864
        R_P = Region(A, 0, BASE)
        R0 = Region(A, BASE, BASE + SZ)
        R1 = Region(A, BASE + SZ, BASE + 2 * SZ)
        R2 = Region(A, BASE + 2 * SZ, BASE + 3 * SZ)
        R3 = Region(A, BASE + 3 * SZ, CAP)
        R23 = Region(A, BASE + 2 * SZ, CAP)

        ident, k_ident = R_P.alloc("ident", [128, 128], BF16)
        identf, k_identf = R_P.alloc("identf", [128, 128], F32)
        tri, k_tri = R_P.alloc("tri", [128, 4, 128], F32)
        ones, k_ones = R_P.alloc("ones", [128, 128], F32)
        MSET("pool", identf, 0.0, [k_identf])
        P.op("pool", lambda e: e.affine_select(out=identf, in_=identf, pattern=[[-1, 128]], compare_op=ALU.not_equal,
                                               fill=1.0, base=0, channel_multiplier=1), [k_identf], [k_identf])
        CP("dve", ident, identf, [k_identf], [k_ident])
        MSET("pool", ones, 1.0, [k_ones])
        MSET("pool", tri, 1.0, [k_tri])
        P.op("pool", lambda e: e.affine_select(out=tri[:, 0, :], in_=tri[:, 0, :], pattern=[[1, 128]], compare_op=ALU.is_ge,
                                               fill=0.0, base=0, channel_multiplier=-1), [k_tri], [k_tri])
        P.op("pool", lambda e: e.affine_select(out=tri[:, 1, :], in_=tri[:, 1, :], pattern=[[-1, 128]], compare_op=ALU.is_ge,
                                               fill=0.0, base=0, channel_multiplier=1), [k_tri], [k_tri])
        P.op("pool", lambda e: e.affine_select(out=tri[:, 2, :], in_=tri[:, 2, :], pattern=[[-1, 128]], compare_op=ALU.is_gt,
                                               fill=0.0, base=0, channel_multiplier=1), [k_tri], [k_tri])
        P.op("pool", lambda e: e.affine_select(out=tri[:, 3, :], in_=tri[:, 3, :], pattern=[[1, 128]], compare_op=ALU.is_gt,
                                               fill=0.0, base=0, channel_multiplier=-1), [k_tri], [k_tri])

        permT, k_perm = R_P.alloc("permT", [128, 128], BF16)
        iv = ident.rearrange("p (a hf f) -> p a hf f", hf=2, f=16)
        pv = permT.rearrange("p (a hf f) -> p a hf f", hf=2, f=16)
        CP("pool", pv[:, :, 0, :], iv[:, :, 1, :], [k_ident], [(k_perm, 0)])
        CP("pool", pv[:, :, 1, :], iv[:, :, 0, :], [k_ident], [(k_perm, 1)])
        cT, k_cT = R_P.alloc("cT", [128, 8, 2], F32)
        scT, k_scT = R_P.alloc("scT", [128, 8, 2], BF16)
        screp, k_screp = R_P.alloc("screp", [128, 8, 2, 128], BF16)
        P.dma("sp", cT, cT_in.rearrange("p (k w) -> p k w", w=2), (), [k_cT])
        ACT(scT, cT, AF.Silu, [k_cT], [k_scT])
        CP("dve", screp, scT.unsqueeze(3).to_broadcast([128, 8, 2, 128]), [k_scT], [k_screp])
        p_mark = (R_P.p, len(R_P.keys))

        dbgbuf = es.enter_context(nc.sbuf_tensor("dbgbuf", [128, 2, 128], F32)) if dbg is not None else None

        def dump(ap, ncols, rkeys, reg=None):
            for i, c0 in enumerate(range(0, ncols, 128)):
                n = min(128, ncols - c0)
                CP("dve", dbgbuf[:, i % 2, 0:n], ap[:, c0:c0 + n], rkeys, [("dbgbuf", i % 2)])
                P.dma("sp", dbg_out[:, c0:c0 + n], dbgbuf[:, i % 2, 0:n], [("dbgbuf", i % 2)], [("dbg", i)], is_output=True)

        def want(stage):
            return dbg is not None and dbg[0] == stage

        def norm_front(xt, kx, n_feat, bufs, i):
            junk, kj, ss, kss, xn, kxn, ev, kev = bufs
            nkc = n_feat // 128
            kxl = list(kx) if isinstance(kx, list) else [kx]
            ACT(junk[:, i % 2, 0:n_feat], xt, AF.Square, kxl, [(kj, i % 2), (kss, i % 2, 0)], accum_out=ss[:, i % 2, 0:1])
            ACT(ss[:, i % 2, 1:2], ss[:, i % 2, 0:1], AF.Ln, [(kss, i % 2, 0)], [(kss, i % 2, 1)], scale=1.0 / n_feat, bias=EPS)
            ACT(ss[:, i % 2, 2:3], ss[:, i % 2, 1:2], AF.Exp, [(kss, i % 2, 1)], [(kss, i % 2, 2)], scale=-0.5)
            TS("dve", xn[:, i % 2, 0:n_feat], xt, ss[:, i % 2, 2:3], None, ALU.mult, None, kxl + [(kss, i % 2, 2)], [(kxn, i % 2)])
            b = 6 + (i % 2)
            pt = PS(b).bitcast(BF16)
            for kc in range(nkc):
                TR(pt[:, kc * 128:(kc + 1) * 128], xn[:, i % 2, kc * 128:(kc + 1) * 128], [(kxn, i % 2)], [PSK(b)])

        def norm_back(t, scale_ap, bias_ap, extra_r, dstT, kdst, bufs, i):
            junk, kj, ss, kss, xn, kxn, ev, kev = bufs
            b = 6 + (i % 2)
            pt = PS(b).bitcast(BF16).rearrange("p (k t) -> p k t", k=8)
            dst = dstT[:, :, t * 128:(t + 1) * 128]
            sc = scale_ap.unsqueeze(2).to_broadcast([128, 8, 128])
            if bias_ap is None:
                TT("dve", dst, pt, sc, ALU.mult, [PSK(b)] + list(extra_r), [(kdst, t)])
            else:
                TT("dve", ev[:, i % 2], pt, sc, ALU.mult, [PSK(b)] + list(extra_r), [(kev, i % 2)])
                TT("dve", dst, ev[:, i % 2], bias_ap.unsqueeze(2).to_broadcast([128, 8, 128]), ALU.add, [(kev, i % 2)] + list(extra_r), [(kdst, t)])

        def norm_bufs(reg):
            junk, kj = reg.alloc("junk", [128, 2, D], BF16)
            ss, kss = reg.alloc("ss", [128, 2, 4], F32)
            xn, kxn = reg.alloc("xn", [128, 2, D], BF16)
            ev, kev = reg.alloc("nev", [128, 2, 8, 128], F32)
            return (junk, kj, ss, kss, xn, kxn, ev, kev)

        for l in range(DEPTH):
            with_ctx = l < DEPTH - 1
            T0 = 0 if with_ctx else 2
            tiles_q = list(range(T0, NT))
            blocks_q = BLOCKS[(0 if with_ctx else 1):]
            lam_init = 0.8 - 0.6 * math.exp(-0.3 * l)

            def xsrc(t, l=l):
                if l == 0:
                    return ctx_in[t * 128:(t + 1) * 128, :] if t < 2 else x_in[(t - 2) * 128:(t - 1) * 128, :]
                return xres[t * 128:(t + 1) * 128, :]

            for k in R_P.keys[p_mark[1]:]:
                A.free(k)
            del R_P.keys[p_mark[1]:]
            R_P.p = p_mark[0]

            bmodT, k_bmodT = R_P.alloc("bmodT", [128, 48], F32)
            modT, k_modT = R_P.alloc("modT", [128, 48, 2], F32)
            scl1, k_scl1 = R_P.alloc("scl1", [128, 8, 2], F32)
            scl2, k_scl2 = R_P.alloc("scl2", [128, 8, 2], F32)
            n1w, k_n1w = R_P.alloc("n1w", [128, 8], F32)
            n2w, k_n2w = R_P.alloc("n2w", [128, 8], F32)
            g1bc, k_g1bc = R_P.alloc("g1bc", [128, 2, D], F32)
            g2bc, k_g2bc = R_P.alloc("g2bc", [128, 2, D], F32)
            P.dma("sp", bmodT, bmodT_in[l], (), [k_bmodT])
            P.dma("sp", n1w, n1wT_in[l], (), [k_n1w])
            P.dma("sp", n2w, n2wT_in[l], (), [k_n2w])
            bmrow, k_bmrow = R3.alloc("bmrow", [128, 2, D], F32)
            P.dma("sp", bmrow[:, 0, :], b_mod[l, 2 * D:3 * D].partition_broadcast(128), (), [(k_bmrow, 0)])
            P.dma("sp", bmrow[:, 1, :], b_mod[l, 5 * D:6 * D].partition_broadcast(128), (), [(k_bmrow, 1)])
            wm = [R3.alloc("wmod%d" % i, [128, 8, 512], BF16) for i in range(2)]
            hT, k_hT = R0.alloc("hT", [128, 8, NTOK], BF16)
            xts = [R3.alloc("xt%d" % i, [128, D], F32) for i in range(3)]
            nbufs = norm_bufs(R3)

            def mod_chunk(cb):
                wt, kw = wm[cb % 2]
                WLOAD(wt, wsrc(w_mod, l, cb * 512, 512), (), [kw])
                which = cb // 2
                if which in (2, 5):
                    gb, kg = (g1bc, k_g1bc) if which == 2 else (g2bc, k_g2bc)
                    hf = cb % 2
                    for w_ in range(2):
                        b = (cb * 2 + w_) % 4
                        for kc in range(8):
                            MM(PS(b), screp[:, kc, w_, :], wt[:, kc, :], kc == 0, kc == 7, [k_screp, kw], [PSK(b)])
                        TT("dve", gb[:, w_, hf * 512:(hf + 1) * 512], PS(b), bmrow[:, 0 if which == 2 else 1, hf * 512:(hf + 1) * 512],
                           ALU.add, [PSK(b), (k_bmrow, 0 if which == 2 else 1)], [(kg, w_, hf)])
                else:
                    for j4 in range(4):
                        j = cb * 4 + j4
                        b = 4 + (j % 2)
                        for kc in range(8):
                            MM(PS(b, 2), wt[:, kc, j4 * 128:(j4 + 1) * 128], scT[:, kc, :], kc == 0, kc == 7, [k_scT, kw], [PSK(b)])
                        TS("dve", modT[:, j, :], PS(b, 2), bmodT[:, j:j + 1], None, ALU.add, None, [PSK(b), k_bmodT], [(k_modT, j)])

            for cb in range(4):
                mod_chunk(cb)
            STT("dve", scl1, modT[:, 8:16, :], 1.0, n1w.unsqueeze(2).to_broadcast([128, 8, 2]), ALU.add, ALU.mult,
                [(k_modT, j) for j in range(8, 16)] + [k_n1w], [k_scl1])

            def n1_front(t):
                xt, kx = xts[t % 3]
                P.dma("sp", xt, xsrc(t), [("xres", t)], [kx])
                norm_front(xt, kx, D, nbufs, t)

            rest = list(range(4, 12))
            n1_front(0)
            for t in range(NT):
                if t + 1 < NT:
                    n1_front(t + 1)
                w_ = 1 if t < 2 else 0
                norm_back(t, scl1[:, :, w_], modT[:, 0:8, w_], [k_scl1] + [(k_modT, j) for j in range(8)], hT, k_hT, nbufs, t)
                if t % 2 == 1 and rest:
                    mod_chunk(rest.pop(0))
            while rest:
                mod_chunk(rest.pop(0))
            STT("dve", scl2, modT[:, 32:40, :], 1.0, n2w.unsqueeze(2).to_broadcast([128, 8, 2]), ALU.add, ALU.mult,
                [(k_modT, j) for j in range(32, 40)] + [k_n2w], [k_scl2])
            k_g1 = [(k_g1bc, w_, hf) for w_ in range(2) for hf in range(2)]
            k_g2 = [(k_g2bc, w_, hf) for w_ in range(2) for hf in range(2)]
            R3.free_all()
            hT_all = [(k_hT, t) for t in range(NT)]
            if want("hT%d" % l):
                dump(hT.rearrange("p a b -> p (a b)"), 8 * NTOK, hT_all, R3)
                break

            ybuf, k_ybuf = R1.alloc("ybuf", [128, NT, D], BF16)
            S = R23
            ssdcw, k_ssdcw = S.alloc("ssdcw", [128, 16, 3], F32)
            ssdcb, k_ssdcb = S.alloc("ssdcb", [128, 16], F32)
            alog, k_alog = S.alloc("alog", [128, 32], F32)
            dtb, k_dtb = S.alloc("dtb", [128, 32], F32)
            dsk, k_dsk = S.alloc("dsk", [128, 16], F32)
            Aneg, k_Aneg = S.alloc("Aneg", [128, 32], F32)
            P.dma("sp", ssdcw, ssdcw_in[l].rearrange("p (c k) -> p c k", k=3), (), [k_ssdcw])
            P.dma("sp", ssdcb, ssdcb_in[l], (), [k_ssdcb])
            P.dma("sp", alog, alog_in[l].partition_broadcast(128), (), [k_alog])
            P.dma("sp", dtb, dtb_in[l].partition_broadcast(128), (), [k_dtb])
            P.dma("sp", dsk, dskip_in[l].partition_broadcast(128), (), [k_dsk])
            ACT(Aneg, alog, AF.Exp, [k_alog], [k_Aneg])
            TS("dve", Aneg, Aneg, -1.0, None, ALU.mult, None, [k_Aneg], [k_Aneg])
            wdt, k_wdt = S.alloc("wdt", [128, 8, 32], BF16)
            WLOAD(wdt, wsrc(w_in, l, OFF_DT, 32), (), [k_wdt])
            dt_all, k_dt = S.alloc("dt_all", [128, NT, 32], F32)
            a_all, k_a = S.alloc("a_all", [128, NT, 32], F32)
            eacs, k_eacs = S.alloc("eacs", [128, NT, 32], F32)
            edte, k_edte = S.alloc("edte", [128, NT, 32], F32)
            etot, k_etot = S.alloc("etot", [128, NT, 32], F32)
            dtdte, k_dtdte = S.alloc("dtdte", [128, NT, 32], F32)
            RT = Region(A, CAP - 2 * 2304, CAP)
            tmpa, k_tmpa = RT.alloc("tmpa", [128, NT, 32], F32)
            tmpb, k_tmpb = RT.alloc("tmpb", [128, NT, 32], F32)

            def pview(b0):
                return psum[:, b0 * 512: b0 * 512 + NT * 32].rearrange("p (t c) -> p t c", c=32)

            def pkeys(b0):
                return [PSK(b0), PSK(b0 + 1)]

            for t in range(NT):
                for kc in range(8):
                    MM(psum[:, t * 32:(t + 1) * 32], hT[:, kc, t * 128:(t + 1) * 128], wdt[:, kc, :], kc == 0, kc == 7,
                       [(k_hT, t), k_wdt], [PSK(0 if t < 16 else 1)])
            TT("dve", tmpa, pview(0), dtb.unsqueeze(1).to_broadcast([128, NT, 32]), ALU.add, pkeys(0) + [k_dtb], [k_tmpa])
            ACT(tmpb, tmpa, AF.Exp, [k_tmpa], [k_tmpb])
            ACT(dt_all, tmpb, AF.Ln, [k_tmpb], [k_dt], bias=1.0)
            TT("dve", a_all, dt_all, Aneg.unsqueeze(1).to_broadcast([128, NT, 32]), ALU.mult, [k_dt, k_Aneg], [k_a])
            RT.free_all()
            for t in range(NT):
                bk = lambda b0: [PSK(b0 + (0 if t < 16 else 1))]
                for d_ in range(2):
                    sl = slice(t * 32 + d_ * 16, t * 32 + d_ * 16 + 16)
                    MM(psum[:, 1024 + sl.start:1024 + sl.stop], tri[:, d_, :], a_all[:, t, d_ * 16:(d_ + 1) * 16], True, True,
                       [k_tri, k_a], bk(2))
                    MM(psum[:, 2048 + sl.start:2048 + sl.stop], tri[:, 2 + d_, :], a_all[:, t, d_ * 16:(d_ + 1) * 16], True, True,
                       [k_tri, k_a], bk(4))
                MM(psum[:, t * 32:(t + 1) * 32], ones, a_all[:, t, :], True, True, [k_ones, k_a], bk(0))
            ACT(eacs, pview(2), AF.Exp, pkeys(2), [k_eacs])
            ACT(edte, pview(4), AF.Exp, pkeys(4), [k_edte])
            ACT(etot, pview(0), AF.Exp, pkeys(0), [k_etot])
            TT("dve", dtdte, dt_all, edte, ALU.mult, [k_dt, k_edte], [k_dtdte])

            if want("dt%d" % l):
                tmp, k_tmp = S.alloc("dd", [128, 5 * NT * 32], F32)
                for i_, (ap_, k_) in enumerate(((dt_all, k_dt), (a_all, k_a), (eacs, k_eacs), (edte, k_edte), (etot, k_etot))):
                    CP("dve", tmp[:, i_ * NT * 32:(i_ + 1) * NT * 32], ap_.rearrange("p a b -> p (a b)"), [k_], [k_tmp])
                P.dma("sp", dbg_out[:, 0:5 * NT * 32], tmp, [k_tmp], ["dbg"], is_output=True)
                break
            wgs = [S.alloc("wg%d" % i, [128, 8, 512], BF16) for i in range(2)]

            def load_wg(g):
                wg, kwg = wgs[g % 2]
                WLOAD(wg[:, :, 0:256], wsrc(w_in, l, OFF_XBC + g * 256, 256), (), [(kwg, 0)])
                WLOAD(wg[:, :, 256:384], wsrc(w_in, l, OFF_XBC + 1024 + g * 128, 128), (), [(kwg, 1)])
                WLOAD(wg[:, :, 384:512], wsrc(w_in, l, OFF_XBC + 1536 + g * 128, 128), (), [(kwg, 2)])

            load_wg(0)
            load_wg(1)
            xbcT, k_xbcT = S.alloc("xbcT", [128, 4, NTOK], BF16)
            xs_tok, k_xs = S.alloc("xs_tok", [128, NT, 256], BF16)
            B_tok, k_Bt = S.alloc("B_tok", [128, NT, 128], BF16)
            CBms = [S.alloc("CBm%d" % i, [128, 2, 128], BF16) for i in range(2)]
            rhsb, k_rhsb = S.alloc("rhsb", [128, 2, 4, 128], F32)
            expEs = [S.alloc("expE%d" % i, [128, 2, 4, 128], BF16) for i in range(2)]
            MTs = [S.alloc("MT%d" % i, [128, 2, 4, 128], BF16) for i in range(2)]
            xds = [S.alloc("xd%d" % i, [128, 2, 4, 64], BF16) for i in range(2)]
            xdds = [S.alloc("xdd%d" % i, [128, 4, 64], BF16) for i in range(4)]
            t1, k_t1 = S.alloc("t1", [128, 4, 64], F32)
            t2, k_t2 = S.alloc("t2", [128, 4, 64], F32)
            t1b, k_t1b = S.alloc("t1b", [128, 4, 64], F32)
            t3s = [S.alloc("t3%d" % i, [128, 4, 64], F32) for i in range(2)]
            Hrun, k_Hrun = S.alloc("Hrun", [128, 2, 2, 256], F32)
            SX = Region(A, S.p, CAP)

            def h4(ap):
                return ap.rearrange("p (h d) -> p h d", h=4)

            for g in range(4):
                wg, kwg = wgs[g % 2]
                Upad, k_U = SX.alloc("Upad", [128, NTOK + 4], F32)
                acc, k_acc = SX.alloc("acc", [128, NTOK], F32)
                MSET("pool", Upad[:, 0:1], 0.0, [(k_U, "p0")])
                MSET("pool", Upad[:, 257:259], 0.0, [(k_U, "p1")])
                MSET("pool", Upad[:, 2307:2308], 0.0, [(k_U, "p2")])
                for cc in range(4):
                    cidx = (g * 2 + cc) if cc < 2 else (8 + g if cc == 2 else 12 + g)
                    kwp = (kwg, 0) if cc < 2 else (kwg, cc - 1)
                    for bi, (t0, n) in enumerate(BLOCKS):
                        b = (cc * 5 + bi) % 6
                        for kc in range(8):
                            MM(PS(b, n), wg[:, kc, cc * 128:(cc + 1) * 128], hT[:, kc, t0:t0 + n], kc == 0, kc == 7,
                               [kwp] + [(k_hT, t) for t in range(t0 // 128, (t0 + n) // 128)], [PSK(b)])
                        off = t0 + 1 if t0 == 0 else t0 + 3
                        ACT(Upad[:, off:off + n], PS(b, n), AF.Copy, [PSK(b)], [(k_U, bi)])
                    ukeys = [(k_U, bi) for bi in range(5)] + [(k_U, "p0"), (k_U, "p1"), (k_U, "p2")]
                    for (o0, n, u0) in ((0, 256, 0), (256, 2048, 258)):
                        ka = (k_acc, o0)
                        TS("dve", acc[:, o0:o0 + n], Upad[:, u0:u0 + n], ssdcw[:, cidx, 0:1], None, ALU.mult, None,
                           ukeys + [k_ssdcw], [ka])
                        STT("dve", acc[:, o0:o0 + n], Upad[:, u0 + 1:u0 + 1 + n], ssdcw[:, cidx, 1:2], acc[:, o0:o0 + n], ALU.mult, ALU.add,
                            ukeys + [k_ssdcw, ka], [ka])
                        STT("dve", acc[:, o0:o0 + n], Upad[:, u0 + 2:u0 + 2 + n], ssdcw[:, cidx, 2:3], acc[:, o0:o0 + n], ALU.mult, ALU.add,
                            ukeys + [k_ssdcw, ka], [ka])
                        ACT(xbcT[:, cc, o0:o0 + n], acc[:, o0:o0 + n], AF.Silu, [ka, k_ssdcb], [(k_xbcT, cc, o0)], bias=ssdcb[:, cidx:cidx + 1])
                SX.free_all()
                if want("xbc%d" % l):
                    dump(xbcT.rearrange("p a b -> p (a b)"), 4 * NTOK, [(k_xbcT, cc, o0) for cc in range(4) for o0 in (0, 256)], SX)
                    break
                if g + 2 < 4 and 'a' not in XF:
                    load_wg(g + 2)
                xk = lambda cc, t: (k_xbcT, cc, 0 if t < 2 else 256)
                for t in range(NT):
                    b = 6 + (t % 2)
                    pt = PS(b).bitcast(BF16)
                    for cc in range(3):
                        TR(pt[:, cc * 128:(cc + 1) * 128], xbcT[:, cc, t * 128:(t + 1) * 128], [xk(cc, t)], [PSK(b)])
                    if 'b' not in XF:
                        CP("dve", xs_tok[:, t, :], pt[:, 0:256], [PSK(b)], [(k_xs, t)])
                    if 'c' not in XF:
                        ACT(B_tok[:, t, :], pt[:, 256:384], AF.Copy, [PSK(b)], [(k_Bt, t)])
                if want("tok%d" % l):
                    dump(xs_tok.rearrange("p a b -> p (a b)"), NT * 256, [(k_xs, t) for t in range(NT)])
                    break
                Hin, k_Hin = SX.alloc("Hin", [128, 2, NT, 256], BF16)
                orders = [list(range(NT)), [1, 0] + list(range(NT - 1, 1, -1))]
                MSET("pool", Hrun[:, :, 0, :], 0.0, [(k_Hrun, d_, 0, h_) for d_ in range(2) for h_ in range(4)])

                def emit_xdd(i, d_):
                    c = orders[d_][i]
                    hs = slice(d_ * 16 + g * 4, d_ * 16 + g * 4 + 4)
                    xdd, k_xdd = xdds[(2 * i + d_) % 4]
                    TT("pool", xdd, h4(xs_tok[:, c, :]), dtdte[:, c, hs].unsqueeze(2).to_broadcast([128, 4, 64]), ALU.mult,
                       [(k_xs, c), k_dtdte], [k_xdd])

                for d_ in range(2):
                    emit_xdd(0, d_)
                for i in range(NT):
                    for d_ in range(2):
                        if i + 1 < NT - 1:
                            emit_xdd(i + 1, d_)
                    for d_ in range(2):
                        c = orders[d_][i]
                        pp = i % 2
                        ACT(Hin[:, d_, c, :], Hrun[:, d_, pp, :], AF.Copy, [(k_Hrun, d_, pp, h_) for h_ in range(4)], [(k_Hin, d_, c)])
                        if i == NT - 1:
                            continue
                        xdd, k_xdd = xdds[(2 * i + d_) % 4]
                        b = 6 + ((2 * i + d_) % 2)
                        MM(PS(b, 256), B_tok[:, c, :], xdd.rearrange("p h d -> p (h d)"), True, True, [(k_Bt, c), k_xdd], [PSK(b)])
                        for h_ in range(4):
                            hh = d_ * 16 + g * 4 + h_
                            STT("dve", Hrun[:, d_, 1 - pp, h_ * 64:(h_ + 1) * 64], Hrun[:, d_, pp, h_ * 64:(h_ + 1) * 64], etot[:, c, hh:hh + 1],
                                PS(b, 64, off=h_ * 64), ALU.mult, ALU.add, [(k_Hrun, d_, pp, h_), k_etot, PSK(b)], [(k_Hrun, d_, 1 - pp, h_)])
                dt_dh = lambda c: dt_all[:, c, :].rearrange("p (d h) -> p d h", d=2)[:, :, g * 4:g * 4 + 4]
                a_dh = lambda c: a_all[:, c, :].rearrange("p (d h) -> p d h", d=2)[:, :, g * 4:g * 4 + 4]

                def prepA1(c):
                    csl = slice(c * 128, (c + 1) * 128)
                    bA = c % 2
                    cbm, k_cbm = CBms[c % 2]
                    MM(PS(bA, 128), xbcT[:, 2, csl], xbcT[:, 3, csl], True, True, [xk(2, c), xk(3, c)], [PSK(bA)])
                    TT("dve", cbm, PS(bA, 128).unsqueeze(1).to_broadcast([128, 2, 128]), tri[:, 0:2, :], ALU.mult, [PSK(bA), k_tri], [k_cbm])
                    TT("pool", rhsb[:, 0], a_dh(c)[:, 0, :].unsqueeze(2).to_broadcast([128, 4, 128]),
                       tri[:, 0, :].unsqueeze(1).to_broadcast([128, 4, 128]), ALU.mult, [k_a, k_tri], [(k_rhsb, 0)])
                    TT("dve", rhsb[:, 1], a_dh(c)[:, 1, :].unsqueeze(2).to_broadcast([128, 4, 128]),
                       tri[:, 1, :].unsqueeze(1).to_broadcast([128, 4, 128]), ALU.mult, [k_a, k_tri], [(k_rhsb, 1)])

                def prepA2(c):
                    ee, k_ee = expEs[c % 2]
                    for d_ in range(2):
                        MM(PS(2 + d_), tri[:, 2 + d_, :], rhsb[:, d_].rearrange("p h l -> p (h l)"), True, True, [k_tri, (k_rhsb, d_)], [PSK(2 + d_)])
                        ACT(ee[:, d_].rearrange("p h l -> p (h l)"), PS(2 + d_), AF.Exp, [PSK(2 + d_)], [(k_ee, d_)])

                def prepB(c):
                    mt, k_mt = MTs[c % 2]
                    xd, k_xd = xds[c % 2]
                    cbm, k_cbm = CBms[c % 2]
                    ee, k_ee = expEs[c % 2]
                    TT("dve", mt, ee, cbm.unsqueeze(2).to_broadcast([128, 2, 4, 128]), ALU.mult, [(k_ee, 0), (k_ee, 1), k_cbm], [k_mt])
                    TT("pool", xd, h4(xs_tok[:, c, :]).unsqueeze(1).to_broadcast([128, 2, 4, 64]),
                       dt_dh(c).unsqueeze(3).to_broadcast([128, 2, 4, 64]), ALU.mult, [(k_xs, c), k_dt], [k_xd])
                    TT("pool", t3s[c % 2][0], h4(xs_tok[:, c, :]), dsk[:, g * 4:g * 4 + 4].unsqueeze(2).to_broadcast([128, 4, 64]), ALU.mult,
                       [(k_xs, c), k_dsk], [t3s[c % 2][1]])

                def finish_pe(c):
                    csl = slice(c * 128, (c + 1) * 128)
                    bY = 4 + (c % 2)
                    bO = 6 + (c % 2)
                    mt, k_mt = MTs[c % 2]
                    xd, k_xd = xds[c % 2]
                    for d_ in range(2):
                        for h_ in range(4):
                            MM(PS(bY, 64, off=h_ * 64), mt[:, d_, h_, :], xd[:, d_, h_, :], d_ == 0 and h_ == 0, d_ == 1, [k_mt, k_xd], [PSK(bY)], skip=True)
                    MM(PS(bY, 256, off=256), xbcT[:, 3, csl], Hin[:, 0, c, :], False, True, [xk(3, c), (k_Hin, 0, c)], [PSK(bY)], skip=True)
                    MM(PS(bO, 256), xbcT[:, 3, csl], Hin[:, 1, c, :], True, True, [xk(3, c), (k_Hin, 1, c)], [PSK(bO)])

                def finish_dve(c):
                    bY = 4 + (c % 2)
                    bO = 6 + (c % 2)
                    t3, k_t3 = t3s[c % 2]
                    TT("dve", t1, h4(PS(bY, 256, off=256)), eacs[:, c, g * 4:g * 4 + 4].unsqueeze(2).to_broadcast([128, 4, 64]), ALU.mult,
                       [PSK(bY), k_eacs], [k_t1])
                    TT("dve", t2, h4(PS(bO, 256)), eacs[:, c, 16 + g * 4:16 + g * 4 + 4].unsqueeze(2).to_broadcast([128, 4, 64]), ALU.mult,
                       [PSK(bO), k_eacs], [k_t2])
                    TT("pool", t2, t2, t3, ALU.add, [k_t2, k_t3], [k_t2])
                    TT("dve", t1b, h4(PS(bY, 256)), t1, ALU.add, [PSK(bY), k_t1], [k_t1b])
                    TT("dve", h4(ybuf[:, c, g * 256:(g + 1) * 256]), t1b, t2, ALU.add, [k_t1b, k_t2], [(k_ybuf, c, g)])

                oc = tiles_q
                no = len(oc)
                prepA1(oc[0])
                prepA2(oc[0])
                prepA1(oc[1])
                prepA2(oc[1])
                prepB(oc[0])
                for ci in range(no):
                    if ci + 2 < no:
                        prepA1(oc[ci + 2])
                    finish_pe(oc[ci])
                    if ci + 2 < no:
                        prepA2(oc[ci + 2])
                    if ci + 1 < no:
                        prepB(oc[ci + 1])
                    finish_dve(oc[ci])
                SX.free_all()
            if want("xbc%d" % l) or want("hin%d" % l) or want("tok%d" % l):
                break
            S.free_all()
            if want("ybuf%d" % l):
                dump(ybuf.rearrange("p a b -> p (a b)"), NT * D, [(k_ybuf, c, g) for c in range(NT) for g in range(4)], R23)
                break

            ysT, k_ysT = R2.alloc("ysT", [128, 8, NTOK], BF16)
            wz, k_wz = R3.alloc("wz", [128, 8, D], BF16)
            ssdnw, k_ssdnw = R3.alloc("ssdnw", [128, 8], F32)
            sz, k_sz = R3.alloc("sz", [128, 2, D], F32)
            yz, k_yz = R3.alloc("yz", [128, 2, D], F32)
            nbufs = norm_bufs(R3)
            P.dma("sp", ssdnw, ssdnwT_in[l], (), [k_ssdnw])
            for hf in range(2):
                WLOAD(wz[:, :, hf * 512:(hf + 1) * 512], wsrc(w_in, l, OFF_Z + hf * 512, 512), (), [(k_wz, hf)])
            def ro_front(i):
                t = tiles_q[i]
                for hf in range(2):
                    b = hf + 2 * (i % 2)
                    for kc in range(8):
                        MM(PS(b), hT[:, kc, t * 128:(t + 1) * 128], wz[:, kc, hf * 512:(hf + 1) * 512], kc == 0, kc == 7,
                           [(k_hT, t), (k_wz, hf)], [PSK(b)])
                    ACT(sz[:, i % 2, hf * 512:(hf + 1) * 512], PS(b), AF.Silu, [PSK(b)], [(k_sz, i % 2, hf)])
                TT("dve", yz[:, i % 2, :], ybuf[:, t, :], sz[:, i % 2, :], ALU.mult,
                   [(k_ybuf, t, g) for g in range(4)] + [(k_sz, i % 2, 0), (k_sz, i % 2, 1)], [(k_yz, i % 2)])
                norm_front(yz[:, i % 2, :], (k_yz, i % 2), D, nbufs, i)

            ro_front(0)
            for i, t in enumerate(tiles_q):
                if i + 1 < len(tiles_q):
                    ro_front(i + 1)
                norm_back(t, ssdnw, None, [k_ssdnw], ysT, k_ysT, nbufs, i)
            R3.free_all()
            R1.free_all()
            ysT_all = [(k_ysT, t) for t in tiles_q]
            if want("ysT%d" % l):
                dump(ysT.rearrange("p a b -> p (a b)"), 8 * NTOK, ysT_all, R3)
                break

            yaT, k_yaT = R1.alloc("yaT", [128, 8, NTOK], BF16)
            T_ = R3
            cosT, k_cos = T_.alloc("cosT", [128, SEQ], F32)
            sinT, k_sin = T_.alloc("sinT", [128, SEQ], F32)
            P.dma("sp", cosT, cos_in, (), [k_cos])
            P.dma("sp", sinT, sin_in, (), [k_sin])
            subln, k_subln = T_.alloc("subln", [128, 1], F32)
            lamv, k_lamv = T_.alloc("lamv", [128, 4, 64], F32)
            lprod, k_lprod = T_.alloc("lprod", [128, 2, 64], F32)
            lsm, k_lsm = T_.alloc("lsm", [128, 4], F32)
            P.dma("sp", subln, sublnT_in[l], (), [k_subln])
            P.dma("sp", lamv, lam_in[l].partition_broadcast(128).rearrange("p (a b) -> p a b", a=4), (), [k_lamv])
            TS("dve", subln, subln, 1.0 - lam_init, None, ALU.mult, None, [k_subln], [k_subln])
            TT("dve", lprod[:, 0, :], lamv[:, 0, :], lamv[:, 1, :], ALU.mult, [k_lamv], [(k_lprod, 0)])
            TT("dve", lprod[:, 1, :], lamv[:, 2, :], lamv[:, 3, :], ALU.mult, [k_lamv], [(k_lprod, 1)])
            P.op("dve", lambda e: e.tensor_reduce(out=lsm[:, 0:2], in_=lprod, axis=AX.X, op=ALU.add), [(k_lprod, 0), (k_lprod, 1)], [(k_lsm, 0)])
            ACT(lsm[:, 2:4], lsm[:, 0:2], AF.Exp, [(k_lsm, 0)], [(k_lsm, 1)])
            TT("dve", lsm[:, 0:1], lsm[:, 3:4], lsm[:, 2:3], ALU.subtract, [(k_lsm, 1)], [(k_lsm, 2)])
            TS("dve", lsm[:, 1:2], lsm[:, 0:1], -lam_init, None, ALU.add, None, [(k_lsm, 2)], [(k_lsm, 3)])
            neglam = lsm[:, 1:2]
            k_neglam = (k_lsm, 3)
            wsl = [T_.alloc("watt%d" % i, [128, 3, 8, 128], BF16) for i in range(2)]
            qraws = [T_.alloc("qraw%d" % i, [128, 512], BF16) for i in range(2)]
            qT, k_qT = T_.alloc("qT", [128, 2, NTOK], BF16)
            MSET("pool", qT[64:128, 0, :], 0.0, [(k_qT, "z0")])
            MSET("pool", qT[0:64, 1, :], 0.0, [(k_qT, "z1")])
            kT, k_kT = T_.alloc("kT", [128, NTOK], BF16)
            v_aug, k_v = T_.alloc("v_aug", [128, NT, 132], BF16)
            ropa = [T_.alloc("ropa%d" % i, [128, 512], F32) for i in range(2)]
            ropb = [T_.alloc("ropb%d" % i, [128, 512], F32) for i in range(2)]
            pTs = [T_.alloc("pT%d" % i, [128, 512], BF16) for i in range(4)]
            o4, k_o4 = T_.alloc("o4", [128, 4, 128], F32)
            on4, k_on4 = T_.alloc("on4", [128, 4, 128], BF16)
            rs4, k_rs4 = T_.alloc("rs4", [128, 4, 8], F32)
            att = {"sb": -1, "sbs": {}}
            MSET("pool", v_aug[:, :, 128:129], 1.0, [(k_v, "ones")])

            def load_watt(h):
                wt, kwt = wsl[h % 2]
                WLOAD(wt[:, 0], wsrc(w_in, l, OFF_Q + h * 128, 128), (), [(kwt, 0)])
                WLOAD(wt[:, 1], wsrc(w_in, l, OFF_K + h * 128, 128), (), [(kwt, 1)])
                WLOAD(wt[:, 2], wsrc(w_in, l, OFF_V + h * 128, 128), (), [(kwt, 2)])

            load_watt(0)
            load_watt(1)
            cnt = {"rope": 0, "post": 0}
            pend = []

            def flush_pend():
                if pend:
                    if pend[1] == 2:
                        post2(pend[0])
                    post3(pend[0], pend[2])
                    pend.clear()

            for h in range(8):
                wt, kwt = wsl[h % 2]
                for (wi, dst, kdst, isq) in ((0, None, None, True), (1, kT, k_kT, False)):
                    for bi, (t0, n) in enumerate(BLOCKS):
                        hk = [(k_hT, t) for t in range(t0 // 128, (t0 + n) // 128)]
                        if bi == 0:
                            if isq and not with_ctx:
                                continue
                            for kc in range(8):
                                MM(PS(0, n), wt[:, wi, kc, :], hT[:, kc, t0:t0 + n], kc == 0, kc == 7, [(kwt, wi)] + hk, [PSK(0)])
                            if isq:
                                ACT(qT[0:64, 0, t0:t0 + n], PS(0, n)[0:64, :], AF.Copy, [PSK(0), (k_qT, "z0")], [(k_qT, 0, bi)])
                                ACT(qT[64:128, 1, t0:t0 + n], PS(0, n)[64:128, :], AF.Copy, [PSK(0), (k_qT, "z1")], [(k_qT, 1, bi)])
                            else:
                                ACT(dst[:, t0:t0 + n], PS(0, n), AF.Copy, [PSK(0)], [(kdst, bi)])
                            continue
                        r_i = cnt["rope"] % 2
                        cnt["rope"] += 1
                        bA, bB = 2 * r_i, 2 * r_i + 1
                        for kc in range(8):
                            MM(PS(bA), wt[:, wi, kc, :], hT[:, kc, t0:t0 + n], kc == 0, kc == 7, [(kwt, wi)] + hk, [PSK(bA)])
                        qr, k_qr = qraws[r_i]
                        ACT(qr, PS(bA), AF.Copy, [PSK(bA)], [k_qr])
                        MM(PS(bB), permT, qr, True, True, [(k_perm, 0), (k_perm, 1), k_qr], [PSK(bB)])
                        ra, k_ra = ropa[r_i]
                        rb_, k_rb_ = ropb[r_i]
                        c0 = t0 - CTX
                        TT("dve", ra, PS(bA), cosT[:, c0:c0 + n], ALU.mult, [PSK(bA), k_cos], [k_ra])
                        TT("dve", rb_, PS(bB), sinT[:, c0:c0 + n], ALU.mult, [PSK(bB), k_sin], [k_rb_])
                        if isq:
                            TT("pool", qT[0:64, 0, t0:t0 + n], ra[0:64, :], rb_[0:64, :], ALU.add, [k_ra, k_rb_, (k_qT, "z0")], [(k_qT, 0, bi)])
                            TT("pool", qT[64:128, 1, t0:t0 + n], ra[64:128, :], rb_[64:128, :], ALU.add, [k_ra, k_rb_, (k_qT, "z1")], [(k_qT, 1, bi)])
                        else:
                            TT("pool", dst[:, t0:t0 + n], ra, rb_, ALU.add, [k_ra, k_rb_], [(kdst, bi)])
                for t4 in range(0, NT, 4):
                    nt4 = min(4, NT - t4)
                    b = 4 + ((t4 // 4) % 2)
                    for ti in range(nt4):
                        t = t4 + ti
                        for kc in range(8):
                            MM(PS(b, 128, off=ti * 128), hT[:, kc, t * 128:(t + 1) * 128], wt[:, 2, kc, :], kc == 0, kc == 7,
                               [(k_hT, t), (kwt, 2)], [PSK(b)])
                    CP("dve", v_aug[:, t4:t4 + nt4, 0:128], PS(b, nt4 * 128).rearrange("p (t e) -> p t e", e=128), [PSK(b)],
                       [(k_v, t) for t in range(t4, t4 + nt4)])
                flush_pend()
                if h + 2 < 8:
                    load_watt(h + 2)
                qblocks = ([(0, 256, [0, 1], [0])] if with_ctx else []) + [(256 + 512 * j, 512, list(range(NT)), [1 + j]) for j in range(4)]
                LOOK = 3

                def acc_of(nq, comp, j, n=129):
                    idx = comp * nq + j
                    return PS(4 + idx // 3, n, off=(idx % 3) * 132), PSK(4 + idx // 3), idx

                def emit_S(qb, i):
                    q0, qn, ktiles, qbi = qb
                    ki, comp = i // 2, i % 2
                    kt = ktiles[ki]
                    sb = att["sb"] = (att["sb"] + 1) % 4
                    att["sbs"][(q0, i)] = sb
                    MM(PS(sb, qn), kT[:, kt * 128:(kt + 1) * 128], qT[:, comp, q0:q0 + qn], True, True,
                       [(k_kT, 0 if kt < 2 else 1 + (kt - 2) // 4), (k_qT, "z%d" % (1 - comp))] + [(k_qT, comp, b_) for b_ in qbi], [PSK(sb)])

                def emit_EP(qb, i):
                    q0, qn, ktiles, qbi = qb
                    nq = qn // 128
                    ki, comp = i // 2, i % 2
                    kt = ktiles[ki]
                    sb = att["sbs"].pop((q0, i))
                    pT_, k_pT = pTs[sb]
                    ACT(pT_[:, 0:qn], PS(sb, qn), AF.Exp, [PSK(sb)], [k_pT], scale=ATT_SCALE)
                    for j in range(nq):
                        a_, ka_, idx = acc_of(nq, comp, j)
                        MM(a_, pT_[:, j * 128:(j + 1) * 128], v_aug[:, kt, 0:129], ki == 0 and idx % 3 == 0, ki == len(ktiles) - 1,
                           [k_pT, (k_v, kt), (k_v, "ones")], [ka_], skip=True)

                def post1(qb):
                    q0, qn, ktiles, qbi = qb
                    nq = qn // 128
                    A0 = [acc_of(nq, 0, j) for j in range(nq)]
                    A1 = [acc_of(nq, 1, j) for j in range(nq)]
                    J = range(nq)
                    for j in J:
                        RECIP(rs4[:, j, 0:1], A0[j][0][:, 128:129], [A0[j][1]], [(k_rs4, j, 0)])
                        RECIP(rs4[:, j, 1:2], A1[j][0][:, 128:129], [A1[j][1]], [(k_rs4, j, 1)])
                    for j in J:
                        TT("dve", rs4[:, j, 2:3], rs4[:, j, 1:2], neglam, ALU.mult, [(k_rs4, j, 1), k_neglam], [(k_rs4, j, 2)])
                    for j in J:
                        TS("dve", o4[:, j, :], A0[j][0][:, 0:128], rs4[:, j, 0:1], None, ALU.mult, None, [A0[j][1], (k_rs4, j, 0)], [(k_o4, j)])
                    for j in J:
                        STT("dve", o4[:, j, :], A1[j][0][:, 0:128], rs4[:, j, 2:3], o4[:, j, :], ALU.mult, ALU.add,
                            [A1[j][1], (k_rs4, j, 2), (k_o4, j)], [(k_o4, j)])

                def post2(qb):
                    J = range(qb[1] // 128)
                    for j in J:
                        ACT(on4[:, j, :], o4[:, j, :], AF.Square, [(k_o4, j)], [(k_on4, j), (k_rs4, j, 3)], accum_out=rs4[:, j, 3:4])
                    for j in J:
                        ACT(rs4[:, j, 4:5], rs4[:, j, 3:4], AF.Ln, [(k_rs4, j, 3)], [(k_rs4, j, 4)], scale=1.0 / 128, bias=EPS)
                    for j in J:
                        ACT(rs4[:, j, 5:6], rs4[:, j, 4:5], AF.Exp, [(k_rs4, j, 4)], [(k_rs4, j, 5)], scale=-0.5)
                    for j in J:
                        TS("dve", on4[:, j, :], o4[:, j, :], rs4[:, j, 5:6], None, ALU.mult, None, [(k_o4, j), (k_rs4, j, 5)], [(k_on4, j)])

                def post3(qb, h):
                    q0 = qb[0]
                    J = range(qb[1] // 128)
                    pt = PS(7).bitcast(BF16)
                    for j in J:
                        TR(pt[:, j * 128:(j + 1) * 128], on4[:, j, :], [(k_on4, j)], [PSK(7)])
                    for j in J:
                        tq = q0 // 128 + j
                        ACT(yaT[:, h, tq * 128:(tq + 1) * 128], pt[:, j * 128:(j + 1) * 128], AF.Identity, [PSK(7), k_subln],
                            [(k_yaT, tq, h)], scale=subln[:, 0:1])

                for bi_, qb in enumerate(qblocks):
                    nst = 2 * len(qb[2])
                    if bi_ == 0:
                        for i in range(min(LOOK, nst)):
                            emit_S(qb, i)
                    for i in range(nst):
                        if i + LOOK < nst:
                            emit_S(qb, i + LOOK)
                        emit_EP(qb, i)
                        if pend and pend[1] == 2 and i >= 3:
                            post2(pend[0])
                            pend[1] = 3
                        if pend and pend[1] == 3 and i >= 8:
                            post3(pend[0], pend[2])
                            pend.clear()
                    flush_pend()
                    if bi_ + 1 < len(qblocks):
                        nb = qblocks[bi_ + 1]
                        for i in range(min(LOOK, 2 * len(nb[2]))):
                            emit_S(nb, i)
                    post1(qb)
                    pend.extend([qb, 2, h])
            flush_pend()
            T_.free_all()
            yaT_all = [(k_yaT, t, h) for t in tiles_q for h in range(8)]
            if want("yaT%d" % l):
                dump(yaT.rearrange("p a b -> p (a b)"), 8 * NTOK, yaT_all)
                break

            mT, k_mT = R3.alloc("mT", [128, 8, NTOK], BF16)
            wms = [R3.alloc("wmrg%d" % i, [128, 4, 8, 128], BF16) for i in range(2)]
            sgs = [R3.alloc("sg%d" % i, [128, 2, 512], F32) for i in range(2)]
            tms = [R3.alloc("tm%d" % i, [128, 2, 512], F32) for i in range(2)]

            def load_wm(j):
                wt, kwt = wms[j % 2]
                WLOAD(wt[:, 0], wsrc(w_brs, l, j * 128, 128), (), [(kwt, 0)])
                WLOAD(wt[:, 1], wsrc(w_bra, l, j * 128, 128), (), [(kwt, 1)])
                WLOAD(wt[:, 2], wsrc(w_in, l, OFF_G + j * 128, 128), (), [(kwt, 2)])
                WLOAD(wt[:, 3], wsrc(w_in, l, OFF_G + D + j * 128, 128), (), [(kwt, 3)])

            load_wm(0)
            load_wm(1)
            it = 0
            for j in range(8):
                wt, kwt = wms[j % 2]
                for (t0, n) in blocks_q:
                    tl = list(range(t0 // 128, (t0 + n) // 128))
                    b0 = 4 * (it % 2)
                    sg, k_sg = sgs[it % 2]
                    tm, k_tm = tms[it % 2]
                    it += 1
                    srcs = ((ysT, [(k_ysT, t) for t in tl]), (yaT, [(k_yaT, t, h_) for t in tl for h_ in range(8)]),
                            (hT, [(k_hT, t) for t in tl]), (hT, [(k_hT, t) for t in tl]))
                    for wi in range(4):
                        src, sk = srcs[wi]
                        for kc in range(8):
                            MM(PS(b0 + wi, n), wt[:, wi, kc, :], src[:, kc, t0:t0 + n], kc == 0, kc == 7, [(kwt, wi)] + sk, [PSK(b0 + wi)])
                    for wi in range(2):
                        ACT(sg[:, wi, 0:n], PS(b0 + 2 + wi, n), AF.Sigmoid, [PSK(b0 + 2 + wi)], [(k_sg, wi)])
                    for wi in range(2):
                        TT("dve", tm[:, wi, 0:n], PS(b0 + wi, n), sg[:, wi, 0:n], ALU.mult, [PSK(b0 + wi), (k_sg, wi)], [(k_tm, wi)])
                    TT("pool", mT[:, j, t0:t0 + n], tm[:, 0, 0:n], tm[:, 1, 0:n], ALU.add, [(k_tm, 0), (k_tm, 1)], [(k_mT, t, j) for t in tl])
                if j + 2 < 8:
                    load_wm(j + 2)
            for k in R3.keys[1:]:
                A.free(k)
            del R3.keys[1:]
            R0.free_all()
            R1.free_all()
            R2.free_all()

            h2T, k_h2T = R0.alloc("h2T", [128, 8, NTOK], BF16)
            wo, k_wo = R1.alloc("wo", [128, 8, D], BF16)
            for hf in range(2):
                WLOAD(wo[:, :, hf * 512:(hf + 1) * 512], wsrc(w_out, l, hf * 512, 512), (), [(k_wo, hf)])
            xts = [R2.alloc("xt%d" % i, [128, D], F32) for i in range(2)]
            xns = [R2.alloc("xnew%d" % i, [128, D], F32) for i in range(2)]
            nbufs = norm_bufs(R2)
            def op_front(i):
                t = tiles_q[i]
                w_ = 1 if t < 2 else 0
                xt, kx = xts[i % 2]
                xnw, kxn_ = xns[i % 2]
                P.dma("sp", xt, xsrc(t), [("xres", t)], [kx])
                for hf in range(2):
                    b = hf + 2 * (i % 2)
                    for kc in range(8):
                        MM(PS(b), mT[:, kc, t * 128:(t + 1) * 128], wo[:, kc, hf * 512:(hf + 1) * 512], kc == 0, kc == 7,
                           [(k_mT, t, kc), (k_wo, hf)], [PSK(b)])
                    TT("dve", xnw[:, hf * 512:(hf + 1) * 512], PS(b), g1bc[:, w_, hf * 512:(hf + 1) * 512], ALU.mult,
                       [PSK(b), (k_g1bc, w_, hf)], [(kxn_, hf)])
                kxh = [(kxn_, 0), (kxn_, 1)]
                TT("pool", xnw, xnw, xt, ALU.add, kxh + [kx], kxh)
                P.dma("sp", xres[t * 128:(t + 1) * 128, :], xnw, kxh, [("xres", t)])
                norm_front(xnw, kxh, D, nbufs, i)

            op_front(0)
            for i, t in enumerate(tiles_q):
                if i + 1 < len(tiles_q):
                    op_front(i + 1)
                w_ = 1 if t < 2 else 0
                norm_back(t, scl2[:, :, w_], modT[:, 24:32, w_], [k_scl2] + [(k_modT, j) for j in range(24, 32)], h2T, k_h2T, nbufs, i)
            R1.free_all()
            R2.free_all()
            R3.free_all()
            if want("xm%d" % l):
                P.dma("sp", dbg_out.rearrange("p (t f) -> p t f", t=NT), xres.rearrange("(t p) f -> p t f", p=128),
                      [("xres", t) for t in range(NT)], ["dbg"], is_output=True)
                break
            if want("h2T%d" % l):
                dump(h2T.rearrange("p a b -> p (a b)"), 8 * NTOK, [(k_h2T, t) for t in tiles_q])
                break

            GT0 = BASE + SZ
            RG = Region(A, GT0, GT0 + NFF * NTOK * 2)
            RF = Region(A, GT0 + NFF * NTOK * 2, CAP)
            gT, k_gT = RG.alloc("gT", [128, NFF, NTOK], BF16)
            ffncw, k_fcw = RF.alloc("ffncw", [128, 2 * NFF, 3], F32)
            ffncb, k_fcb = RF.alloc("ffncb", [128, 2 * NFF], F32)
            P.dma("sp", ffncw, ffncw_in[l].rearrange("p (c k) -> p c k", k=3), (), [k_fcw])
            P.dma("sp", ffncb, ffncb_in[l], (), [k_fcb])
            Upads = [RF.alloc("Upad%d" % i, [128, NTOK + 4], F32) for i in range(2)]
            acc_, k_acc = RF.alloc("acc", [128, NTOK], F32)
            sa, k_sa = RF.alloc("sa", [128, NTOK], BF16)
            NWU = 2
            wus = [RF.alloc("wup%d" % i, [128, 8, 256], BF16) for i in range(NWU)]
            for Upad, k_U in Upads:
                MSET("pool", Upad[:, 0:1], 0.0, [(k_U, "p0")])
                MSET("pool", Upad[:, 257:259], 0.0, [(k_U, "p1")])
                MSET("pool", Upad[:, 2307:2308], 0.0, [(k_U, "p2")])

            def load_wu(j):
                wt, kwt = wus[j % NWU]
                WLOAD(wt[:, :, 0:128], wsrc(w_up, l, j * 128, 128), (), [(kwt, 0)])
                WLOAD(wt[:, :, 128:256], wsrc(w_up, l, D_FF + j * 128, 128), (), [(kwt, 1)])

            for j in range(NWU):
                load_wu(j)
            ranges = ([(0, 256, 0)] if with_ctx else []) + [(256, 2048, 258)]
            bi_q = list(range(0 if with_ctx else 1, 5))
            it = 0
            for j in range(NFF):
                wt, kwt = wus[j % NWU]
                for part in range(2):
                    cidx = j + part * NFF
                    Upad, k_U = Upads[part]
                    for bi in bi_q:
                        t0, n = BLOCKS[bi]
                        b = it % 6
                        it += 1
                        for kc in range(8):
                            MM(PS(b, n), wt[:, kc, part * 128:(part + 1) * 128], h2T[:, kc, t0:t0 + n], kc == 0, kc == 7,
                               [(kwt, part)] + [(k_h2T, t) for t in range(t0 // 128, (t0 + n) // 128)], [PSK(b)])
                        off = t0 + 1 if t0 == 0 else t0 + 3
                        ACT(Upad[:, off:off + n], PS(b, n), AF.Copy, [PSK(b)], [(k_U, bi)])
                    ukeys = [(k_U, bi) for bi in bi_q] + [(k_U, "p0"), (k_U, "p1"), (k_U, "p2")]
                    for (o0, n, u0) in ranges:
                        ka = (k_acc, o0)
                        TS("dve", acc_[:, o0:o0 + n], Upad[:, u0:u0 + n], ffncw[:, cidx, 0:1], None, ALU.mult, None, ukeys + [k_fcw], [ka])
                        STT("dve", acc_[:, o0:o0 + n], Upad[:, u0 + 1:u0 + 1 + n], ffncw[:, cidx, 1:2], acc_[:, o0:o0 + n], ALU.mult, ALU.add,
                            ukeys + [k_fcw, ka], [ka])
                        STT("dve", acc_[:, o0:o0 + n], Upad[:, u0 + 2:u0 + 2 + n], ffncw[:, cidx, 2:3], acc_[:, o0:o0 + n], ALU.mult, ALU.add,
                            ukeys + [k_fcw, ka], [ka])
                        if part == 0:
                            ACT(sa[:, o0:o0 + n], acc_[:, o0:o0 + n], AF.Silu, [ka, k_fcb], [(k_sa, o0)], bias=ffncb[:, cidx:cidx + 1])
                        else:
                            STT("dve", gT[:, j, o0:o0 + n], acc_[:, o0:o0 + n], ffncb[:, cidx:cidx + 1], sa[:, o0:o0 + n], ALU.add, ALU.mult,
                                [ka, k_fcb, (k_sa, o0)], [(k_gT, j, o0)])
                if j + NWU < NFF:
                    load_wu(j + NWU)
            RF.free_all()
            R0.free_all()
            if want("gT%d" % l):
                dump(gT.rearrange("p a b -> p (a b)"), NFF * NTOK, [(k_gT, j, o0) for j in range(NFF) for (o0, _, _) in ranges])
                break

            last = l == DEPTH - 1
            wds = [R0.alloc("wd0", [128, NFF, 512], BF16), RF.alloc("wd1", [128, NFF, 512], BF16)]
            for hf in range(2):
                for j0 in range(0, NFF, 6):
                    j1 = min(NFF, j0 + 6)
                    WLOAD(wds[hf][0][:, j0:j1, :], w_down[l].rearrange("(j p) n -> p j n", p=128)[:, j0:j1, hf * 512:(hf + 1) * 512],
                          (), [(wds[hf][1], j0)])
            xts = [RF.alloc("xt%d" % i, [128, D], F32) for i in range(2)]
            xns = [RF.alloc("xnew%d" % i, [128, D], F32) for i in range(2)]
            if last:
                fnw, k_fnw = R0.alloc("fnw", [128, D], F32)
                fjunk, k_fjunk = R0.alloc("fjunk", [128, 2, D], BF16)
                fss, k_fss = R0.alloc("fss", [128, 2, 4], F32)
                P.dma("sp", fnw, fnw_in.partition_broadcast(128), (), [k_fnw])
            for i, t in enumerate(tiles_q):
                w_ = 1 if t < 2 else 0
                xt, kx = xts[i % 2]
                xnw, kxn_ = xns[i % 2]
                P.dma("sp", xt, xres[t * 128:(t + 1) * 128, :], [("xres", t)], [kx])
                gk = [(k_gT, j, 0 if t < 2 else 256) for j in range(NFF)]
                for hf in range(2):
                    b = hf + 2 * (i % 2)
                    wd, kwd = wds[hf]
                    for j in range(NFF):
                        MM(PS(b), gT[:, j, t * 128:(t + 1) * 128], wd[:, j, :], j == 0, j == NFF - 1,
                           [(k_gT, j, 0 if t < 2 else 256), (kwd, (j // 6) * 6)], [PSK(b)])
                    TT("dve", xnw[:, hf * 512:(hf + 1) * 512], PS(b), g2bc[:, w_, hf * 512:(hf + 1) * 512], ALU.mult,
                       [PSK(b), (k_g2bc, w_, hf)], [(kxn_, hf)])
                kxh = [(kxn_, 0), (kxn_, 1)]
                TT("pool", xnw, xnw, xt, ALU.add, kxh + [kx], kxh)
                if not last:
                    P.dma("sp", xres[t * 128:(t + 1) * 128, :], xnw, kxh, [("xres", t)])
                else:
                    pi = i % 2
                    ACT(fjunk[:, pi, :], xnw, AF.Square, kxh, [(k_fjunk, pi), (k_fss, pi, 0)], accum_out=fss[:, pi, 0:1])
                    ACT(fss[:, pi, 1:2], fss[:, pi, 0:1], AF.Ln, [(k_fss, pi, 0)], [(k_fss, pi, 1)], scale=1.0 / D, bias=EPS)
                    ACT(fss[:, pi, 2:3], fss[:, pi, 1:2], AF.Exp, [(k_fss, pi, 1)], [(k_fss, pi, 2)], scale=-0.5)
                    STT("dve", xt, xnw, fss[:, pi, 2:3], fnw, ALU.mult, ALU.mult, kxh + [(k_fss, pi, 2), k_fnw, kx], [kx])
                    P.dma("sp", y_out[(t - 2) * 128:(t - 1) * 128, :], xt, [kx], [("y", t)], is_output=True)
            RF.free_all()
            R0.free_all()
            RG.free_all()
            if want("xf%d" % l):
                P.dma("sp", dbg_out.rearrange("p (t f) -> p t f", t=NT), xres.rearrange("(t p) f -> p t f", p=128),
                      [("xres", t) for t in range(NT)], ["dbg"], is_output=True)
                break

        P.finish()
        P.emit()
        print("ops", P.nops, "arena peak", A.peak)
    return nc


def rope_tables():
    inv_freq = (np.float32(10000.0) ** (-np.arange(16, dtype=np.float32) / np.float32(16))).astype(np.float32)
    t = np.arange(SEQ)
    row = (t // 64).astype(np.float32)
    col = (t % 64).astype(np.float32)
    cosT = np.zeros((128, SEQ), np.float32)
    sinT = np.zeros((128, SEQ), np.float32)
    for p in range(128):
        d = p % 64
        axis, half, f = d // 32, (d % 32) // 16, d % 16
        ang = ((row if axis == 0 else col) * inv_freq[f]).astype(np.float32)
        cosT[p] = np.cos(ang)
        sinT[p] = np.sin(ang) * (-1.0 if half == 0 else 1.0)
    return cosT, sinT


def make_in_maps(inp):
    f = lambda a: np.ascontiguousarray(np.asarray(a, dtype=np.float32))
    cosT, sinT = rope_tables()
    shared = {
        "w_mod": f(inp["w_mod"]), "b_mod": f(inp["b_mod"]),
        "bmodT": f(np.asarray(inp["b_mod"]).reshape(DEPTH, 48, 128).transpose(0, 2, 1)),
        "n1wT": f(np.asarray(inp["norm1_w"]).reshape(DEPTH, 8, 128).transpose(0, 2, 1)),
        "n2wT": f(np.asarray(inp["norm2_w"]).reshape(DEPTH, 8, 128).transpose(0, 2, 1)),
        "ssdnwT": f(np.asarray(inp["ssd_norm_w"]).reshape(DEPTH, 8, 128).transpose(0, 2, 1)),
        "sublnT": f(np.asarray(inp["att_subln_w"]).reshape(DEPTH, 128, 1)),
        "w_in": f(inp["w_in"]),
        "ssdcw": f(np.asarray(inp["ssd_conv_w"]).transpose(0, 2, 1).reshape(DEPTH, 16, 128, 3).transpose(0, 2, 1, 3).reshape(DEPTH, 128, 48)),
        "ssdcb": f(np.asarray(inp["ssd_conv_b"]).reshape(DEPTH, 16, 128).transpose(0, 2, 1)),
        "alog": f(np.asarray(inp["ssd_a_log"]).reshape(DEPTH, 32)),
        "dtb": f(np.asarray(inp["ssd_dt_bias"]).reshape(DEPTH, 32)),
        "dskip": f(inp["ssd_d"]),
        "lamv": f(np.asarray(inp["diff_lambda"]).reshape(DEPTH, 256)),
        "w_br_ssd": f(inp["w_br_ssd"]), "w_br_att": f(inp["w_br_att"]), "w_out": f(inp["w_out"]),
        "w_up": f(inp["w_up"]),
        "ffncw": f(np.asarray(inp["ffn_conv_w"]).transpose(0, 2, 1).reshape(DEPTH, 2 * NFF, 128, 3).transpose(0, 2, 1, 3).reshape(DEPTH, 128, 2 * NFF * 3)),
        "ffncb": f(np.asarray(inp["ffn_conv_b"]).reshape(DEPTH, 2 * NFF, 128).transpose(0, 2, 1)),
        "w_down": f(inp["w_down"]),
        "fnw": f(inp["final_norm_w"]),
        "ropecos": cosT, "ropesin": sinT,
    }
    x = np.asarray(inp["x"], dtype=np.float32)
    ctx = np.asarray(inp["ctx"], dtype=np.float32)
    c = np.asarray(inp["c"], dtype=np.float32)
    c_ctx = np.asarray(inp["c_ctx"], dtype=np.float32)
    maps = []
    for b in range(8):
        cT = np.stack([c[b].reshape(8, 128).T, c_ctx.reshape(8, 128).T], axis=-1).reshape(128, 16)
        m = dict(shared)
        m["x_b"] = f(x[b])
        m["ctx_b"] = f(ctx[b])
        m["cT"] = f(cT)
        maps.append(m)
    return maps


def kernel(**inputs):
    nc = build_program()
    res = run_bass_kernel_spmd(nc, make_in_maps(inputs), core_ids=list(range(8)))
    return np.stack([np.asarray(r["y"], dtype=np.float32) for r in res.results], axis=0)
```

```python
import math
import os
XF = os.environ.get('KX', '')
from contextlib import ExitStack

import numpy as np
import concourse.bass as bass
import concourse.mybir as mybir
from concourse.bass_utils import run_bass_kernel_spmd

F32 = mybir.dt.float32
BF16 = mybir.dt.bfloat16
AF = mybir.ActivationFunctionType
ALU = mybir.AluOpType
AX = mybir.AxisListType

D = 1024
SEQ = 2048
CTX = 256
NTOK = SEQ + CTX
NT = NTOK // 128
DEPTH = 2
EPS = 1e-6
NH_SSD = 16
D_FF = 2816
NFF = D_FF // 128
IN_W = 8224
OFF_Z, OFF_XBC, OFF_DT, OFF_Q, OFF_K, OFF_V, OFF_G = 0, 1024, 3072, 3104, 4128, 5152, 6176
ATT_SCALE = 64 ** -0.5
BLOCKS = [(0, 256)] + [(256 + 512 * j, 512) for j in range(4)]

ENGS = ("pe", "act", "dve", "pool", "sp")


def _base(k):
    return k if isinstance(k, str) else k[0]


class Prog:
    NDMA = 16

    def __init__(self, nc, es):
        self.nc = nc
        self.lists = {e: [] for e in ENGS}
        self.cnt = {e: 0 for e in ENGS}
        self.sem = {e: es.enter_context(nc.semaphore("s_" + e)) for e in ENGS}
        self.waited = {e: {} for e in ENGS}
        self.last_w = {}
        self.readers = {}
        self.touch = {}
        self.inherit = {}
        self.seen = set()
        self.dsem = {}
        self.dcnt = {}
        self.drot = {q: 0 for q in ("sp", "act", "pool")}
        for q in ("sp", "act", "pool"):
            for i in range(self.NDMA):
                nm = "d_%s%d" % (q, i)
                self.dsem[nm] = es.enter_context(nc.semaphore(nm))
                self.dcnt[nm] = 0
        self.out_waits = {}
        self.nops = 0
        self.ps_last = {}

    def _semh(self, name):
        return self.sem[name] if name in self.sem else self.dsem[name]

    @staticmethod
    def _merge(d, e, s):
        if d.get(e, 0) < s:
            d[e] = s

    def _deps(self, reads, writes, eng=None):
        deps = {}
        for k in list(reads) + list(writes):
            if _base(k) == "ps":
                for e, s in self.ps_last.get(k, {}).items():
                    if e != eng:
                        self._merge(deps, e, s)
        for k in list(reads) + list(writes):
            if k not in self.seen:
                for e, s in self.inherit.get(_base(k), {}).items():
                    self._merge(deps, e, s)
        for k in reads:
            w = self.last_w.get(k)
            if w is not None:
                self._merge(deps, *w)
        for k in writes:
            w = self.last_w.get(k)
            if w is not None:
                self._merge(deps, *w)
            for e, s in self.readers.get(k, {}).items():
                self._merge(deps, e, s)
        return deps

    def _waits(self, eng, deps):
        out = []
        for e2, s2 in deps.items():
            if e2 == eng and eng == "pe":
                continue
            if self.waited[eng].get(e2, 0) >= s2:
                continue
            self.waited[eng][e2] = s2
            out.append((e2, s2))
        return out

    def _commit(self, tag, reads, writes):
        e, s = tag
        for k in reads:
            self.seen.add(k)
            self._merge(self.readers.setdefault(k, {}), e, s)
            self._merge(self.touch.setdefault(_base(k), {}), e, s)
        for k in writes:
            self.seen.add(k)
            self.last_w[k] = tag
            self.readers[k] = {}
            self._merge(self.touch.setdefault(_base(k), {}), e, s)
        for k in list(reads) + list(writes):
            if _base(k) == "ps":
                self.ps_last.setdefault(k, {})[e] = s

    def op(self, eng, emit, reads=(), writes=()):
        deps = self._deps(reads, writes, eng)
        waits = self._waits(eng, deps)
        self.cnt[eng] += 1
        self.lists[eng].append((waits, emit, (eng, 1)))
        self._commit((eng, self.cnt[eng]), reads, writes)
        self.nops += 1

    def dma(self, q, out, in_, reads=(), writes=(), is_output=False, **kw):
        deps = self._deps(reads, writes)
        nm = "d_%s%d" % (q, self.drot[q] % self.NDMA)
        self.drot[q] += 1
        if self.dcnt[nm] > 0:
            self._merge(deps, nm, self.dcnt[nm])
        waits = self._waits(q, deps)
        self.dcnt[nm] += 16
        seq = self.dcnt[nm]
        self.lists[q].append((waits, lambda e: e.dma_start(out=out, in_=in_, **kw), (nm, 16)))
        self._commit((nm, seq), reads, writes)
        if is_output:
            self._merge(self.out_waits, nm, seq)
        self.nops += 1

    def finish(self):
        self.lists["sp"].append((list(self.out_waits.items()), None, None))

    def emit(self):
        nc = self.nc
        with nc.Block() as block:
            def run(e, lst):
                for waits, emit, inc in lst:
                    for (nm, v) in waits:
                        e.wait_ge(self._semh(nm), v)
                    if emit is not None:
                        emit(e).then_inc(self._semh(inc[0]), inc[1])

            @block.tensor
            def _(e):
                run(e, self.lists["pe"])

            @block.scalar
            def _(e):
                run(e, self.lists["act"])

            @block.vector
            def _(e):
                run(e, self.lists["dve"])

            @block.gpsimd
            def _(e):
                run(e, self.lists["pool"])

            @block.sync
            def _(e):
                run(e, self.lists["sp"])


class Arena:
    def __init__(self, tensor, cap_bytes, prog):
        self.t = tensor
        self.cap = cap_bytes
        self.P = prog
        self.live = {}
        self.hist = []
        self.uid = 0
        self.peak = 0

    def alloc_at(self, name, shape, dt, off):
        esz = 4 if dt == F32 else 2
        n = int(np.prod(shape[1:]))
        size = (n * esz + 31) // 32 * 32
        assert off % 32 == 0
        assert off + size <= self.cap, "SBUF arena overflow at %s: %d > %d" % (name, off + size, self.cap)
        for k2, (o, s_) in self.live.items():
            assert not (o < off + size and off < o + s_), "overlap %s with live %s" % (name, k2)
        self.peak = max(self.peak, off + size)
        self.uid += 1
        key = "%s#%d" % (name, self.uid)
        inh = {}
        keep = []
        for (nm, o, s_) in self.hist:
            if o < off + size and off < o + s_:
                for e, q in self.P.touch.get(nm, {}).items():
                    Prog._merge(inh, e, q)
                for e, q in self.P.inherit.get(nm, {}).items():
                    Prog._merge(inh, e, q)
                if off <= o and o + s_ <= off + size:
                    continue
            keep.append((nm, o, s_))
        self.hist = keep
        self.P.inherit[key] = inh
        a = self.t[0:shape[0], off // 4:(off + size) // 4]
        if dt != F32:
            a = a.bitcast(dt)
        a = a[:, 0:n]
        if len(shape) == 3:
            a = a.rearrange("p (a b) -> p a b", a=shape[1])
        elif len(shape) == 4:
            a = a.rearrange("p (a b c) -> p a b c", a=shape[1], b=shape[2])
        self.live[key] = (off, size)
        return a, key, size

    def free(self, key):
        o, s_ = self.live.pop(key)
        self.hist.append((key, o, s_))


class Region:
    def __init__(self, arena, lo, hi):
        self.A, self.lo, self.hi = arena, lo, hi
        self.p = lo
        self.keys = []

    def alloc(self, name, shape, dt):
        a, key, size = self.A.alloc_at(name, shape, dt, self.p)
        self.p += size
        assert self.p <= self.hi, "region overflow at %s: %d > %d" % (name, self.p, self.hi)
        self.keys.append(key)
        return a, key

    def free_all(self):
        for k in self.keys:
            self.A.free(k)
        self.keys = []
        self.p = self.lo


def build_program(dbg=None):
    nc = bass.Bass("TRN2", target_bir_lowering=False)

    def din(name, shape):
        return nc.dram_tensor(name, list(shape), F32, kind="ExternalInput").ap()

    x_in = din("x_b", [SEQ, D])
    ctx_in = din("ctx_b", [CTX, D])
    cT_in = din("cT", [128, 16])
    w_mod = din("w_mod", [DEPTH, D, 6 * D])
    b_mod = din("b_mod", [DEPTH, 6 * D])
    bmodT_in = din("bmodT", [DEPTH, 128, 48])
    n1wT_in = din("n1wT", [DEPTH, 128, 8])
    n2wT_in = din("n2wT", [DEPTH, 128, 8])
    ssdnwT_in = din("ssdnwT", [DEPTH, 128, 8])
    sublnT_in = din("sublnT", [DEPTH, 128, 1])
    w_in = din("w_in", [DEPTH, D, IN_W])
    ssdcw_in = din("ssdcw", [DEPTH, 128, 16 * 3])
    ssdcb_in = din("ssdcb", [DEPTH, 128, 16])
    alog_in = din("alog", [DEPTH, 32])
    dtb_in = din("dtb", [DEPTH, 32])
    dskip_in = din("dskip", [DEPTH, 16])
    lam_in = din("lamv", [DEPTH, 256])
    w_brs = din("w_br_ssd", [DEPTH, D, D])
    w_bra = din("w_br_att", [DEPTH, D, D])
    w_out = din("w_out", [DEPTH, D, D])
    w_up = din("w_up", [DEPTH, D, 2 * D_FF])
    ffncw_in = din("ffncw", [DEPTH, 128, 2 * NFF * 3])
    ffncb_in = din("ffncb", [DEPTH, 128, 2 * NFF])
    w_down = din("w_down", [DEPTH, D_FF, D])
    fnw_in = din("fnw", [D])
    cos_in = din("ropecos", [128, SEQ])
    sin_in = din("ropesin", [128, SEQ])
    y_out = nc.dram_tensor("y", [SEQ, D], F32, kind="ExternalOutput").ap()
    xres = nc.dram_tensor("xres", [NTOK, D], F32).ap()
    dbg_out = None
    if dbg is not None:
        dbg_out = nc.dram_tensor("dbg", [128, dbg[1]], F32, kind="ExternalOutput").ap()

    es = ExitStack()
    with es:
        P = Prog(nc, es)
        CAP = 206 * 1024
        arena_t = es.enter_context(nc.sbuf_tensor("arena", [128, CAP // 4], F32))
        psum = es.enter_context(nc.psum_tensor("psum", [128, 4096], F32))
        A = Arena(arena_t, CAP, P)

        def PS(b, n=512, off=0):
            return psum[:, b * 512 + off: b * 512 + off + n]

        def PSK(b):
            return ("ps", b)

        def MM(out, lhsT, rhs, start, stop, r, w, skip=False):
            if skip:
                P.op("pe", lambda e: e.matmul(out, lhsT=lhsT, rhs=rhs, start=start, stop=stop, skip_group_check=True), r, w)
            else:
                P.op("pe", lambda e: e.matmul(out, lhsT=lhsT, rhs=rhs, start=start, stop=stop), r, w)

        def ACT(out, in_, func, r, w, **kw):
            P.op("act", lambda e: e.activation(out=out, in_=in_, func=func, **kw), r, w)

        def TT(eng, out, in0, in1, op, r, w):
            P.op(eng, lambda e: e.tensor_tensor(out=out, in0=in0, in1=in1, op=op), r, w)

        def TS(eng, out, in0, s1, s2, op0, op1, r, w):
            if op1 is None:
                P.op(eng, lambda e: e.tensor_scalar(out=out, in0=in0, scalar1=s1, scalar2=None, op0=op0), r, w)
            else:
                P.op(eng, lambda e: e.tensor_scalar(out=out, in0=in0, scalar1=s1, scalar2=s2, op0=op0, op1=op1), r, w)

        def STT(eng, out, in0, scalar, in1, op0, op1, r, w):
            P.op(eng, lambda e: e.scalar_tensor_tensor(out=out, in0=in0, scalar=scalar, in1=in1, op0=op0, op1=op1), r, w)

        def CP(eng, out, in_, r, w):
            P.op(eng, lambda e: e.tensor_copy(out=out, in_=in_), r, w)

        def MSET(eng, ap, val, w):
            P.op(eng, lambda e: e.memset(ap, val), (), w)

        def RECIP(out, in_, r, w):
            P.op("dve", lambda e: e.reciprocal(out=out, in_=in_), r, w)

        def TR(out, in_, r, w):
            P.op("pe", lambda e: e.transpose(out=out, in_=in_, identity=ident), list(r) + [k_ident], w)

        def WLOAD(dst, src, r, w):
            P.dma("pool", dst, src, r, w)

        def wsrc(wt, l, c0, n):
            return wt[l].rearrange("(kc p) n -> p kc n", p=128)[:, :, c0:c0 + n]

        BASE = 27 * 1024
        SZ = 36864
        R_P = Region(A, 0, BASE)
        R0 = Region(A, BASE, BASE + SZ)
        R1 = Region(A, BASE + SZ, BASE + 2 * SZ)
        R2 = Region(A, BASE + 2 * SZ, BASE + 3 * SZ)
        R3 = Region(A, BASE + 3 * SZ, CAP)
        R23 = Region(A, BASE + 2 * SZ, CAP)

        ident, k_ident = R_P.alloc("ident", [128, 128], BF16)
        identf, k_identf = R_P.alloc("identf", [128, 128], F32)
        tri, k_tri = R_P.alloc("tri", [128, 4, 128], F32)
        ones, k_ones = R_P.alloc("ones", [128, 128], F32)
        MSET("pool", identf, 0.0, [k_identf])
        P.op("pool", lambda e: e.affine_select(out=identf, in_=identf, pattern=[[-1, 128]], compare_op=ALU.not_equal,
                                               fill=1.0, base=0, channel_multiplier=1), [k_identf], [k_identf])
        CP("dve", ident, identf, [k_identf], [k_ident])
        MSET("pool", ones, 1.0, [k_ones])
        MSET("pool", tri, 1.0, [k_tri])
        P.op("pool", lambda e: e.affine_select(out=tri[:, 0, :], in_=tri[:, 0, :], pattern=[[1, 128]], compare_op=ALU.is_ge,
                                               fill=0.0, base=0, channel_multiplier=-1), [k_tri], [k_tri])
        P.op("pool", lambda e: e.affine_select(out=tri[:, 1, :], in_=tri[:, 1, :], pattern=[[-1, 128]], compare_op=ALU.is_ge,
                                               fill=0.0, base=0, channel_multiplier=1), [k_tri], [k_tri])
        P.op("pool", lambda e: e.affine_select(out=tri[:, 2, :], in_=tri[:, 2, :], pattern=[[-1, 128]], compare_op=ALU.is_gt,
                                               fill=0.0, base=0, channel_multiplier=1), [k_tri], [k_tri])
        P.op("pool", lambda e: e.affine_select(out=tri[:, 3, :], in_=tri[:, 3, :], pattern=[[1, 128]], compare_op=ALU.is_gt,
                                               fill=0.0, base=0, channel_multiplier=-1), [k_tri], [k_tri])

        permT, k_perm = R_P.alloc("permT", [128, 128], BF16)
        iv = ident.rearrange("p (a hf f) -> p a hf f", hf=2, f=16)
        pv = permT.rearrange("p (a hf f) -> p a hf f", hf=2, f=16)
        CP("pool", pv[:, :, 0, :], iv[:, :, 1, :], [k_ident], [(k_perm, 0)])
        CP("pool", pv[:, :, 1, :], iv[:, :, 0, :], [k_ident], [(k_perm, 1)])
        cT, k_cT = R_P.alloc("cT", [128, 8, 2], F32)
        scT, k_scT = R_P.alloc("scT", [128, 8, 2], BF16)
        screp, k_screp = R_P.alloc("screp", [128, 8, 2, 128], BF16)
        P.dma("sp", cT, cT_in.rearrange("p (k w) -> p k w", w=2), (), [k_cT])
        ACT(scT, cT, AF.Silu, [k_cT], [k_scT])
        CP("dve", screp, scT.unsqueeze(3).to_broadcast([128, 8, 2, 128]), [k_scT], [k_screp])
        p_mark = (R_P.p, len(R_P.keys))

        dbgbuf = es.enter_context(nc.sbuf_tensor("dbgbuf", [128, 2, 128], F32)) if dbg is not None else None

        def dump(ap, ncols, rkeys, reg=None):
            for i, c0 in enumerate(range(0, ncols, 128)):
                n = min(128, ncols - c0)
                CP("dve", dbgbuf[:, i % 2, 0:n], ap[:, c0:c0 + n], rkeys, [("dbgbuf", i % 2)])
                P.dma("sp", dbg_out[:, c0:c0 + n], dbgbuf[:, i % 2, 0:n], [("dbgbuf", i % 2)], [("dbg", i)], is_output=True)

        def want(stage):
            return dbg is not None and dbg[0] == stage

        def norm_front(xt, kx, n_feat, bufs, i):
            junk, kj, ss, kss, xn, kxn, ev, kev = bufs
            nkc = n_feat // 128
            kxl = list(kx) if isinstance(kx, list) else [kx]
            ACT(junk[:, i % 2, 0:n_feat], xt, AF.Square, kxl, [(kj, i % 2), (kss, i % 2, 0)], accum_out=ss[:, i % 2, 0:1])
            ACT(ss[:, i % 2, 1:2], ss[:, i % 2, 0:1], AF.Ln, [(kss, i % 2, 0)], [(kss, i % 2, 1)], scale=1.0 / n_feat, bias=EPS)
            ACT(ss[:, i % 2, 2:3], ss[:, i % 2, 1:2], AF.Exp, [(kss, i % 2, 1)], [(kss, i % 2, 2)], scale=-0.5)
            TS("dve", xn[:, i % 2, 0:n_feat], xt, ss[:, i % 2, 2:3], None, ALU.mult, None, kxl + [(kss, i % 2, 2)], [(kxn, i % 2)])
            b = 6 + (i % 2)
            pt = PS(b).bitcast(BF16)
            for kc in range(nkc):
                TR(pt[:, kc * 128:(kc + 1) * 128], xn[:, i % 2, kc * 128:(kc + 1) * 128], [(kxn, i % 2)], [PSK(b)])

        def norm_back(t, scale_ap, bias_ap, extra_r, dstT, kdst, bufs, i):
            junk, kj, ss, kss, xn, kxn, ev, kev = bufs
            b = 6 + (i % 2)
            pt = PS(b).bitcast(BF16).rearrange("p (k t) -> p k t", k=8)
            dst = dstT[:, :, t * 128:(t + 1) * 128]
            sc = scale_ap.unsqueeze(2).to_broadcast([128, 8, 128])
            if bias_ap is None:
                TT("dve", dst, pt, sc, ALU.mult, [PSK(b)] + list(extra_r), [(kdst, t)])
            else:
                TT("dve", ev[:, i % 2], pt, sc, ALU.mult, [PSK(b)] + list(extra_r), [(kev, i % 2)])
                TT("dve", dst, ev[:, i % 2], bias_ap.unsqueeze(2).to_broadcast([128, 8, 128]), ALU.add, [(kev, i % 2)] + list(extra_r), [(kdst, t)])

        def norm_bufs(reg):
            junk, kj = reg.alloc("junk", [128, 2, D], BF16)
            ss, kss = reg.alloc("ss", [128, 2, 4], F32)
            xn, kxn = reg.alloc("xn", [128, 2, D], BF16)
            ev, kev = reg.alloc("nev", [128, 2, 8, 128], F32)
            return (junk, kj, ss, kss, xn, kxn, ev, kev)

        for l in range(DEPTH):
            with_ctx = l < DEPTH - 1
            T0 = 0 if with_ctx else 2
            tiles_q = list(range(T0, NT))
            blocks_q = BLOCKS[(0 if with_ctx else 1):]
            lam_init = 0.8 - 0.6 * math.exp(-0.3 * l)

            def xsrc(t, l=l):
                if l == 0:
                    return ctx_in[t * 128:(t + 1) * 128, :] if t < 2 else x_in[(t - 2) * 128:(t - 1) * 128, :]
                return xres[t * 128:(t + 1) * 128, :]

            for k in R_P.keys[p_mark[1]:]:
                A.free(k)
            del R_P.keys[p_mark[1]:]
            R_P.p = p_mark[0]

            bmodT, k_bmodT = R_P.alloc("bmodT", [128, 48], F32)
            modT, k_modT = R_P.alloc("modT", [128, 48, 2], F32)
            scl1, k_scl1 = R_P.alloc("scl1", [128, 8, 2], F32)
            scl2, k_scl2 = R_P.alloc("scl2", [128, 8, 2], F32)
            n1w, k_n1w = R_P.alloc("n1w", [128, 8], F32)
            n2w, k_n2w = R_P.alloc("n2w", [128, 8], F32)
            g1bc, k_g1bc = R_P.alloc("g1bc", [128, 2, D], F32)
            g2bc, k_g2bc = R_P.alloc("g2bc", [128, 2, D], F32)
            P.dma("sp", bmodT, bmodT_in[l], (), [k_bmodT])
            P.dma("sp", n1w, n1wT_in[l], (), [k_n1w])
            P.dma("sp", n2w, n2wT_in[l], (), [k_n2w])
            bmrow, k_bmrow = R3.alloc("bmrow", [128, 2, D], F32)
            P.dma("sp", bmrow[:, 0, :], b_mod[l, 2 * D:3 * D].partition_broadcast(128), (), [(k_bmrow, 0)])
            P.dma("sp", bmrow[:, 1, :], b_mod[l, 5 * D:6 * D].partition_broadcast(128), (), [(k_bmrow, 1)])
            wm = [R3.alloc("wmod%d" % i, [128, 8, 512], BF16) for i in range(2)]
            hT, k_hT = R0.alloc("hT", [128, 8, NTOK], BF16)
            xts = [R3.alloc("xt%d" % i, [128, D], F32) for i in range(3)]
            nbufs = norm_bufs(R3)

            def mod_chunk(cb):
                wt, kw = wm[cb % 2]
                WLOAD(wt, wsrc(w_mod, l, cb * 512, 512), (), [kw])
                which = cb // 2
                if which in (2, 5):
                    gb, kg = (g1bc, k_g1bc) if which == 2 else (g2bc, k_g2bc)
                    hf = cb % 2
                    for w_ in range(2):
                        b = (cb * 2 + w_) % 4
                        for kc in range(8):
                            MM(PS(b), screp[:, kc, w_, :], wt[:, kc, :], kc == 0, kc == 7, [k_screp, kw], [PSK(b)])
                        TT("dve", gb[:, w_, hf * 512:(hf + 1) * 512], PS(b), bmrow[:, 0 if which == 2 else 1, hf * 512:(hf + 1) * 512],
                           ALU.add, [PSK(b), (k_bmrow, 0 if which == 2 else 1)], [(kg, w_, hf)])
                else:
                    for j4 in range(4):
                        j = cb * 4 + j4
                        b = 4 + (j % 2)
                        for kc in range(8):
                            MM(PS(b, 2), wt[:, kc, j4 * 128:(j4 + 1) * 128], scT[:, kc, :], kc == 0, kc == 7, [k_scT, kw], [PSK(b)])
                        TS("dve", modT[:, j, :], PS(b, 2), bmodT[:, j:j + 1], None, ALU.add, None, [PSK(b), k_bmodT], [(k_modT, j)])

            for cb in range(4):
                mod_chunk(cb)
            STT("dve", scl1, modT[:, 8:16, :], 1.0, n1w.unsqueeze(2).to_broadcast([128, 8, 2]), ALU.add, ALU.mult,
                [(k_modT, j) for j in range(8, 16)] + [k_n1w], [k_scl1])

            def n1_front(t):
                xt, kx = xts[t % 3]
                P.dma("sp", xt, xsrc(t), [("xres", t)], [kx])
                norm_front(xt, kx, D, nbufs, t)

            rest = list(range(4, 12))
            n1_front(0)
            for t in range(NT):
                if t + 1 < NT:
                    n1_front(t + 1)
                w_ = 1 if t < 2 else 0
                norm_back(t, scl1[:, :, w_], modT[:, 0:8, w_], [k_scl1] + [(k_modT, j) for j in range(8)], hT, k_hT, nbufs, t)
                if t % 2 == 1 and rest:
                    mod_chunk(rest.pop(0))
            while rest:
                mod_chunk(rest.pop(0))
            STT("dve", scl2, modT[:, 32:40, :], 1.0, n2w.unsqueeze(2).to_broadcast([128, 8, 2]), ALU.add, ALU.mult,
                [(k_modT, j) for j in range(32, 40)] + [k_n2w], [k_scl2])
            k_g1 = [(k_g1bc, w_, hf) for w_ in range(2) for hf in range(2)]
            k_g2 = [(k_g2bc, w_, hf) for w_ in range(2) for hf in range(2)]
            R3.free_all()
            hT_all = [(k_hT, t) for t in range(NT)]
            if want("hT%d" % l):
                dump(hT.rearrange("p a b -> p (a b)"), 8 * NTOK, hT_all, R3)
                break

            ybuf, k_ybuf = R1.alloc("ybuf", [128, NT, D], BF16)
            S = R23
            ssdcw, k_ssdcw = S.alloc("ssdcw", [128, 16, 3], F32)
            ssdcb, k_ssdcb = S.alloc("ssdcb", [128, 16], F32)
            alog, k_alog = S.alloc("alog", [128, 32], F32)
            dtb, k_dtb = S.alloc("dtb", [128, 32], F32)
            dsk, k_dsk = S.alloc("dsk", [128, 16], F32)
            Aneg, k_Aneg = S.alloc("Aneg", [128, 32], F32)
            P.dma("sp", ssdcw, ssdcw_in[l].rearrange("p (c k) -> p c k", k=3), (), [k_ssdcw])
            P.dma("sp", ssdcb, ssdcb_in[l], (), [k_ssdcb])
            P.dma("sp", alog, alog_in[l].partition_broadcast(128), (), [k_alog])
            P.dma("sp", dtb, dtb_in[l].partition_broadcast(128), (), [k_dtb])
            P.dma("sp", dsk, dskip_in[l].partition_broadcast(128), (), [k_dsk])
            ACT(Aneg, alog, AF.Exp, [k_alog], [k_Aneg])
            TS("dve", Aneg, Aneg, -1.0, None, ALU.mult, None, [k_Aneg], [k_Aneg])
            wdt, k_wdt = S.alloc("wdt", [128, 8, 32], BF16)
            WLOAD(wdt, wsrc(w_in, l, OFF_DT, 32), (), [k_wdt])
            dt_all, k_dt = S.alloc("dt_all", [128, NT, 32], F32)
            a_all, k_a = S.alloc("a_all", [128, NT, 32], F32)
            eacs, k_eacs = S.alloc("eacs", [128, NT, 32], F32)
            edte, k_edte = S.alloc("edte", [128, NT, 32], F32)
            etot, k_etot = S.alloc("etot", [128, NT, 32], F32)
            dtdte, k_dtdte = S.alloc("dtdte", [128, NT, 32], F32)
            RT = Region(A, CAP - 2 * 2304, CAP)
            tmpa, k_tmpa = RT.alloc("tmpa", [128, NT, 32], F32)
            tmpb, k_tmpb = RT.alloc("tmpb", [128, NT, 32], F32)

            def pview(b0):
                return psum[:, b0 * 512: b0 * 512 + NT * 32].rearrange("p (t c) -> p t c", c=32)

            def pkeys(b0):
                return [PSK(b0), PSK(b0 + 1)]

            for t in range(NT):
                for kc in range(8):
                    MM(psum[:, t * 32:(t + 1) * 32], hT[:, kc, t * 128:(t + 1) * 128], wdt[:, kc, :], kc == 0, kc == 7,
                       [(k_hT, t), k_wdt], [PSK(0 if t < 16 else 1)])
            TT("dve", tmpa, pview(0), dtb.unsqueeze(1).to_broadcast([128, NT, 32]), ALU.add, pkeys(0) + [k_dtb], [k_tmpa])
            ACT(tmpb, tmpa, AF.Exp, [k_tmpa], [k_tmpb])
            ACT(dt_all, tmpb, AF.Ln, [k_tmpb], [k_dt], bias=1.0)
            TT("dve", a_all, dt_all, Aneg.unsqueeze(1).to_broadcast([128, NT, 32]), ALU.mult, [k_dt, k_Aneg], [k_a])
            RT.free_all()
            for t in range(NT):
                bk = lambda b0: [PSK(b0 + (0 if t < 16 else 1))]
                for d_ in range(2):
                    sl = slice(t * 32 + d_ * 16, t * 32 + d_ * 16 + 16)
                    MM(psum[:, 1024 + sl.start:1024 + sl.stop], tri[:, d_, :], a_all[:, t, d_ * 16:(d_ + 1) * 16], True, True,
                       [k_tri, k_a], bk(2))
                    MM(psum[:, 2048 + sl.start:2048 + sl.stop], tri[:, 2 + d_, :], a_all[:, t, d_ * 16:(d_ + 1) * 16], True, True,
                       [k_tri, k_a], bk(4))
                MM(psum[:, t * 32:(t + 1) * 32], ones, a_all[:, t, :], True, True, [k_ones, k_a], bk(0))
            ACT(eacs, pview(2), AF.Exp, pkeys(2), [k_eacs])
            ACT(edte, pview(4), AF.Exp, pkeys(4), [k_edte])
            ACT(etot, pview(0), AF.Exp, pkeys(0), [k_etot])
            TT("dve", dtdte, dt_all, edte, ALU.mult, [k_dt, k_edte], [k_dtdte])

            if want("dt%d" % l):
                tmp, k_tmp = S.alloc("dd", [128, 5 * NT * 32], F32)
                for i_, (ap_, k_) in enumerate(((dt_all, k_dt), (a_all, k_a), (eacs, k_eacs), (edte, k_edte), (etot, k_etot))):
                    CP("dve", tmp[:, i_ * NT * 32:(i_ + 1) * NT * 32], ap_.rearrange("p a b -> p (a b)"), [k_], [k_tmp])
                P.dma("sp", dbg_out[:, 0:5 * NT * 32], tmp, [k_tmp], ["dbg"], is_output=True)
                break
            wgs = [S.alloc("wg%d" % i, [128, 8, 512], BF16) for i in range(2)]

            def load_wg(g):
                wg, kwg = wgs[g % 2]
                WLOAD(wg[:, :, 0:256], wsrc(w_in, l, OFF_XBC + g * 256, 256), (), [(kwg, 0)])
                WLOAD(wg[:, :, 256:384], wsrc(w_in, l, OFF_XBC + 1024 + g * 128, 128), (), [(kwg, 1)])
                WLOAD(wg[:, :, 384:512], wsrc(w_in, l, OFF_XBC + 1536 + g * 128, 128), (), [(kwg, 2)])

            load_wg(0)
            load_wg(1)
            xbcT, k_xbcT = S.alloc("xbcT", [128, 4, NTOK], BF16)
            xs_tok, k_xs = S.alloc("xs_tok", [128, NT, 256], BF16)
            B_tok, k_Bt = S.alloc("B_tok", [128, NT, 128], BF16)
            CBms = [S.alloc("CBm%d" % i, [128, 2, 128], BF16) for i in range(2)]
            rhsb, k_rhsb = S.alloc("rhsb", [128, 2, 4, 128], F32)
            expEs = [S.alloc("expE%d" % i, [128, 2, 4, 128], BF16) for i in range(2)]
            MTs = [S.alloc("MT%d" % i, [128, 2, 4, 128], BF16) for i in range(2)]
            xds = [S.alloc("xd%d" % i, [128, 2, 4, 64], BF16) for i in range(2)]
            xdds = [S.alloc("xdd%d" % i, [128, 4, 64], BF16) for i in range(4)]
            t1, k_t1 = S.alloc("t1", [128, 4, 64], F32)
            t2, k_t2 = S.alloc("t2", [128, 4, 64], F32)
            t1b, k_t1b = S.alloc("t1b", [128, 4, 64], F32)
            t3s = [S.alloc("t3%d" % i, [128, 4, 64], F32) for i in range(2)]
            Hrun, k_Hrun = S.alloc("Hrun", [128, 2, 2, 256], F32)
            SX = Region(A, S.p, CAP)

            def h4(ap):
                return ap.rearrange("p (h d) -> p h d", h=4)

            for g in range(4):
                wg, kwg = wgs[g % 2]
                Upad, k_U = SX.alloc("Upad", [128, NTOK + 4], F32)
                acc, k_acc = SX.alloc("acc", [128, NTOK], F32)
                MSET("pool", Upad[:, 0:1], 0.0, [(k_U, "p0")])
                MSET("pool", Upad[:, 257:259], 0.0, [(k_U, "p1")])
                MSET("pool", Upad[:, 2307:2308], 0.0, [(k_U, "p2")])
                for cc in range(4):
                    cidx = (g * 2 + cc) if cc < 2 else (8 + g if cc == 2 else 12 + g)
                    kwp = (kwg, 0) if cc < 2 else (kwg, cc - 1)
                    for bi, (t0, n) in enumerate(BLOCKS):
                        b = (cc * 5 + bi) % 6
                        for kc in range(8):
                            MM(PS(b, n), wg[:, kc, cc * 128:(cc + 1) * 128], hT[:, kc, t0:t0 + n], kc == 0, kc == 7,
                               [kwp] + [(k_hT, t) for t in range(t0 // 128, (t0 + n) // 128)], [PSK(b)])
                        off = t0 + 1 if t0 == 0 else t0 + 3
                        ACT(Upad[:, off:off + n], PS(b, n), AF.Copy, [PSK(b)], [(k_U, bi)])
                    ukeys = [(k_U, bi) for bi in range(5)] + [(k_U, "p0"), (k_U, "p1"), (k_U, "p2")]
                    for (o0, n, u0) in ((0, 256, 0), (256, 2048, 258)):
                        ka = (k_acc, o0)
                        TS("dve", acc[:, o0:o0 + n], Upad[:, u0:u0 + n], ssdcw[:, cidx, 0:1], None, ALU.mult, None,
                           ukeys + [k_ssdcw], [ka])
                        STT("dve", acc[:, o0:o0 + n], Upad[:, u0 + 1:u0 + 1 + n], ssdcw[:, cidx, 1:2], acc[:, o0:o0 + n], ALU.mult, ALU.add,
                            ukeys + [k_ssdcw, ka], [ka])
                        STT("dve", acc[:, o0:o0 + n], Upad[:, u0 + 2:u0 + 2 + n], ssdcw[:, cidx, 2:3], acc[:, o0:o0 + n], ALU.mult, ALU.add,
                            ukeys + [k_ssdcw, ka], [ka])
                        ACT(xbcT[:, cc, o0:o0 + n], acc[:, o0:o0 + n], AF.Silu, [ka, k_ssdcb], [(k_xbcT, cc, o0)], bias=ssdcb[:, cidx:cidx + 1])
                SX.free_all()
                if want("xbc%d" % l):
                    dump(xbcT.rearrange("p a b -> p (a b)"), 4 * NTOK, [(k_xbcT, cc, o0) for cc in range(4) for o0 in (0, 256)], SX)
                    break
                if g + 2 < 4 and 'a' not in XF:
                    load_wg(g + 2)
                xk = lambda cc, t: (k_xbcT, cc, 0 if t < 2 else 256)
                for t in range(NT):
                    b = 6 + (t % 2)
                    pt = PS(b).bitcast(BF16)
                    for cc in range(3):
                        TR(pt[:, cc * 128:(cc + 1) * 128], xbcT[:, cc, t * 128:(t + 1) * 128], [xk(cc, t)], [PSK(b)])
                    if 'b' not in XF:
                        CP("dve", xs_tok[:, t, :], pt[:, 0:256], [PSK(b)], [(k_xs, t)])
                    if 'c' not in XF:
                        ACT(B_tok[:, t, :], pt[:, 256:384], AF.Copy, [PSK(b)], [(k_Bt, t)])
                if want("tok%d" % l):
                    dump(xs_tok.rearrange("p a b -> p (a b)"), NT * 256, [(k_xs, t) for t in range(NT)])
                    break
                Hin, k_Hin = SX.alloc("Hin", [128, 2, NT, 256], BF16)
                orders = [list(range(NT)), [1, 0] + list(range(NT - 1, 1, -1))]
                MSET("pool", Hrun[:, :, 0, :], 0.0, [(k_Hrun, d_, 0, h_) for d_ in range(2) for h_ in range(4)])

                def emit_xdd(i, d_):
                    c = orders[d_][i]
                    hs = slice(d_ * 16 + g * 4, d_ * 16 + g * 4 + 4)
                    xdd, k_xdd = xdds[(2 * i + d_) % 4]
                    TT("pool", xdd, h4(xs_tok[:, c, :]), dtdte[:, c, hs].unsqueeze(2).to_broadcast([128, 4, 64]), ALU.mult,
                       [(k_xs, c), k_dtdte], [k_xdd])

                for d_ in range(2):
                    emit_xdd(0, d_)
                for i in range(NT):
                    for d_ in range(2):
                        if i + 1 < NT - 1:
                            emit_xdd(i + 1, d_)
                    for d_ in range(2):
                        c = orders[d_][i]
                        pp = i % 2
                        ACT(Hin[:, d_, c, :], Hrun[:, d_, pp, :], AF.Copy, [(k_Hrun, d_, pp, h_) for h_ in range(4)], [(k_Hin, d_, c)])
                        if i == NT - 1:
                            continue
                        xdd, k_xdd = xdds[(2 * i + d_) % 4]
                        b = 6 + ((2 * i + d_) % 2)
                        MM(PS(b, 256), B_tok[:, c, :], xdd.rearrange("p h d -> p (h d)"), True, True, [(k_Bt, c), k_xdd], [PSK(b)])
                        for h_ in range(4):
                            hh = d_ * 16 + g * 4 + h_
                            STT("dve", Hrun[:, d_, 1 - pp, h_ * 64:(h_ + 1) * 64], Hrun[:, d_, pp, h_ * 64:(h_ + 1) * 64], etot[:, c, hh:hh + 1],
                                PS(b, 64, off=h_ * 64), ALU.mult, ALU.add, [(k_Hrun, d_, pp, h_), k_etot, PSK(b)], [(k_Hrun, d_, 1 - pp, h_)])
                dt_dh = lambda c: dt_all[:, c, :].rearrange("p (d h) -> p d h", d=2)[:, :, g * 4:g * 4 + 4]
                a_dh = lambda c: a_all[:, c, :].rearrange("p (d h) -> p d h", d=2)[:, :, g * 4:g * 4 + 4]

                def prepA1(c):
                    csl = slice(c * 128, (c + 1) * 128)
                    bA = c % 2
                    cbm, k_cbm = CBms[c % 2]
                    MM(PS(bA, 128), xbcT[:, 2, csl], xbcT[:, 3, csl], True, True, [xk(2, c), xk(3, c)], [PSK(bA)])
                    TT("dve", cbm, PS(bA, 128).unsqueeze(1).to_broadcast([128, 2, 128]), tri[:, 0:2, :], ALU.mult, [PSK(bA), k_tri], [k_cbm])
                    TT("pool", rhsb[:, 0], a_dh(c)[:, 0, :].unsqueeze(2).to_broadcast([128, 4, 128]),
                       tri[:, 0, :].unsqueeze(1).to_broadcast([128, 4, 128]), ALU.mult, [k_a, k_tri], [(k_rhsb, 0)])
                    TT("dve", rhsb[:, 1], a_dh(c)[:, 1, :].unsqueeze(2).to_broadcast([128, 4, 128]),
                       tri[:, 1, :].unsqueeze(1).to_broadcast([128, 4, 128]), ALU.mult, [k_a, k_tri], [(k_rhsb, 1)])

                def prepA2(c):
                    ee, k_ee = expEs[c % 2]
                    for d_ in range(2):
                        MM(PS(2 + d_), tri[:, 2 + d_, :], rhsb[:, d_].rearrange("p h l -> p (h l)"), True, True, [k_tri, (k_rhsb, d_)], [PSK(2 + d_)])
                        ACT(ee[:, d_].rearrange("p h l -> p (h l)"), PS(2 + d_), AF.Exp, [PSK(2 + d_)], [(k_ee, d_)])

                def prepB(c):
                    mt, k_mt = MTs[c % 2]
                    xd, k_xd = xds[c % 2]
                    cbm, k_cbm = CBms[c % 2]
                    ee, k_ee = expEs[c % 2]
                    TT("dve", mt, ee, cbm.unsqueeze(2).to_broadcast([128, 2, 4, 128]), ALU.mult, [(k_ee, 0), (k_ee, 1), k_cbm], [k_mt])
                    TT("pool", xd, h4(xs_tok[:, c, :]).unsqueeze(1).to_broadcast([128, 2, 4, 64]),
                       dt_dh(c).unsqueeze(3).to_broadcast([128, 2, 4, 64]), ALU.mult, [(k_xs, c), k_dt], [k_xd])
                    TT("pool", t3s[c % 2][0], h4(xs_tok[:, c, :]), dsk[:, g * 4:g * 4 + 4].unsqueeze(2).to_broadcast([128, 4, 64]), ALU.mult,
                       [(k_xs, c), k_dsk], [t3s[c % 2][1]])

                def finish_pe(c):
                    csl = slice(c * 128, (c + 1) * 128)
                    bY = 4 + (c % 2)
                    bO = 6 + (c % 2)
                    mt, k_mt = MTs[c % 2]
                    xd, k_xd = xds[c % 2]
                    for d_ in range(2):
                        for h_ in range(4):
                            MM(PS(bY, 64, off=h_ * 64), mt[:, d_, h_, :], xd[:, d_, h_, :], d_ == 0 and h_ == 0, d_ == 1, [k_mt, k_xd], [PSK(bY)], skip=True)
                    MM(PS(bY, 256, off=256), xbcT[:, 3, csl], Hin[:, 0, c, :], False, True, [xk(3, c), (k_Hin, 0, c)], [PSK(bY)], skip=True)
                    MM(PS(bO, 256), xbcT[:, 3, csl], Hin[:, 1, c, :], True, True, [xk(3, c), (k_Hin, 1, c)], [PSK(bO)])

                def finish_dve(c):
                    bY = 4 + (c % 2)
                    bO = 6 + (c % 2)
                    t3, k_t3 = t3s[c % 2]
                    TT("dve", t1, h4(PS(bY, 256, off=256)), eacs[:, c, g * 4:g * 4 + 4].unsqueeze(2).to_broadcast([128, 4, 64]), ALU.mult,
                       [PSK(bY), k_eacs], [k_t1])
                    TT("dve", t2, h4(PS(bO, 256)), eacs[:, c, 16 + g * 4:16 + g * 4 + 4].unsqueeze(2).to_broadcast([128, 4, 64]), ALU.mult,
                       [PSK(bO), k_eacs], [k_t2])
                    TT("pool", t2, t2, t3, ALU.add, [k_t2, k_t3], [k_t2])
                    TT("dve", t1b, h4(PS(bY, 256)), t1, ALU.add, [PSK(bY), k_t1], [k_t1b])
                    TT("dve", h4(ybuf[:, c, g * 256:(g + 1) * 256]), t1b, t2, ALU.add, [k_t1b, k_t2], [(k_ybuf, c, g)])

                oc = tiles_q
                no = len(oc)
                prepA1(oc[0])
                prepA2(oc[0])
                prepA1(oc[1])
                prepA2(oc[1])
                prepB(oc[0])
                for ci in range(no):
                    if ci + 2 < no:
                        prepA1(oc[ci + 2])
                    finish_pe(oc[ci])
                    if ci + 2 < no:
                        prepA2(oc[ci + 2])
                    if ci + 1 < no:
                        prepB(oc[ci + 1])
                    finish_dve(oc[ci])
                SX.free_all()
            if want("xbc%d" % l) or want("hin%d" % l) or want("tok%d" % l):
                break
            S.free_all()
            if want("ybuf%d" % l):
                dump(ybuf.rearrange("p a b -> p (a b)"), NT * D, [(k_ybuf, c, g) for c in range(NT) for g in range(4)], R23)
                break

            ysT, k_ysT = R2.alloc("ysT", [128, 8, NTOK], BF16)
            wz, k_wz = R3.alloc("wz", [128, 8, D], BF16)
            ssdnw, k_ssdnw = R3.alloc("ssdnw", [128, 8], F32)
            sz, k_sz = R3.alloc("sz", [128, 2, D], F32)
            yz, k_yz = R3.alloc("yz", [128, 2, D], F32)
            nbufs = norm_bufs(R3)
            P.dma("sp", ssdnw, ssdnwT_in[l], (), [k_ssdnw])
            for hf in range(2):
                WLOAD(wz[:, :, hf * 512:(hf + 1) * 512], wsrc(w_in, l, OFF_Z + hf * 512, 512), (), [(k_wz, hf)])
            def ro_front(i):
                t = tiles_q[i]
                for hf in range(2):
                    b = hf + 2 * (i % 2)
                    for kc in range(8):
                        MM(PS(b), hT[:, kc, t * 128:(t + 1) * 128], wz[:, kc, hf * 512:(hf + 1) * 512], kc == 0, kc == 7,
                           [(k_hT, t), (k_wz, hf)], [PSK(b)])
                    ACT(sz[:, i % 2, hf * 512:(hf + 1) * 512], PS(b), AF.Silu, [PSK(b)], [(k_sz, i % 2, hf)])
                TT("dve", yz[:, i % 2, :], ybuf[:, t, :], sz[:, i % 2, :], ALU.mult,
                   [(k_ybuf, t, g) for g in range(4)] + [(k_sz, i % 2, 0), (k_sz, i % 2, 1)], [(k_yz, i % 2)])
                norm_front(yz[:, i % 2, :], (k_yz, i % 2), D, nbufs, i)

            ro_front(0)
            for i, t in enumerate(tiles_q):
                if i + 1 < len(tiles_q):
                    ro_front(i + 1)
                norm_back(t, ssdnw, None, [k_ssdnw], ysT, k_ysT, nbufs, i)
            R3.free_all()
            R1.free_all()
            ysT_all = [(k_ysT, t) for t in tiles_q]
            if want("ysT%d" % l):
                dump(ysT.rearrange("p a b -> p (a b)"), 8 * NTOK, ysT_all, R3)
                break

            yaT, k_yaT = R1.alloc("yaT", [128, 8, NTOK], BF16)
            T_ = R3
            cosT, k_cos = T_.alloc("cosT", [128, SEQ], F32)
            sinT, k_sin = T_.alloc("sinT", [128, SEQ], F32)
            P.dma("sp", cosT, cos_in, (), [k_cos])
            P.dma("sp", sinT, sin_in, (), [k_sin])
            subln, k_subln = T_.alloc("subln", [128, 1], F32)
            lamv, k_lamv = T_.alloc("lamv", [128, 4, 64], F32)
            lprod, k_lprod = T_.alloc("lprod", [128, 2, 64], F32)
            lsm, k_lsm = T_.alloc("lsm", [128, 4], F32)
            P.dma("sp", subln, sublnT_in[l], (), [k_subln])
            P.dma("sp", lamv, lam_in[l].partition_broadcast(128).rearrange("p (a b) -> p a b", a=4), (), [k_lamv])
            TS("dve", subln, subln, 1.0 - lam_init, None, ALU.mult, None, [k_subln], [k_subln])
            TT("dve", lprod[:, 0, :], lamv[:, 0, :], lamv[:, 1, :], ALU.mult, [k_lamv], [(k_lprod, 0)])
            TT("dve", lprod[:, 1, :], lamv[:, 2, :], lamv[:, 3, :], ALU.mult, [k_lamv], [(k_lprod, 1)])
            P.op("dve", lambda e: e.tensor_reduce(out=lsm[:, 0:2], in_=lprod, axis=AX.X, op=ALU.add), [(k_lprod, 0), (k_lprod, 1)], [(k_lsm, 0)])
            ACT(lsm[:, 2:4], lsm[:, 0:2], AF.Exp, [(k_lsm, 0)], [(k_lsm, 1)])
            TT("dve", lsm[:, 0:1], lsm[:, 3:4], lsm[:, 2:3], ALU.subtract, [(k_lsm, 1)], [(k_lsm, 2)])
            TS("dve", lsm[:, 1:2], lsm[:, 0:1], -lam_init, None, ALU.add, None, [(k_lsm, 2)], [(k_lsm, 3)])
            neglam = lsm[:, 1:2]
            k_neglam = (k_lsm, 3)
            wsl = [T_.alloc("watt%d" % i, [128, 3, 8, 128], BF16) for i in range(2)]
            qraws = [T_.alloc("qraw%d" % i, [128, 512], BF16) for i in range(2)]
            qT, k_qT = T_.alloc("qT", [128, 2, NTOK], BF16)
            MSET("pool", qT[64:128, 0, :], 0.0, [(k_qT, "z0")])
            MSET("pool", qT[0:64, 1, :], 0.0, [(k_qT, "z1")])
            kT, k_kT = T_.alloc("kT", [128, NTOK], BF16)
            v_aug, k_v = T_.alloc("v_aug", [128, NT, 132], BF16)
            ropa = [T_.alloc("ropa%d" % i, [128, 512], F32) for i in range(2)]
            ropb = [T_.alloc("ropb%d" % i, [128, 512], F32) for i in range(2)]
            pTs = [T_.alloc("pT%d" % i, [128, 512], BF16) for i in range(4)]
            o4, k_o4 = T_.alloc("o4", [128, 4, 128], F32)
            on4, k_on4 = T_.alloc("on4", [128, 4, 128], BF16)
            rs4, k_rs4 = T_.alloc("rs4", [128, 4, 8], F32)
            att = {"sb": -1, "sbs": {}}
            MSET("pool", v_aug[:, :, 128:129], 1.0, [(k_v, "ones")])

            def load_watt(h):
                wt, kwt = wsl[h % 2]
                WLOAD(wt[:, 0], wsrc(w_in, l, OFF_Q + h * 128, 128), (), [(kwt, 0)])
                WLOAD(wt[:, 1], wsrc(w_in, l, OFF_K + h * 128, 128), (), [(kwt, 1)])
                WLOAD(wt[:, 2], wsrc(w_in, l, OFF_V + h * 128, 128), (), [(kwt, 2)])

            load_watt(0)
            load_watt(1)
            cnt = {"rope": 0, "post": 0}
            pend = []

            def flush_pend():
                if pend:
                    if pend[1] == 2:
                        post2(pend[0])
                    post3(pend[0], pend[2])
                    pend.clear()

            for h in range(8):
                wt, kwt = wsl[h % 2]
                for (wi, dst, kdst, isq) in ((0, None, None, True), (1, kT, k_kT, False)):
                    for bi, (t0, n) in enumerate(BLOCKS):
                        hk = [(k_hT, t) for t in range(t0 // 128, (t0 + n) // 128)]
                        if bi == 0:
                            if isq and not with_ctx:
                                continue
                            for kc in range(8):
                                MM(PS(0, n), wt[:, wi, kc, :], hT[:, kc, t0:t0 + n], kc == 0, kc == 7, [(kwt, wi)] + hk, [PSK(0)])
                            if isq:
                                ACT(qT[0:64, 0, t0:t0 + n], PS(0, n)[0:64, :], AF.Copy, [PSK(0), (k_qT, "z0")], [(k_qT, 0, bi)])
                                ACT(qT[64:128, 1, t0:t0 + n], PS(0, n)[64:128, :], AF.Copy, [PSK(0), (k_qT, "z1")], [(k_qT, 1, bi)])
                            else:
                                ACT(dst[:, t0:t0 + n], PS(0, n), AF.Copy, [PSK(0)], [(kdst, bi)])
                            continue
                        r_i = cnt["rope"] % 2
                        cnt["rope"] += 1
                        bA, bB = 2 * r_i, 2 * r_i + 1
                        for kc in range(8):
                            MM(PS(bA), wt[:, wi, kc, :], hT[:, kc, t0:t0 + n], kc == 0, kc == 7, [(kwt, wi)] + hk, [PSK(bA)])
                        qr, k_qr = qraws[r_i]
                        ACT(qr, PS(bA), AF.Copy, [PSK(bA)], [k_qr])
                        MM(PS(bB), permT, qr, True, True, [(k_perm, 0), (k_perm, 1), k_qr], [PSK(bB)])
                        ra, k_ra = ropa[r_i]
                        rb_, k_rb_ = ropb[r_i]
                        c0 = t0 - CTX
                        TT("dve", ra, PS(bA), cosT[:, c0:c0 + n], ALU.mult, [PSK(bA), k_cos], [k_ra])
                        TT("dve", rb_, PS(bB), sinT[:, c0:c0 + n], ALU.mult, [PSK(bB), k_sin], [k_rb_])
                        if isq:
                            TT("pool", qT[0:64, 0, t0:t0 + n], ra[0:64, :], rb_[0:64, :], ALU.add, [k_ra, k_rb_, (k_qT, "z0")], [(k_qT, 0, bi)])
                            TT("pool", qT[64:128, 1, t0:t0 + n], ra[64:128, :], rb_[64:128, :], ALU.add, [k_ra, k_rb_, (k_qT, "z1")], [(k_qT, 1, bi)])
                        else:
                            TT("pool", dst[:, t0:t0 + n], ra, rb_, ALU.add, [k_ra, k_rb_], [(kdst, bi)])
                for t4 in range(0, NT, 4):
                    nt4 = min(4, NT - t4)
                    b = 4 + ((t4 // 4) % 2)
                    for ti in range(nt4):
                        t = t4 + ti
                        for kc in range(8):
                            MM(PS(b, 128, off=ti * 128), hT[:, kc, t * 128:(t + 1) * 128], wt[:, 2, kc, :], kc == 0, kc == 7,
                               [(k_hT, t), (kwt, 2)], [PSK(b)])
                    CP("dve", v_aug[:, t4:t4 + nt4, 0:128], PS(b, nt4 * 128).rearrange("p (t e) -> p t e", e=128), [PSK(b)],
                       [(k_v, t) for t in range(t4, t4 + nt4)])
                flush_pend()
                if h + 2 < 8:
                    load_watt(h + 2)
                qblocks = ([(0, 256, [0, 1], [0])] if with_ctx else []) + [(256 + 512 * j, 512, list(range(NT)), [1 + j]) for j in range(4)]
                LOOK = 3

                def acc_of(nq, comp, j, n=129):
                    idx = comp * nq + j
                    return PS(4 + idx // 3, n, off=(idx % 3) * 132), PSK(4 + idx // 3), idx

                def emit_S(qb, i):
                    q0, qn, ktiles, qbi = qb
                    ki, comp = i // 2, i % 2
                    kt = ktiles[ki]
                    sb = att["sb"] = (att["sb"] + 1) % 4
                    att["sbs"][(q0, i)] = sb
                    MM(PS(sb, qn), kT[:, kt * 128:(kt + 1) * 128], qT[:, comp, q0:q0 + qn], True, True,
                       [(k_kT, 0 if kt < 2 else 1 + (kt - 2) // 4), (k_qT, "z%d" % (1 - comp))] + [(k_qT, comp, b_) for b_ in qbi], [PSK(sb)])

                def emit_EP(qb, i):
                    q0, qn, ktiles, qbi = qb
                    nq = qn // 128
                    ki, comp = i // 2, i % 2
                    kt = ktiles[ki]
                    sb = att["sbs"].pop((q0, i))
                    pT_, k_pT = pTs[sb]
                    ACT(pT_[:, 0:qn], PS(sb, qn), AF.Exp, [PSK(sb)], [k_pT], scale=ATT_SCALE)
                    for j in range(nq):
                        a_, ka_, idx = acc_of(nq, comp, j)
                        MM(a_, pT_[:, j * 128:(j + 1) * 128], v_aug[:, kt, 0:129], ki == 0 and idx % 3 == 0, ki == len(ktiles) - 1,
                           [k_pT, (k_v, kt), (k_v, "ones")], [ka_], skip=True)

                def post1(qb):
                    q0, qn, ktiles, qbi = qb
                    nq = qn // 128
                    A0 = [acc_of(nq, 0, j) for j in range(nq)]
                    A1 = [acc_of(nq, 1, j) for j in range(nq)]
                    J = range(nq)
                    for j in J:
                        RECIP(rs4[:, j, 0:1], A0[j][0][:, 128:129], [A0[j][1]], [(k_rs4, j, 0)])
                        RECIP(rs4[:, j, 1:2], A1[j][0][:, 128:129], [A1[j][1]], [(k_rs4, j, 1)])
                    for j in J:
                        TT("dve", rs4[:, j, 2:3], rs4[:, j, 1:2], neglam, ALU.mult, [(k_rs4, j, 1), k_neglam], [(k_rs4, j, 2)])
                    for j in J:
                        TS("dve", o4[:, j, :], A0[j][0][:, 0:128], rs4[:, j, 0:1], None, ALU.mult, None, [A0[j][1], (k_rs4, j, 0)], [(k_o4, j)])
                    for j in J:
                        STT("dve", o4[:, j, :], A1[j][0][:, 0:128], rs4[:, j, 2:3], o4[:, j, :], ALU.mult, ALU.add,
                            [A1[j][1], (k_rs4, j, 2), (k_o4, j)], [(k_o4, j)])

                def post2(qb):
                    J = range(qb[1] // 128)
                    for j in J:
                        ACT(on4[:, j, :], o4[:, j, :], AF.Square, [(k_o4, j)], [(k_on4, j), (k_rs4, j, 3)], accum_out=rs4[:, j, 3:4])
                    for j in J:
                        ACT(rs4[:, j, 4:5], rs4[:, j, 3:4], AF.Ln, [(k_rs4, j, 3)], [(k_rs4, j, 4)], scale=1.0 / 128, bias=EPS)
                    for j in J:
                        ACT(rs4[:, j, 5:6], rs4[:, j, 4:5], AF.Exp, [(k_rs4, j, 4)], [(k_rs4, j, 5)], scale=-0.5)
                    for j in J:
                        TS("dve", on4[:, j, :], o4[:, j, :], rs4[:, j, 5:6], None, ALU.mult, None, [(k_o4, j), (k_rs4, j, 5)], [(k_on4, j)])

                def post3(qb, h):
                    q0 = qb[0]
                    J = range(qb[1] // 128)
                    pt = PS(7).bitcast(BF16)
                    for j in J:
                        TR(pt[:, j * 128:(j + 1) * 128], on4[:, j, :], [(k_on4, j)], [PSK(7)])
                    for j in J:
                        tq = q0 // 128 + j
                        ACT(yaT[:, h, tq * 128:(tq + 1) * 128], pt[:, j * 128:(j + 1) * 128], AF.Identity, [PSK(7), k_subln],
                            [(k_yaT, tq, h)], scale=subln[:, 0:1])

                for bi_, qb in enumerate(qblocks):
                    nst = 2 * len(qb[2])
                    if bi_ == 0:
                        for i in range(min(LOOK, nst)):
                            emit_S(qb, i)
                    for i in range(nst):
                        if i + LOOK < nst:
                            emit_S(qb, i + LOOK)
                        emit_EP(qb, i)
                        if pend and pend[1] == 2 and i >= 3:
                            post2(pend[0])
                            pend[1] = 3
                        if pend and pend[1] == 3 and i >= 8:
                            post3(pend[0], pend[2])
                            pend.clear()
                    flush_pend()
                    if bi_ + 1 < len(qblocks):
                        nb = qblocks[bi_ + 1]
                        for i in range(min(LOOK, 2 * len(nb[2]))):
                            emit_S(nb, i)
                    post1(qb)
                    pend.extend([qb, 2, h])
            flush_pend()
            T_.free_all()
            yaT_all = [(k_yaT, t, h) for t in tiles_q for h in range(8)]
            if want("yaT%d" % l):
                dump(yaT.rearrange("p a b -> p (a b)"), 8 * NTOK, yaT_all)
                break

            mT, k_mT = R3.alloc("mT", [128, 8, NTOK], BF16)
            wms = [R3.alloc("wmrg%d" % i, [128, 4, 8, 128], BF16) for i in range(2)]
            sgs = [R3.alloc("sg%d" % i, [128, 2, 512], F32) for i in range(2)]
            tms = [R3.alloc("tm%d" % i, [128, 2, 512], F32) for i in range(2)]

            def load_wm(j):
                wt, kwt = wms[j % 2]
                WLOAD(wt[:, 0], wsrc(w_brs, l, j * 128, 128), (), [(kwt, 0)])
                WLOAD(wt[:, 1], wsrc(w_bra, l, j * 128, 128), (), [(kwt, 1)])
                WLOAD(wt[:, 2], wsrc(w_in, l, OFF_G + j * 128, 128), (), [(kwt, 2)])
                WLOAD(wt[:, 3], wsrc(w_in, l, OFF_G + D + j * 128, 128), (), [(kwt, 3)])

            load_wm(0)
            load_wm(1)
            it = 0
            for j in range(8):
                wt, kwt = wms[j % 2]
                for (t0, n) in blocks_q:
                    tl = list(range(t0 // 128, (t0 + n) // 128))
                    b0 = 4 * (it % 2)
                    sg, k_sg = sgs[it % 2]
                    tm, k_tm = tms[it % 2]
                    it += 1
                    srcs = ((ysT, [(k_ysT, t) for t in tl]), (yaT, [(k_yaT, t, h_) for t in tl for h_ in range(8)]),
                            (hT, [(k_hT, t) for t in tl]), (hT, [(k_hT, t) for t in tl]))
                    for wi in range(4):
                        src, sk = srcs[wi]
                        for kc in range(8):
                            MM(PS(b0 + wi, n), wt[:, wi, kc, :], src[:, kc, t0:t0 + n], kc == 0, kc == 7, [(kwt, wi)] + sk, [PSK(b0 + wi)])
                    for wi in range(2):
                        ACT(sg[:, wi, 0:n], PS(b0 + 2 + wi, n), AF.Sigmoid, [PSK(b0 + 2 + wi)], [(k_sg, wi)])
                    for wi in range(2):
                        TT("dve", tm[:, wi, 0:n], PS(b0 + wi, n), sg[:, wi, 0:n], ALU.mult, [PSK(b0 + wi), (k_sg, wi)], [(k_tm, wi)])
                    TT("pool", mT[:, j, t0:t0 + n], tm[:, 0, 0:n], tm[:, 1, 0:n], ALU.add, [(k_tm, 0), (k_tm, 1)], [(k_mT, t, j) for t in tl])
                if j + 2 < 8:
                    load_wm(j + 2)
            for k in R3.keys[1:]:
                A.free(k)
            del R3.keys[1:]
            R0.free_all()
            R1.free_all()
            R2.free_all()

            h2T, k_h2T = R0.alloc("h2T", [128, 8, NTOK], BF16)
            wo, k_wo = R1.alloc("wo", [128, 8, D], BF16)
            for hf in range(2):
                WLOAD(wo[:, :, hf * 512:(hf + 1) * 512], wsrc(w_out, l, hf * 512, 512), (), [(k_wo, hf)])
            xts = [R2.alloc("xt%d" % i, [128, D], F32) for i in range(2)]
            xns = [R2.alloc("xnew%d" % i, [128, D], F32) for i in range(2)]
            nbufs = norm_bufs(R2)
            def op_front(i):
                t = tiles_q[i]
                w_ = 1 if t < 2 else 0
                xt, kx = xts[i % 2]
                xnw, kxn_ = xns[i % 2]
                P.dma("sp", xt, xsrc(t), [("xres", t)], [kx])
                for hf in range(2):
                    b = hf + 2 * (i % 2)
                    for kc in range(8):
                        MM(PS(b), mT[:, kc, t * 128:(t + 1) * 128], wo[:, kc, hf * 512:(hf + 1) * 512], kc == 0, kc == 7,
                           [(k_mT, t, kc), (k_wo, hf)], [PSK(b)])
                    TT("dve", xnw[:, hf * 512:(hf + 1) * 512], PS(b), g1bc[:, w_, hf * 512:(hf + 1) * 512], ALU.mult,
                       [PSK(b), (k_g1bc, w_, hf)], [(kxn_, hf)])
                kxh = [(kxn_, 0), (kxn_, 1)]
                TT("pool", xnw, xnw, xt, ALU.add, kxh + [kx], kxh)
                P.dma("sp", xres[t * 128:(t + 1) * 128, :], xnw, kxh, [("xres", t)])
                norm_front(xnw, kxh, D, nbufs, i)

            op_front(0)
            for i, t in enumerate(tiles_q):
                if i + 1 < len(tiles_q):
                    op_front(i + 1)
                w_ = 1 if t < 2 else 0
                norm_back(t, scl2[:, :, w_], modT[:, 24:32, w_], [k_scl2] + [(k_modT, j) for j in range(24, 32)], h2T, k_h2T, nbufs, i)
            R1.free_all()
            R2.free_all()
            R3.free_all()
            if want("xm%d" % l):
                P.dma("sp", dbg_out.rearrange("p (t f) -> p t f", t=NT), xres.rearrange("(t p) f -> p t f", p=128),
                      [("xres", t) for t in range(NT)], ["dbg"], is_output=True)
                break
            if want("h2T%d" % l):
                dump(h2T.rearrange("p a b -> p (a b)"), 8 * NTOK, [(k_h2T, t) for t in tiles_q])
                break

            GT0 = BASE + SZ
            RG = Region(A, GT0, GT0 + NFF * NTOK * 2)
            RF = Region(A, GT0 + NFF * NTOK * 2, CAP)
            gT, k_gT = RG.alloc("gT", [128, NFF, NTOK], BF16)
            ffncw, k_fcw = RF.alloc("ffncw", [128, 2 * NFF, 3], F32)
            ffncb, k_fcb = RF.alloc("ffncb", [128, 2 * NFF], F32)
            P.dma("sp", ffncw, ffncw_in[l].rearrange("p (c k) -> p c k", k=3), (), [k_fcw])
            P.dma("sp", ffncb, ffncb_in[l], (), [k_fcb])
            Upads = [RF.alloc("Upad%d" % i, [128, NTOK + 4], F32) for i in range(2)]
            acc_, k_acc = RF.alloc("acc", [128, NTOK], F32)
            sa, k_sa = RF.alloc("sa", [128, NTOK], BF16)
            NWU = 2
            wus = [RF.alloc("wup%d" % i, [128, 8, 256], BF16) for i in range(NWU)]
            for Upad, k_U in Upads:
                MSET("pool", Upad[:, 0:1], 0.0, [(k_U, "p0")])
                MSET("pool", Upad[:, 257:259], 0.0, [(k_U, "p1")])
                MSET("pool", Upad[:, 2307:2308], 0.0, [(k_U, "p2")])

            def load_wu(j):
                wt, kwt = wus[j % NWU]
                WLOAD(wt[:, :, 0:128], wsrc(w_up, l, j * 128, 128), (), [(kwt, 0)])
                WLOAD(wt[:, :, 128:256], wsrc(w_up, l, D_FF + j * 128, 128), (), [(kwt, 1)])

            for j in range(NWU):
                load_wu(j)
            ranges = ([(0, 256, 0)] if with_ctx else []) + [(256, 2048, 258)]
            bi_q = list(range(0 if with_ctx else 1, 5))
            it = 0
            for j in range(NFF):
                wt, kwt = wus[j % NWU]
                for part in range(2):
                    cidx = j + part * NFF
                    Upad, k_U = Upads[part]
                    for bi in bi_q:
                        t0, n = BLOCKS[bi]
                        b = it % 6
                        it += 1
                        for kc in range(8):
                            MM(PS(b, n), wt[:, kc, part * 128:(part + 1) * 128], h2T[:, kc, t0:t0 + n], kc == 0, kc == 7,
                               [(kwt, part)] + [(k_h2T, t) for t in range(t0 // 128, (t0 + n) // 128)], [PSK(b)])
                        off = t0 + 1 if t0 == 0 else t0 + 3
                        ACT(Upad[:, off:off + n], PS(b, n), AF.Copy, [PSK(b)], [(k_U, bi)])
                    ukeys = [(k_U, bi) for bi in bi_q] + [(k_U, "p0"), (k_U, "p1"), (k_U, "p2")]
                    for (o0, n, u0) in ranges:
                        ka = (k_acc, o0)
                        TS("dve", acc_[:, o0:o0 + n], Upad[:, u0:u0 + n], ffncw[:, cidx, 0:1], None, ALU.mult, None, ukeys + [k_fcw], [ka])
                        STT("dve", acc_[:, o0:o0 + n], Upad[:, u0 + 1:u0 + 1 + n], ffncw[:, cidx, 1:2], acc_[:, o0:o0 + n], ALU.mult, ALU.add,
                            ukeys + [k_fcw, ka], [ka])
                        STT("dve", acc_[:, o0:o0 + n], Upad[:, u0 + 2:u0 + 2 + n], ffncw[:, cidx, 2:3], acc_[:, o0:o0 + n], ALU.mult, ALU.add,
                            ukeys + [k_fcw, ka], [ka])
                        if part == 0:
                            ACT(sa[:, o0:o0 + n], acc_[:, o0:o0 + n], AF.Silu, [ka, k_fcb], [(k_sa, o0)], bias=ffncb[:, cidx:cidx + 1])
                        else:
                            STT("dve", gT[:, j, o0:o0 + n], acc_[:, o0:o0 + n], ffncb[:, cidx:cidx + 1], sa[:, o0:o0 + n], ALU.add, ALU.mult,
                                [ka, k_fcb, (k_sa, o0)], [(k_gT, j, o0)])
                if j + NWU < NFF:
                    load_wu(j + NWU)
            RF.free_all()
            R0.free_all()
            if want("gT%d" % l):
                dump(gT.rearrange("p a b -> p (a b)"), NFF * NTOK, [(k_gT, j, o0) for j in range(NFF) for (o0, _, _) in ranges])
                break

            last = l == DEPTH - 1
            wds = [R0.alloc("wd0", [128, NFF, 512], BF16), RF.alloc("wd1", [128, NFF, 512], BF16)]
            for hf in range(2):
                for j0 in range(0, NFF, 6):
                    j1 = min(NFF, j0 + 6)
                    WLOAD(wds[hf][0][:, j0:j1, :], w_down[l].rearrange("(j p) n -> p j n", p=128)[:, j0:j1, hf * 512:(hf + 1) * 512],
                          (), [(wds[hf][1], j0)])
            xts = [RF.alloc("xt%d" % i, [128, D], F32) for i in range(2)]
            xns = [RF.alloc("xnew%d" % i, [128, D], F32) for i in range(2)]
            if last:
                fnw, k_fnw = R0.alloc("fnw", [128, D], F32)
                fjunk, k_fjunk = R0.alloc("fjunk", [128, 2, D], BF16)
                fss, k_fss = R0.alloc("fss", [128, 2, 4], F32)
                P.dma("sp", fnw, fnw_in.partition_broadcast(128), (), [k_fnw])
            for i, t in enumerate(tiles_q):
                w_ = 1 if t < 2 else 0
                xt, kx = xts[i % 2]
                xnw, kxn_ = xns[i % 2]
                P.dma("sp", xt, xres[t * 128:(t + 1) * 128, :], [("xres", t)], [kx])
                gk = [(k_gT, j, 0 if t < 2 else 256) for j in range(NFF)]
                for hf in range(2):
                    b = hf + 2 * (i % 2)
                    wd, kwd = wds[hf]
                    for j in range(NFF):
                        MM(PS(b), gT[:, j, t * 128:(t + 1) * 128], wd[:, j, :], j == 0, j == NFF - 1,
                           [(k_gT, j, 0 if t < 2 else 256), (kwd, (j // 6) * 6)], [PSK(b)])
                    TT("dve", xnw[:, hf * 512:(hf + 1) * 512], PS(b), g2bc[:, w_, hf * 512:(hf + 1) * 512], ALU.mult,
                       [PSK(b), (k_g2bc, w_, hf)], [(kxn_, hf)])
                kxh = [(kxn_, 0), (kxn_, 1)]
                TT("pool", xnw, xnw, xt, ALU.add, kxh + [kx], kxh)
                if not last:
                    P.dma("sp", xres[t * 128:(t + 1) * 128, :], xnw, kxh, [("xres", t)])
                else:
                    pi = i % 2
                    ACT(fjunk[:, pi, :], xnw, AF.Square, kxh, [(k_fjunk, pi), (k_fss, pi, 0)], accum_out=fss[:, pi, 0:1])
                    ACT(fss[:, pi, 1:2], fss[:, pi, 0:1], AF.Ln, [(k_fss, pi, 0)], [(k_fss, pi, 1)], scale=1.0 / D, bias=EPS)
                    ACT(fss[:, pi, 2:3], fss[:, pi, 1:2], AF.Exp, [(k_fss, pi, 1)], [(k_fss, pi, 2)], scale=-0.5)
                    STT("dve", xt, xnw, fss[:, pi, 2:3], fnw, ALU.mult, ALU.mult, kxh + [(k_fss, pi, 2), k_fnw, kx], [kx])
                    P.dma("sp", y_out[(t - 2) * 128:(t - 1) * 128, :], xt, [kx], [("y", t)], is_output=True)
            RF.free_all()
            R0.free_all()
            RG.free_all()
            if want("xf%d" % l):
                P.dma("sp", dbg_out.rearrange("p (t f) -> p t f", t=NT), xres.rearrange("(t p) f -> p t f", p=128),
                      [("xres", t) for t in range(NT)], ["dbg"], is_output=True)
                break

        P.finish()
        P.emit()
        print("ops", P.nops, "arena peak", A.peak)
    return nc


def rope_tables():
    inv_freq = (np.float32(10000.0) ** (-np.arange(16, dtype=np.float32) / np.float32(16))).astype(np.float32)
    t = np.arange(SEQ)
    row = (t // 64).astype(np.float32)
    col = (t % 64).astype(np.float32)
    cosT = np.zeros((128, SEQ), np.float32)
    sinT = np.zeros((128, SEQ), np.float32)
    for p in range(128):
        d = p % 64
        axis, half, f = d // 32, (d % 32) // 16, d % 16
        ang = ((row if axis == 0 else col) * inv_freq[f]).astype(np.float32)
        cosT[p] = np.cos(ang)
        sinT[p] = np.sin(ang) * (-1.0 if half == 0 else 1.0)
    return cosT, sinT


def make_in_maps(inp):
    f = lambda a: np.ascontiguousarray(np.asarray(a, dtype=np.float32))
    cosT, sinT = rope_tables()
    shared = {
        "w_mod": f(inp["w_mod"]), "b_mod": f(inp["b_mod"]),
        "bmodT": f(np.asarray(inp["b_mod"]).reshape(DEPTH, 48, 128).transpose(0, 2, 1)),
        "n1wT": f(np.asarray(inp["norm1_w"]).reshape(DEPTH, 8, 128).transpose(0, 2, 1)),
        "n2wT": f(np.asarray(inp["norm2_w"]).reshape(DEPTH, 8, 128).transpose(0, 2, 1)),
        "ssdnwT": f(np.asarray(inp["ssd_norm_w"]).reshape(DEPTH, 8, 128).transpose(0, 2, 1)),
        "sublnT": f(np.asarray(inp["att_subln_w"]).reshape(DEPTH, 128, 1)),
        "w_in": f(inp["w_in"]),
        "ssdcw": f(np.asarray(inp["ssd_conv_w"]).transpose(0, 2, 1).reshape(DEPTH, 16, 128, 3).transpose(0, 2, 1, 3).reshape(DEPTH, 128, 48)),
        "ssdcb": f(np.asarray(inp["ssd_conv_b"]).reshape(DEPTH, 16, 128).transpose(0, 2, 1)),
        "alog": f(np.asarray(inp["ssd_a_log"]).reshape(DEPTH, 32)),
        "dtb": f(np.asarray(inp["ssd_dt_bias"]).reshape(DEPTH, 32)),
        "dskip": f(inp["ssd_d"]),
        "lamv": f(np.asarray(inp["diff_lambda"]).reshape(DEPTH, 256)),
        "w_br_ssd": f(inp["w_br_ssd"]), "w_br_att": f(inp["w_br_att"]), "w_out": f(inp["w_out"]),
        "w_up": f(inp["w_up"]),
        "ffncw": f(np.asarray(inp["ffn_conv_w"]).transpose(0, 2, 1).reshape(DEPTH, 2 * NFF, 128, 3).transpose(0, 2, 1, 3).reshape(DEPTH, 128, 2 * NFF * 3)),
        "ffncb": f(np.asarray(inp["ffn_conv_b"]).reshape(DEPTH, 2 * NFF, 128).transpose(0, 2, 1)),
        "w_down": f(inp["w_down"]),
        "fnw": f(inp["final_norm_w"]),
        "ropecos": cosT, "ropesin": sinT,
    }
    x = np.asarray(inp["x"], dtype=np.float32)
    ctx = np.asarray(inp["ctx"], dtype=np.float32)
    c = np.asarray(inp["c"], dtype=np.float32)
    c_ctx = np.asarray(inp["c_ctx"], dtype=np.float32)
    maps = []
    for b in range(8):
        cT = np.stack([c[b].reshape(8, 128).T, c_ctx.reshape(8, 128).T], axis=-1).reshape(128, 16)
        m = dict(shared)
        m["x_b"] = f(x[b])
        m["ctx_b"] = f(ctx[b])
        m["cT"] = f(cT)
        maps.append(m)
    return maps


def kernel(**inputs):
    nc = build_program()
    res = run_bass_kernel_spmd(nc, make_in_maps(inputs), core_ids=list(range(8)))
    return np.stack([np.asarray(r["y"], dtype=np.float32) for r in res.results], axis=0)
```

```python
import math
import os
XF = os.environ.get('KX', '')
from contextlib import ExitStack

import numpy as np
import concourse.bass as bass
import concourse.mybir as mybir
from concourse.bass_utils import run_bass_kernel_spmd

F32 = mybir.dt.float32
BF16 = mybir.dt.bfloat16
AF = mybir.ActivationFunctionType
ALU = mybir.AluOpType
AX = mybir.AxisListType

D = 1024
SEQ = 2048
CTX = 256
NTOK = SEQ + CTX
NT = NTOK // 128
DEPTH = 2
EPS = 1e-6
NH_SSD = 16
D_FF = 2816
NFF = D_FF // 128
IN_W = 8224
OFF_Z, OFF_XBC, OFF_DT, OFF_Q, OFF_K, OFF_V, OFF_G = 0, 1024, 3072, 3104, 4128, 5152, 6176
ATT_SCALE = 64 ** -0.5
BLOCKS = [(0, 256)] + [(256 + 512 * j, 512) for j in range(4)]

ENGS = ("pe", "act", "dve", "pool", "sp")


def _base(k):
    return k if isinstance(k, str) else k[0]


class Prog:
    NDMA = 8

    def __init__(self, nc, es):
        self.nc = nc
        self.lists = {e: [] for e in ENGS}
        self.cnt = {e: 0 for e in ENGS}
        self.sem = {e: es.enter_context(nc.semaphore("s_" + e)) for e in ENGS}
        self.waited = {e: {} for e in ENGS}
        self.last_w = {}
        self.readers = {}
        self.touch = {}
        self.inherit = {}
        self.seen = set()
        self.dsem = {}
        self.dcnt = {}
        self.drot = {q: 0 for q in ("sp", "act", "pool")}
        for q in ("sp", "act", "pool"):
            for i in range(self.NDMA):
                nm = "d_%s%d" % (q, i)
                self.dsem[nm] = es.enter_context(nc.semaphore(nm))
                self.dcnt[nm] = 0
        self.out_waits = {}
        self.nops = 0
        self.ps_last = {}

    def _semh(self, name):
        return self.sem[name] if name in self.sem else self.dsem[name]

    @staticmethod
    def _merge(d, e, s):
        if d.get(e, 0) < s:
            d[e] = s

    def _deps(self, reads, writes, eng=None):
        deps = {}
        for k in list(reads) + list(writes):
            if _base(k) == "ps":
                for e, s in self.ps_last.get(k, {}).items():
                    if e != eng:
                        self._merge(deps, e, s)
        for k in list(reads) + list(writes):
            if k not in self.seen:
                for e, s in self.inherit.get(_base(k), {}).items():
                    self._merge(deps, e, s)
        for k in reads:
            w = self.last_w.get(k)
            if w is not None:
                self._merge(deps, *w)
        for k in writes:
            w = self.last_w.get(k)
            if w is not None:
                self._merge(deps, *w)
            for e, s in self.readers.get(k, {}).items():
                self._merge(deps, e, s)
        return deps

    def _waits(self, eng, deps):
        out = []
        for e2, s2 in deps.items():
            if e2 == eng and eng == "pe":
                continue
            if self.waited[eng].get(e2, 0) >= s2:
                continue
            self.waited[eng][e2] = s2
            out.append((e2, s2))
        return out

    def _commit(self, tag, reads, writes):
        e, s = tag
        for k in reads:
            self.seen.add(k)
            self._merge(self.readers.setdefault(k, {}), e, s)
            self._merge(self.touch.setdefault(_base(k), {}), e, s)
        for k in writes:
            self.seen.add(k)
            self.last_w[k] = tag
            self.readers[k] = {}
            self._merge(self.touch.setdefault(_base(k), {}), e, s)
        for k in list(reads) + list(writes):
            if _base(k) == "ps":
                self.ps_last.setdefault(k, {})[e] = s

    def op(self, eng, emit, reads=(), writes=()):
        deps = self._deps(reads, writes, eng)
        waits = self._waits(eng, deps)
        self.cnt[eng] += 1
        self.lists[eng].append((waits, emit, (eng, 1)))
        self._commit((eng, self.cnt[eng]), reads, writes)
        self.nops += 1

    def dma(self, q, out, in_, reads=(), writes=(), is_output=False, **kw):
        deps = self._deps(reads, writes)
        nm = "d_%s%d" % (q, self.drot[q] % self.NDMA)
        self.drot[q] += 1
        if self.dcnt[nm] > 0:
            self._merge(deps, nm, self.dcnt[nm])
        waits = self._waits(q, deps)
        self.dcnt[nm] += 16
        seq = self.dcnt[nm]
        self.lists[q].append((waits, lambda e: e.dma_start(out=out, in_=in_, **kw), (nm, 16)))
        self._commit((nm, seq), reads, writes)
        if is_output:
            self._merge(self.out_waits, nm, seq)
        self.nops += 1

    def finish(self):
        self.lists["sp"].append((list(self.out_waits.items()), None, None))

    def emit(self):
        nc = self.nc
        with nc.Block() as block:
            def run(e, lst):
                for waits, emit, inc in lst:
                    for (nm, v) in waits:
                        e.wait_ge(self._semh(nm), v)
                    if emit is not None:
                        emit(e).then_inc(self._semh(inc[0]), inc[1])

            @block.tensor
            def _(e):
                run(e, self.lists["pe"])

            @block.scalar
            def _(e):
                run(e, self.lists["act"])

            @block.vector
            def _(e):
                run(e, self.lists["dve"])

            @block.gpsimd
            def _(e):
                run(e, self.lists["pool"])

            @block.sync
            def _(e):
                run(e, self.lists["sp"])


class Arena:
    def __init__(self, tensor, cap_bytes, prog):
        self.t = tensor
        self.cap = cap_bytes
        self.P = prog
        self.live = {}
        self.hist = []
        self.uid = 0
        self.peak = 0

    def alloc_at(self, name, shape, dt, off):
        esz = 4 if dt == F32 else 2
        n = int(np.prod(shape[1:]))
        size = (n * esz + 31) // 32 * 32
        assert off % 32 == 0
        assert off + size <= self.cap, "SBUF arena overflow at %s: %d > %d" % (name, off + size, self.cap)
        for k2, (o, s_) in self.live.items():
            assert not (o < off + size and off < o + s_), "overlap %s with live %s" % (name, k2)
        self.peak = max(self.peak, off + size)
        self.uid += 1
        key = "%s#%d" % (name, self.uid)
        inh = {}
        keep = []
        for (nm, o, s_) in self.hist:
            if o < off + size and off < o + s_:
                for e, q in self.P.touch.get(nm, {}).items():
                    Prog._merge(inh, e, q)
                for e, q in self.P.inherit.get(nm, {}).items():
                    Prog._merge(inh, e, q)
                if off <= o and o + s_ <= off + size:
                    continue
            keep.append((nm, o, s_))
        self.hist = keep
        self.P.inherit[key] = inh
        a = self.t[0:shape[0], off // 4:(off + size) // 4]
        if dt != F32:
            a = a.bitcast(dt)
        a = a[:, 0:n]
        if len(shape) == 3:
            a = a.rearrange("p (a b) -> p a b", a=shape[1])
        elif len(shape) == 4:
            a = a.rearrange("p (a b c) -> p a b c", a=shape[1], b=shape[2])
        self.live[key] = (off, size)
        return a, key, size

    def free(self, key):
        o, s_ = self.live.pop(key)
        self.hist.append((key, o, s_))


class Region:
    def __init__(self, arena, lo, hi):
        self.A, self.lo, self.hi = arena, lo, hi
        self.p = lo
        self.keys = []

    def alloc(self, name, shape, dt):
        a, key, size = self.A.alloc_at(name, shape, dt, self.p)
        self.p += size
        assert self.p <= self.hi, "region overflow at %s: %d > %d" % (name, self.p, self.hi)
        self.keys.append(key)
        return a, key

    def free_all(self):
        for k in self.keys:
            self.A.free(k)
        self.keys = []
        self.p = self.lo


def build_program(dbg=None):
    nc = bass.Bass("TRN2", target_bir_lowering=False)

    def din(name, shape):
        return nc.dram_tensor(name, list(shape), F32, kind="ExternalInput").ap()

    x_in = din("x_b", [SEQ, D])
    ctx_in = din("ctx_b", [CTX, D])
    cT_in = din("cT", [128, 16])
    w_mod = din("w_mod", [DEPTH, D, 6 * D])
    b_mod = din("b_mod", [DEPTH, 6 * D])
    bmodT_in = din("bmodT", [DEPTH, 128, 48])
    n1wT_in = din("n1wT", [DEPTH, 128, 8])
    n2wT_in = din("n2wT", [DEPTH, 128, 8])
    ssdnwT_in = din("ssdnwT", [DEPTH, 128, 8])
    sublnT_in = din("sublnT", [DEPTH, 128, 1])
    w_in = din("w_in", [DEPTH, D, IN_W])
    ssdcw_in = din("ssdcw", [DEPTH, 128, 16 * 3])
    ssdcb_in = din("ssdcb", [DEPTH, 128, 16])
    alog_in = din("alog", [DEPTH, 32])
    dtb_in = din("dtb", [DEPTH, 32])
    dskip_in = din("dskip", [DEPTH, 16])
    lam_in = din("lamv", [DEPTH, 256])
    w_brs = din("w_br_ssd", [DEPTH, D, D])
    w_bra = din("w_br_att", [DEPTH, D, D])
    w_out = din("w_out", [DEPTH, D, D])
    w_up = din("w_up", [DEPTH, D, 2 * D_FF])
    ffncw_in = din("ffncw", [DEPTH, 128, 2 * NFF * 3])
    ffncb_in = din("ffncb", [DEPTH, 128, 2 * NFF])
    w_down = din("w_down", [DEPTH, D_FF, D])
    fnw_in = din("fnw", [D])
    cos_in = din("ropecos", [128, SEQ])
    sin_in = din("ropesin", [128, SEQ])
    y_out = nc.dram_tensor("y", [SEQ, D], F32, kind="ExternalOutput").ap()
    xres = nc.dram_tensor("xres", [NTOK, D], F32).ap()
    dbg_out = None
    if dbg is not None:
        dbg_out = nc.dram_tensor("dbg", [128, dbg[1]], F32, kind="ExternalOutput").ap()

    es = ExitStack()
    with es:
        P = Prog(nc, es)
        CAP = 206 * 1024
        arena_t = es.enter_context(nc.sbuf_tensor("arena", [128, CAP // 4], F32))
        psum = es.enter_context(nc.psum_tensor("psum", [128, 4096], F32))
        A = Arena(arena_t, CAP, P)

        def PS(b, n=512, off=0):
            return psum[:, b * 512 + off: b * 512 + off + n]

        def PSK(b):
            return ("ps", b)

        def MM(out, lhsT, rhs, start, stop, r, w, skip=False):
            if skip:
                P.op("pe", lambda e: e.matmul(out, lhsT=lhsT, rhs=rhs, start=start, stop=stop, skip_group_check=True), r, w)
            else:
                P.op("pe", lambda e: e.matmul(out, lhsT=lhsT, rhs=rhs, start=start, stop=stop), r, w)

        def ACT(out, in_, func, r, w, **kw):
            P.op("act", lambda e: e.activation(out=out, in_=in_, func=func, **kw), r, w)

        def TT(eng, out, in0, in1, op, r, w):
            P.op(eng, lambda e: e.tensor_tensor(out=out, in0=in0, in1=in1, op=op), r, w)

        def TS(eng, out, in0, s1, s2, op0, op1, r, w):
            if op1 is None:
                P.op(eng, lambda e: e.tensor_scalar(out=out, in0=in0, scalar1=s1, scalar2=None, op0=op0), r, w)
            else:
                P.op(eng, lambda e: e.tensor_scalar(out=out, in0=in0, scalar1=s1, scalar2=s2, op0=op0, op1=op1), r, w)

        def STT(eng, out, in0, scalar, in1, op0, op1, r, w):
            P.op(eng, lambda e: e.scalar_tensor_tensor(out=out, in0=in0, scalar=scalar, in1=in1, op0=op0, op1=op1), r, w)

        def CP(eng, out, in_, r, w):
            P.op(eng, lambda e: e.tensor_copy(out=out, in_=in_), r, w)

        def MSET(eng, ap, val, w):
            P.op(eng, lambda e: e.memset(ap, val), (), w)

        def RECIP(out, in_, r, w):
            P.op("dve", lambda e: e.reciprocal(out=out, in_=in_), r, w)

        def TR(out, in_, r, w):
            P.op("pe", lambda e: e.transpose(out=out, in_=in_, identity=ident), list(r) + [k_ident], w)

        def WLOAD(dst, src, r, w):
            P.dma("pool", dst, src, r, w)

        def wsrc(wt, l, c0, n):
            return wt[l].rearrange("(kc p) n -> p kc n", p=128)[:, :, c0:c0 + n]

        BASE = 27 * 1024
        SZ = 36864
        R_P = Region(A, 0, BASE)
        R0 = Region(A, BASE, BASE + SZ)
        R1 = Region(A, BASE + SZ, BASE + 2 * SZ)
        R2 = Region(A, BASE + 2 * SZ, BASE + 3 * SZ)
        R3 = Region(A, BASE + 3 * SZ, CAP)
        R23 = Region(A, BASE + 2 * SZ, CAP)

        ident, k_ident = R_P.alloc("ident", [128, 128], BF16)
        identf, k_identf = R_P.alloc("identf", [128, 128], F32)
        tri, k_tri = R_P.alloc("tri", [128, 4, 128], F32)
        ones, k_ones = R_P.alloc("ones", [128, 128], F32)
        MSET("pool", identf, 0.0, [k_identf])
        P.op("pool", lambda e: e.affine_select(out=identf, in_=identf, pattern=[[-1, 128]], compare_op=ALU.not_equal,
                                               fill=1.0, base=0, channel_multiplier=1), [k_identf], [k_identf])
        CP("dve", ident, identf, [k_identf], [k_ident])
        MSET("pool", ones, 1.0, [k_ones])
        MSET("pool", tri, 1.0, [k_tri])
        P.op("pool", lambda e: e.affine_select(out=tri[:, 0, :], in_=tri[:, 0, :], pattern=[[1, 128]], compare_op=ALU.is_ge,
                                               fill=0.0, base=0, channel_multiplier=-1), [k_tri], [k_tri])
        P.op("pool", lambda e: e.affine_select(out=tri[:, 1, :], in_=tri[:, 1, :], pattern=[[-1, 128]], compare_op=ALU.is_ge,
                                               fill=0.0, base=0, channel_multiplier=1), [k_tri], [k_tri])
        P.op("pool", lambda e: e.affine_select(out=tri[:, 2, :], in_=tri[:, 2, :], pattern=[[-1, 128]], compare_op=ALU.is_gt,
                                               fill=0.0, base=0, channel_multiplier=1), [k_tri], [k_tri])
        P.op("pool", lambda e: e.affine_select(out=tri[:, 3, :], in_=tri[:, 3, :], pattern=[[1, 128]], compare_op=ALU.is_gt,
                                               fill=0.0, base=0, channel_multiplier=-1), [k_tri], [k_tri])

        permT, k_perm = R_P.alloc("permT", [128, 128], BF16)
        iv = ident.rearrange("p (a hf f) -> p a hf f", hf=2, f=16)
        pv = permT.rearrange("p (a hf f) -> p a hf f", hf=2, f=16)
        CP("pool", pv[:, :, 0, :], iv[:, :, 1, :], [k_ident], [(k_perm, 0)])
        CP("pool", pv[:, :, 1, :], iv[:, :, 0, :], [k_ident], [(k_perm, 1)])
        cT, k_cT = R_P.alloc("cT", [128, 8, 2], F32)
        scT, k_scT = R_P.alloc("scT", [128, 8, 2], BF16)
        screp, k_screp = R_P.alloc("screp", [128, 8, 2, 128], BF16)
        P.dma("sp", cT, cT_in.rearrange("p (k w) -> p k w", w=2), (), [k_cT])
        ACT(scT, cT, AF.Silu, [k_cT], [k_scT])
        CP("dve", screp, scT.unsqueeze(3).to_broadcast([128, 8, 2, 128]), [k_scT], [k_screp])
        p_mark = (R_P.p, len(R_P.keys))

        dbgbuf = es.enter_context(nc.sbuf_tensor("dbgbuf", [128, 2, 128], F32)) if dbg is not None else None

        def dump(ap, ncols, rkeys, reg=None):
            for i, c0 in enumerate(range(0, ncols, 128)):
                n = min(128, ncols - c0)
                CP("dve", dbgbuf[:, i % 2, 0:n], ap[:, c0:c0 + n], rkeys, [("dbgbuf", i % 2)])
                P.dma("sp", dbg_out[:, c0:c0 + n], dbgbuf[:, i % 2, 0:n], [("dbgbuf", i % 2)], [("dbg", i)], is_output=True)

        def want(stage):
            return dbg is not None and dbg[0] == stage

        def norm_front(xt, kx, n_feat, bufs, i):
            junk, kj, ss, kss, xn, kxn, ev, kev = bufs
            nkc = n_feat // 128
            kxl = list(kx) if isinstance(kx, list) else [kx]
            ACT(junk[:, i % 2, 0:n_feat], xt, AF.Square, kxl, [(kj, i % 2), (kss, i % 2, 0)], accum_out=ss[:, i % 2, 0:1])
            ACT(ss[:, i % 2, 1:2], ss[:, i % 2, 0:1], AF.Ln, [(kss, i % 2, 0)], [(kss, i % 2, 1)], scale=1.0 / n_feat, bias=EPS)
            ACT(ss[:, i % 2, 2:3], ss[:, i % 2, 1:2], AF.Exp, [(kss, i % 2, 1)], [(kss, i % 2, 2)], scale=-0.5)
            TS("dve", xn[:, i % 2, 0:n_feat], xt, ss[:, i % 2, 2:3], None, ALU.mult, None, kxl + [(kss, i % 2, 2)], [(kxn, i % 2)])
            b = 6 + (i % 2)
            pt = PS(b).bitcast(BF16)
            for kc in range(nkc):
                TR(pt[:, kc * 128:(kc + 1) * 128], xn[:, i % 2, kc * 128:(kc + 1) * 128], [(kxn, i % 2)], [PSK(b)])

        def norm_back(t, scale_ap, bias_ap, extra_r, dstT, kdst, bufs, i):
            junk, kj, ss, kss, xn, kxn, ev, kev = bufs
            b = 6 + (i % 2)
            pt = PS(b).bitcast(BF16).rearrange("p (k t) -> p k t", k=8)
            dst = dstT[:, :, t * 128:(t + 1) * 128]
            sc = scale_ap.unsqueeze(2).to_broadcast([128, 8, 128])
            if bias_ap is None:
                TT("dve", dst, pt, sc, ALU.mult, [PSK(b)] + list(extra_r), [(kdst, t)])
            else:
                TT("dve", ev[:, i % 2], pt, sc, ALU.mult, [PSK(b)] + list(extra_r), [(kev, i % 2)])
                TT("dve", dst, ev[:, i % 2], bias_ap.unsqueeze(2).to_broadcast([128, 8, 128]), ALU.add, [(kev, i % 2)] + list(extra_r), [(kdst, t)])

        def norm_bufs(reg):
            junk, kj = reg.alloc("junk", [128, 2, D], BF16)
            ss, kss = reg.alloc("ss", [128, 2, 4], F32)
            xn, kxn = reg.alloc("xn", [128, 2, D], BF16)
            ev, kev = reg.alloc("nev", [128, 2, 8, 128], F32)
            return (junk, kj, ss, kss, xn, kxn, ev, kev)

        for l in range(DEPTH):
            with_ctx = l < DEPTH - 1
            T0 = 0 if with_ctx else 2
            tiles_q = list(range(T0, NT))
            blocks_q = BLOCKS[(0 if with_ctx else 1):]
            lam_init = 0.8 - 0.6 * math.exp(-0.3 * l)

            def xsrc(t, l=l):
                if l == 0:
                    return ctx_in[t * 128:(t + 1) * 128, :] if t < 2 else x_in[(t - 2) * 128:(t - 1) * 128, :]
                return xres[t * 128:(t + 1) * 128, :]

            for k in R_P.keys[p_mark[1]:]:
                A.free(k)
            del R_P.keys[p_mark[1]:]
            R_P.p = p_mark[0]

            bmodT, k_bmodT = R_P.alloc("bmodT", [128, 48], F32)
            modT, k_modT = R_P.alloc("modT", [128, 48, 2], F32)
            scl1, k_scl1 = R_P.alloc("scl1", [128, 8, 2], F32)
            scl2, k_scl2 = R_P.alloc("scl2", [128, 8, 2], F32)
            n1w, k_n1w = R_P.alloc("n1w", [128, 8], F32)
            n2w, k_n2w = R_P.alloc("n2w", [128, 8], F32)
            g1bc, k_g1bc = R_P.alloc("g1bc", [128, 2, D], F32)
            g2bc, k_g2bc = R_P.alloc("g2bc", [128, 2, D], F32)
            P.dma("sp", bmodT, bmodT_in[l], (), [k_bmodT])
            P.dma("sp", n1w, n1wT_in[l], (), [k_n1w])
            P.dma("sp", n2w, n2wT_in[l], (), [k_n2w])
            bmrow, k_bmrow = R3.alloc("bmrow", [128, 2, D], F32)
            P.dma("sp", bmrow[:, 0, :], b_mod[l, 2 * D:3 * D].partition_broadcast(128), (), [(k_bmrow, 0)])
            P.dma("sp", bmrow[:, 1, :], b_mod[l, 5 * D:6 * D].partition_broadcast(128), (), [(k_bmrow, 1)])
            wm = [R3.alloc("wmod%d" % i, [128, 8, 512], BF16) for i in range(2)]
            hT, k_hT = R0.alloc("hT", [128, 8, NTOK], BF16)
            xts = [R3.alloc("xt%d" % i, [128, D], F32) for i in range(3)]
            nbufs = norm_bufs(R3)

            def mod_chunk(cb):
                wt, kw = wm[cb % 2]
                WLOAD(wt, wsrc(w_mod, l, cb * 512, 512), (), [kw])
                which = cb // 2
                if which in (2, 5):
                    gb, kg = (g1bc, k_g1bc) if which == 2 else (g2bc, k_g2bc)
                    hf = cb % 2
                    for w_ in range(2):
                        b = (cb * 2 + w_) % 4
                        for kc in range(8):
                            MM(PS(b), screp[:, kc, w_, :], wt[:, kc, :], kc == 0, kc == 7, [k_screp, kw], [PSK(b)])
                        TT("dve", gb[:, w_, hf * 512:(hf + 1) * 512], PS(b), bmrow[:, 0 if which == 2 else 1, hf * 512:(hf + 1) * 512],
                           ALU.add, [PSK(b), (k_bmrow, 0 if which == 2 else 1)], [(kg, w_, hf)])
                else:
                    for j4 in range(4):
                        j = cb * 4 + j4
                        b = 4 + (j % 2)
                        for kc in range(8):
                            MM(PS(b, 2), wt[:, kc, j4 * 128:(j4 + 1) * 128], scT[:, kc, :], kc == 0, kc == 7, [k_scT, kw], [PSK(b)])
                        TS("dve", modT[:, j, :], PS(b, 2), bmodT[:, j:j + 1], None, ALU.add, None, [PSK(b), k_bmodT], [(k_modT, j)])

            for cb in range(4):
                mod_chunk(cb)
            STT("dve", scl1, modT[:, 8:16, :], 1.0, n1w.unsqueeze(2).to_broadcast([128, 8, 2]), ALU.add, ALU.mult,
                [(k_modT, j) for j in range(8, 16)] + [k_n1w], [k_scl1])

            def n1_front(t):
                xt, kx = xts[t % 3]
                P.dma("sp", xt, xsrc(t), [("xres", t)], [kx])
                norm_front(xt, kx, D, nbufs, t)

            rest = list(range(4, 12))
            n1_front(0)
            for t in range(NT):
                if t + 1 < NT:
                    n1_front(t + 1)
                w_ = 1 if t < 2 else 0
                norm_back(t, scl1[:, :, w_], modT[:, 0:8, w_], [k_scl1] + [(k_modT, j) for j in range(8)], hT, k_hT, nbufs, t)
                if t % 2 == 1 and rest:
                    mod_chunk(rest.pop(0))
            while rest:
                mod_chunk(rest.pop(0))
            STT("dve", scl2, modT[:, 32:40, :], 1.0, n2w.unsqueeze(2).to_broadcast([128, 8, 2]), ALU.add, ALU.mult,
                [(k_modT, j) for j in range(32, 40)] + [k_n2w], [k_scl2])
            k_g1 = [(k_g1bc, w_, hf) for w_ in range(2) for hf in range(2)]
            k_g2 = [(k_g2bc, w_, hf) for w_ in range(2) for hf in range(2)]
            R3.free_all()
            hT_all = [(k_hT, t) for t in range(NT)]
            if want("hT%d" % l):
                dump(hT.rearrange("p a b -> p (a b)"), 8 * NTOK, hT_all, R3)
                break

            ybuf, k_ybuf = R1.alloc("ybuf", [128, NT, D], BF16)
            S = R23
            ssdcw, k_ssdcw = S.alloc("ssdcw", [128, 16, 3], F32)
            ssdcb, k_ssdcb = S.alloc("ssdcb", [128, 16], F32)
            alog, k_alog = S.alloc("alog", [128, 32], F32)
            dtb, k_dtb = S.alloc("dtb", [128, 32], F32)
            dsk, k_dsk = S.alloc("dsk", [128, 16], F32)
            Aneg, k_Aneg = S.alloc("Aneg", [128, 32], F32)
            P.dma("sp", ssdcw, ssdcw_in[l].rearrange("p (c k) -> p c k", k=3), (), [k_ssdcw])
            P.dma("sp", ssdcb, ssdcb_in[l], (), [k_ssdcb])
            P.dma("sp", alog, alog_in[l].partition_broadcast(128), (), [k_alog])
            P.dma("sp", dtb, dtb_in[l].partition_broadcast(128), (), [k_dtb])
            P.dma("sp", dsk, dskip_in[l].partition_broadcast(128), (), [k_dsk])
            ACT(Aneg, alog, AF.Exp, [k_alog], [k_Aneg])
            TS("dve", Aneg, Aneg, -1.0, None, ALU.mult, None, [k_Aneg], [k_Aneg])
            wdt, k_wdt = S.alloc("wdt", [128, 8, 32], BF16)
            WLOAD(wdt, wsrc(w_in, l, OFF_DT, 32), (), [k_wdt])
            dt_all, k_dt = S.alloc("dt_all", [128, NT, 32], F32)
            a_all, k_a = S.alloc("a_all", [128, NT, 32], F32)
            eacs, k_eacs = S.alloc("eacs", [128, NT, 32], F32)
            edte, k_edte = S.alloc("edte", [128, NT, 32], F32)
            etot, k_etot = S.alloc("etot", [128, NT, 32], F32)
            dtdte, k_dtdte = S.alloc("dtdte", [128, NT, 32], F32)
            RT = Region(A, CAP - 2 * 2304, CAP)
            tmpa, k_tmpa = RT.alloc("tmpa", [128, NT, 32], F32)
            tmpb, k_tmpb = RT.alloc("tmpb", [128, NT, 32], F32)

            def pview(b0):
                return psum[:, b0 * 512: b0 * 512 + NT * 32].rearrange("p (t c) -> p t c", c=32)

            def pkeys(b0):
                return [PSK(b0), PSK(b0 + 1)]

            for t in range(NT):
                for kc in range(8):
                    MM(psum[:, t * 32:(t + 1) * 32], hT[:, kc, t * 128:(t + 1) * 128], wdt[:, kc, :], kc == 0, kc == 7,
                       [(k_hT, t), k_wdt], [PSK(0 if t < 16 else 1)])
            TT("dve", tmpa, pview(0), dtb.unsqueeze(1).to_broadcast([128, NT, 32]), ALU.add, pkeys(0) + [k_dtb], [k_tmpa])
            ACT(tmpb, tmpa, AF.Exp, [k_tmpa], [k_tmpb])
            ACT(dt_all, tmpb, AF.Ln, [k_tmpb], [k_dt], bias=1.0)
            TT("dve", a_all, dt_all, Aneg.unsqueeze(1).to_broadcast([128, NT, 32]), ALU.mult, [k_dt, k_Aneg], [k_a])
            RT.free_all()
            for t in range(NT):
                bk = lambda b0: [PSK(b0 + (0 if t < 16 else 1))]
                for d_ in range(2):
                    sl = slice(t * 32 + d_ * 16, t * 32 + d_ * 16 + 16)
                    MM(psum[:, 1024 + sl.start:1024 + sl.stop], tri[:, d_, :], a_all[:, t, d_ * 16:(d_ + 1) * 16], True, True,
                       [k_tri, k_a], bk(2))
                    MM(psum[:, 2048 + sl.start:2048 + sl.stop], tri[:, 2 + d_, :], a_all[:, t, d_ * 16:(d_ + 1) * 16], True, True,
                       [k_tri, k_a], bk(4))
                MM(psum[:, t * 32:(t + 1) * 32], ones, a_all[:, t, :], True, True, [k_ones, k_a], bk(0))
            ACT(eacs, pview(2), AF.Exp, pkeys(2), [k_eacs])
            ACT(edte, pview(4), AF.Exp, pkeys(4), [k_edte])
            ACT(etot, pview(0), AF.Exp, pkeys(0), [k_etot])
            TT("dve", dtdte, dt_all, edte, ALU.mult, [k_dt, k_edte], [k_dtdte])

            if want("dt%d" % l):
                tmp, k_tmp = S.alloc("dd", [128, 5 * NT * 32], F32)
                for i_, (ap_, k_) in enumerate(((dt_all, k_dt), (a_all, k_a), (eacs, k_eacs), (edte, k_edte), (etot, k_etot))):
                    CP("dve", tmp[:, i_ * NT * 32:(i_ + 1) * NT * 32], ap_.rearrange("p a b -> p (a b)"), [k_], [k_tmp])
                P.dma("sp", dbg_out[:, 0:5 * NT * 32], tmp, [k_tmp], ["dbg"], is_output=True)
                break
            wgs = [S.alloc("wg%d" % i, [128, 8, 512], BF16) for i in range(2)]

            def load_wg(g):
                wg, kwg = wgs[g % 2]
                WLOAD(wg[:, :, 0:256], wsrc(w_in, l, OFF_XBC + g * 256, 256), (), [(kwg, 0)])
                WLOAD(wg[:, :, 256:384], wsrc(w_in, l, OFF_XBC + 1024 + g * 128, 128), (), [(kwg, 1)])
                WLOAD(wg[:, :, 384:512], wsrc(w_in, l, OFF_XBC + 1536 + g * 128, 128), (), [(kwg, 2)])

            load_wg(0)
            load_wg(1)
            xbcT, k_xbcT = S.alloc("xbcT", [128, 4, NTOK], BF16)
            xs_tok, k_xs = S.alloc("xs_tok", [128, NT, 256], BF16)
            B_tok, k_Bt = S.alloc("B_tok", [128, NT, 128], BF16)
            CBms = [S.alloc("CBm%d" % i, [128, 2, 128], BF16) for i in range(2)]
            rhsb, k_rhsb = S.alloc("rhsb", [128, 2, 4, 128], F32)
            expEs = [S.alloc("expE%d" % i, [128, 2, 4, 128], BF16) for i in range(2)]
            MTs = [S.alloc("MT%d" % i, [128, 2, 4, 128], BF16) for i in range(2)]
            xds = [S.alloc("xd%d" % i, [128, 2, 4, 64], BF16) for i in range(2)]
            xdds = [S.alloc("xdd%d" % i, [128, 4, 64], BF16) for i in range(4)]
            t1, k_t1 = S.alloc("t1", [128, 4, 64], F32)
            t2, k_t2 = S.alloc("t2", [128, 4, 64], F32)
            t1b, k_t1b = S.alloc("t1b", [128, 4, 64], F32)
            t3s = [S.alloc("t3%d" % i, [128, 4, 64], F32) for i in range(2)]
            Hrun, k_Hrun = S.alloc("Hrun", [128, 2, 2, 256], F32)
            SX = Region(A, S.p, CAP)

            def h4(ap):
                return ap.rearrange("p (h d) -> p h d", h=4)

            for g in range(4):
                wg, kwg = wgs[g % 2]
                Upad, k_U = SX.alloc("Upad", [128, NTOK + 4], F32)
                acc, k_acc = SX.alloc("acc", [128, NTOK], F32)
                MSET("pool", Upad[:, 0:1], 0.0, [(k_U, "p0")])
                MSET("pool", Upad[:, 257:259], 0.0, [(k_U, "p1")])
                MSET("pool", Upad[:, 2307:2308], 0.0, [(k_U, "p2")])
                for cc in range(4):
                    cidx = (g * 2 + cc) if cc < 2 else (8 + g if cc == 2 else 12 + g)
                    kwp = (kwg, 0) if cc < 2 else (kwg, cc - 1)
                    for bi, (t0, n) in enumerate(BLOCKS):
                        b = (cc * 5 + bi) % 6
                        for kc in range(8):
                            MM(PS(b, n), wg[:, kc, cc * 128:(cc + 1) * 128], hT[:, kc, t0:t0 + n], kc == 0, kc == 7,
                               [kwp] + [(k_hT, t) for t in range(t0 // 128, (t0 + n) // 128)], [PSK(b)])
                        off = t0 + 1 if t0 == 0 else t0 + 3
                        ACT(Upad[:, off:off + n], PS(b, n), AF.Copy, [PSK(b)], [(k_U, bi)])
                    ukeys = [(k_U, bi) for bi in range(5)] + [(k_U, "p0"), (k_U, "p1"), (k_U, "p2")]
                    for (o0, n, u0) in ((0, 256, 0), (256, 2048, 258)):
                        ka = (k_acc, o0)
                        TS("dve", acc[:, o0:o0 + n], Upad[:, u0:u0 + n], ssdcw[:, cidx, 0:1], None, ALU.mult, None,
                           ukeys + [k_ssdcw], [ka])
                        STT("dve", acc[:, o0:o0 + n], Upad[:, u0 + 1:u0 + 1 + n], ssdcw[:, cidx, 1:2], acc[:, o0:o0 + n], ALU.mult, ALU.add,
                            ukeys + [k_ssdcw, ka], [ka])
                        STT("dve", acc[:, o0:o0 + n], Upad[:, u0 + 2:u0 + 2 + n], ssdcw[:, cidx, 2:3], acc[:, o0:o0 + n], ALU.mult, ALU.add,
                            ukeys + [k_ssdcw, ka], [ka])
                        ACT(xbcT[:, cc, o0:o0 + n], acc[:, o0:o0 + n], AF.Silu, [ka, k_ssdcb], [(k_xbcT, cc, o0)], bias=ssdcb[:, cidx:cidx + 1])
                SX.free_all()
                if want("xbc%d" % l):
                    dump(xbcT.rearrange("p a b -> p (a b)"), 4 * NTOK, [(k_xbcT, cc, o0) for cc in range(4) for o0 in (0, 256)], SX)
                    break
                if g + 2 < 4 and 'a' not in XF:
                    load_wg(g + 2)
                xk = lambda cc, t: (k_xbcT, cc, 0 if t < 2 else 256)
                for t in range(NT):
                    b = 6 + (t % 2)
                    pt = PS(b).bitcast(BF16)
                    for cc in range(3):
                        TR(pt[:, cc * 128:(cc + 1) * 128], xbcT[:, cc, t * 128:(t + 1) * 128], [xk(cc, t)], [PSK(b)])
                    if 'b' not in XF:
                        CP("dve", xs_tok[:, t, :], pt[:, 0:256], [PSK(b)], [(k_xs, t)])
                    if 'c' not in XF:
                        ACT(B_tok[:, t, :], pt[:, 256:384], AF.Copy, [PSK(b)], [(k_Bt, t)])
                if want("tok%d" % l):
                    dump(xs_tok.rearrange("p a b -> p (a b)"), NT * 256, [(k_xs, t) for t in range(NT)])
                    break
                Hin, k_Hin = SX.alloc("Hin", [128, 2, NT, 256], BF16)
                orders = [list(range(NT)), [1, 0] + list(range(NT - 1, 1, -1))]
                MSET("pool", Hrun[:, :, 0, :], 0.0, [(k_Hrun, d_, 0, h_) for d_ in range(2) for h_ in range(4)])

                def emit_xdd(i, d_):
                    c = orders[d_][i]
                    hs = slice(d_ * 16 + g * 4, d_ * 16 + g * 4 + 4)
                    xdd, k_xdd = xdds[(2 * i + d_) % 4]
                    TT("pool", xdd, h4(xs_tok[:, c, :]), dtdte[:, c, hs].unsqueeze(2).to_broadcast([128, 4, 64]), ALU.mult,
                       [(k_xs, c), k_dtdte], [k_xdd])

                for d_ in range(2):
                    emit_xdd(0, d_)
                for i in range(NT):
                    for d_ in range(2):
                        if i + 1 < NT - 1:
                            emit_xdd(i + 1, d_)
                    for d_ in range(2):
                        c = orders[d_][i]
                        pp = i % 2
                        ACT(Hin[:, d_, c, :], Hrun[:, d_, pp, :], AF.Copy, [(k_Hrun, d_, pp, h_) for h_ in range(4)], [(k_Hin, d_, c)])
                        if i == NT - 1:
                            continue
                        xdd, k_xdd = xdds[(2 * i + d_) % 4]
                        b = 6 + ((2 * i + d_) % 2)
                        MM(PS(b, 256), B_tok[:, c, :], xdd.rearrange("p h d -> p (h d)"), True, True, [(k_Bt, c), k_xdd], [PSK(b)])
                        for h_ in range(4):
                            hh = d_ * 16 + g * 4 + h_
                            STT("dve", Hrun[:, d_, 1 - pp, h_ * 64:(h_ + 1) * 64], Hrun[:, d_, pp, h_ * 64:(h_ + 1) * 64], etot[:, c, hh:hh + 1],
                                PS(b, 64, off=h_ * 64), ALU.mult, ALU.add, [(k_Hrun, d_, pp, h_), k_etot, PSK(b)], [(k_Hrun, d_, 1 - pp, h_)])
                dt_dh = lambda c: dt_all[:, c, :].rearrange("p (d h) -> p d h", d=2)[:, :, g * 4:g * 4 + 4]
                a_dh = lambda c: a_all[:, c, :].rearrange("p (d h) -> p d h", d=2)[:, :, g * 4:g * 4 + 4]

                def prepA1(c):
                    csl = slice(c * 128, (c + 1) * 128)
                    bA = c % 2
                    cbm, k_cbm = CBms[c % 2]
                    MM(PS(bA, 128), xbcT[:, 2, csl], xbcT[:, 3, csl], True, True, [xk(2, c), xk(3, c)], [PSK(bA)])
                    TT("dve", cbm, PS(bA, 128).unsqueeze(1).to_broadcast([128, 2, 128]), tri[:, 0:2, :], ALU.mult, [PSK(bA), k_tri], [k_cbm])
                    TT("pool", rhsb[:, 0], a_dh(c)[:, 0, :].unsqueeze(2).to_broadcast([128, 4, 128]),
                       tri[:, 0, :].unsqueeze(1).to_broadcast([128, 4, 128]), ALU.mult, [k_a, k_tri], [(k_rhsb, 0)])
                    TT("dve", rhsb[:, 1], a_dh(c)[:, 1, :].unsqueeze(2).to_broadcast([128, 4, 128]),
                       tri[:, 1, :].unsqueeze(1).to_broadcast([128, 4, 128]), ALU.mult, [k_a, k_tri], [(k_rhsb, 1)])

                def prepA2(c):
                    ee, k_ee = expEs[c % 2]
                    for d_ in range(2):
                        MM(PS(2 + d_), tri[:, 2 + d_, :], rhsb[:, d_].rearrange("p h l -> p (h l)"), True, True, [k_tri, (k_rhsb, d_)], [PSK(2 + d_)])
                        ACT(ee[:, d_].rearrange("p h l -> p (h l)"), PS(2 + d_), AF.Exp, [PSK(2 + d_)], [(k_ee, d_)])

                def prepB(c):
                    mt, k_mt = MTs[c % 2]
                    xd, k_xd = xds[c % 2]
                    cbm, k_cbm = CBms[c % 2]
                    ee, k_ee = expEs[c % 2]
                    TT("dve", mt, ee, cbm.unsqueeze(2).to_broadcast([128, 2, 4, 128]), ALU.mult, [(k_ee, 0), (k_ee, 1), k_cbm], [k_mt])
                    TT("pool", xd, h4(xs_tok[:, c, :]).unsqueeze(1).to_broadcast([128, 2, 4, 64]),
                       dt_dh(c).unsqueeze(3).to_broadcast([128, 2, 4, 64]), ALU.mult, [(k_xs, c), k_dt], [k_xd])
                    TT("pool", t3s[c % 2][0], h4(xs_tok[:, c, :]), dsk[:, g * 4:g * 4 + 4].unsqueeze(2).to_broadcast([128, 4, 64]), ALU.mult,
                       [(k_xs, c), k_dsk], [t3s[c % 2][1]])

                def finish_pe(c):
                    csl = slice(c * 128, (c + 1) * 128)
                    bY = 4 + (c % 2)
                    bO = 6 + (c % 2)
                    mt, k_mt = MTs[c % 2]
                    xd, k_xd = xds[c % 2]
                    for d_ in range(2):
                        for h_ in range(4):
                            MM(PS(bY, 64, off=h_ * 64), mt[:, d_, h_, :], xd[:, d_, h_, :], d_ == 0 and h_ == 0, d_ == 1, [k_mt, k_xd], [PSK(bY)], skip=True)
                    MM(PS(bY, 256, off=256), xbcT[:, 3, csl], Hin[:, 0, c, :], False, True, [xk(3, c), (k_Hin, 0, c)], [PSK(bY)], skip=True)
                    MM(PS(bO, 256), xbcT[:, 3, csl], Hin[:, 1, c, :], True, True, [xk(3, c), (k_Hin, 1, c)], [PSK(bO)])

                def finish_dve(c):
                    bY = 4 + (c % 2)
                    bO = 6 + (c % 2)
                    t3, k_t3 = t3s[c % 2]
                    TT("dve", t1, h4(PS(bY, 256, off=256)), eacs[:, c, g * 4:g * 4 + 4].unsqueeze(2).to_broadcast([128, 4, 64]), ALU.mult,
                       [PSK(bY), k_eacs], [k_t1])
                    TT("dve", t2, h4(PS(bO, 256)), eacs[:, c, 16 + g * 4:16 + g * 4 + 4].unsqueeze(2).to_broadcast([128, 4, 64]), ALU.mult,
                       [PSK(bO), k_eacs], [k_t2])
                    TT("pool", t2, t2, t3, ALU.add, [k_t2, k_t3], [k_t2])
                    TT("dve", t1b, h4(PS(bY, 256)), t1, ALU.add, [PSK(bY), k_t1], [k_t1b])

                def finish_out(c):
                    TT("dve", h4(ybuf[:, c, g * 256:(g + 1) * 256]), t1b, t2, ALU.add, [k_t1b, k_t2], [(k_ybuf, c, g)])

                oc = tiles_q
                no = len(oc)
                prepA1(oc[0])
                prepA2(oc[0])
                prepA1(oc[1])
                prepA2(oc[1])
                prepB(oc[0])
                for ci in range(no):
                    if ci + 2 < no:
                        prepA1(oc[ci + 2])
                    finish_pe(oc[ci])
                    if ci + 2 < no:
                        prepA2(oc[ci + 2])
                    if ci + 1 < no:
                        prepB(oc[ci + 1])
                    if ci > 0:
                        finish_out(oc[ci - 1])
                    finish_dve(oc[ci])
                finish_out(oc[no - 1])
                SX.free_all()
            if want("xbc%d" % l) or want("hin%d" % l) or want("tok%d" % l):
                break
            S.free_all()
            if want("ybuf%d" % l):
                dump(ybuf.rearrange("p a b -> p (a b)"), NT * D, [(k_ybuf, c, g) for c in range(NT) for g in range(4)], R23)
                break

            ysT, k_ysT = R2.alloc("ysT", [128, 8, NTOK], BF16)
            wz, k_wz = R3.alloc("wz", [128, 8, D], BF16)
            ssdnw, k_ssdnw = R3.alloc("ssdnw", [128, 8], F32)
            sz, k_sz = R3.alloc("sz", [128, 2, D], F32)
            yz, k_yz = R3.alloc("yz", [128, 2, D], F32)
            nbufs = norm_bufs(R3)
            P.dma("sp", ssdnw, ssdnwT_in[l], (), [k_ssdnw])
            for hf in range(2):
                WLOAD(wz[:, :, hf * 512:(hf + 1) * 512], wsrc(w_in, l, OFF_Z + hf * 512, 512), (), [(k_wz, hf)])
            def ro_front(i):
                t = tiles_q[i]
                for hf in range(2):
                    b = hf + 2 * (i % 2)
                    for kc in range(8):
                        MM(PS(b), hT[:, kc, t * 128:(t + 1) * 128], wz[:, kc, hf * 512:(hf + 1) * 512], kc == 0, kc == 7,
                           [(k_hT, t), (k_wz, hf)], [PSK(b)])
                    ACT(sz[:, i % 2, hf * 512:(hf + 1) * 512], PS(b), AF.Silu, [PSK(b)], [(k_sz, i % 2, hf)])
                TT("dve", yz[:, i % 2, :], ybuf[:, t, :], sz[:, i % 2, :], ALU.mult,
                   [(k_ybuf, t, g) for g in range(4)] + [(k_sz, i % 2, 0), (k_sz, i % 2, 1)], [(k_yz, i % 2)])
                norm_front(yz[:, i % 2, :], (k_yz, i % 2), D, nbufs, i)

            ro_front(0)
            for i, t in enumerate(tiles_q):
                if i + 1 < len(tiles_q):
                    ro_front(i + 1)
                norm_back(t, ssdnw, None, [k_ssdnw], ysT, k_ysT, nbufs, i)
            R3.free_all()
            R1.free_all()
            ysT_all = [(k_ysT, t) for t in tiles_q]
            if want("ysT%d" % l):
                dump(ysT.rearrange("p a b -> p (a b)"), 8 * NTOK, ysT_all, R3)
                break

            yaT, k_yaT = R1.alloc("yaT", [128, 8, NTOK], BF16)
            T_ = R3
            cosT, k_cos = T_.alloc("cosT", [128, SEQ], F32)
            sinT, k_sin = T_.alloc("sinT", [128, SEQ], F32)
            P.dma("sp", cosT, cos_in, (), [k_cos])
            P.dma("sp", sinT, sin_in, (), [k_sin])
            subln, k_subln = T_.alloc("subln", [128, 1], F32)
            lamv, k_lamv = T_.alloc("lamv", [128, 4, 64], F32)
            lprod, k_lprod = T_.alloc("lprod", [128, 2, 64], F32)
            lsm, k_lsm = T_.alloc("lsm", [128, 4], F32)
            P.dma("sp", subln, sublnT_in[l], (), [k_subln])
            P.dma("sp", lamv, lam_in[l].partition_broadcast(128).rearrange("p (a b) -> p a b", a=4), (), [k_lamv])
            TS("dve", subln, subln, 1.0 - lam_init, None, ALU.mult, None, [k_subln], [k_subln])
            TT("dve", lprod[:, 0, :], lamv[:, 0, :], lamv[:, 1, :], ALU.mult, [k_lamv], [(k_lprod, 0)])
            TT("dve", lprod[:, 1, :], lamv[:, 2, :], lamv[:, 3, :], ALU.mult, [k_lamv], [(k_lprod, 1)])
            P.op("dve", lambda e: e.tensor_reduce(out=lsm[:, 0:2], in_=lprod, axis=AX.X, op=ALU.add), [(k_lprod, 0), (k_lprod, 1)], [(k_lsm, 0)])
            ACT(lsm[:, 2:4], lsm[:, 0:2], AF.Exp, [(k_lsm, 0)], [(k_lsm, 1)])
            TT("dve", lsm[:, 0:1], lsm[:, 3:4], lsm[:, 2:3], ALU.subtract, [(k_lsm, 1)], [(k_lsm, 2)])
            TS("dve", lsm[:, 1:2], lsm[:, 0:1], -lam_init, None, ALU.add, None, [(k_lsm, 2)], [(k_lsm, 3)])
            neglam = lsm[:, 1:2]
            k_neglam = (k_lsm, 3)
            wsl = [T_.alloc("watt%d" % i, [128, 3, 8, 128], BF16) for i in range(2)]
            qraws = [T_.alloc("qraw%d" % i, [128, 512], BF16) for i in range(2)]
            qT, k_qT = T_.alloc("qT", [128, 2, NTOK], BF16)
            MSET("pool", qT[64:128, 0, :], 0.0, [(k_qT, "z0")])
            MSET("pool", qT[0:64, 1, :], 0.0, [(k_qT, "z1")])
            kT, k_kT = T_.alloc("kT", [128, NTOK], BF16)
            v_aug, k_v = T_.alloc("v_aug", [128, NT, 132], BF16)
            ropa = [T_.alloc("ropa%d" % i, [128, 512], F32) for i in range(2)]
            ropb = [T_.alloc("ropb%d" % i, [128, 512], F32) for i in range(2)]
            pTs = [T_.alloc("pT%d" % i, [128, 512], BF16) for i in range(4)]
            o4, k_o4 = T_.alloc("o4", [128, 4, 128], F32)
            on4, k_on4 = T_.alloc("on4", [128, 4, 128], BF16)
            rs4, k_rs4 = T_.alloc("rs4", [128, 4, 8], F32)
            att = {"sb": -1, "sbs": {}}
            MSET("pool", v_aug[:, :, 128:129], 1.0, [(k_v, "ones")])

            def load_watt(h):
                wt, kwt = wsl[h % 2]
                WLOAD(wt[:, 0], wsrc(w_in, l, OFF_Q + h * 128, 128), (), [(kwt, 0)])
                WLOAD(wt[:, 1], wsrc(w_in, l, OFF_K + h * 128, 128), (), [(kwt, 1)])
                WLOAD(wt[:, 2], wsrc(w_in, l, OFF_V + h * 128, 128), (), [(kwt, 2)])

            load_watt(0)
            load_watt(1)
            cnt = {"rope": 0, "post": 0}
            pend = []

            def flush_pend():
                if pend:
                    if pend[1] == 2:
                        post2(pend[0])
                    post3(pend[0], pend[2])
                    pend.clear()

            for h in range(8):
                wt, kwt = wsl[h % 2]
                for (wi, dst, kdst, isq) in ((0, None, None, True), (1, kT, k_kT, False)):
                    for bi, (t0, n) in enumerate(BLOCKS):
                        hk = [(k_hT, t) for t in range(t0 // 128, (t0 + n) // 128)]
                        if bi == 0:
                            if isq and not with_ctx:
                                continue
                            for kc in range(8):
                                MM(PS(0, n), wt[:, wi, kc, :], hT[:, kc, t0:t0 + n], kc == 0, kc == 7, [(kwt, wi)] + hk, [PSK(0)])
                            if isq:
                                ACT(qT[0:64, 0, t0:t0 + n], PS(0, n)[0:64, :], AF.Copy, [PSK(0), (k_qT, "z0")], [(k_qT, 0, bi)])
                                ACT(qT[64:128, 1, t0:t0 + n], PS(0, n)[64:128, :], AF.Copy, [PSK(0), (k_qT, "z1")], [(k_qT, 1, bi)])
                            else:
                                ACT(dst[:, t0:t0 + n], PS(0, n), AF.Copy, [PSK(0)], [(kdst, bi)])
                            continue
                        r_i = cnt["rope"] % 2
                        cnt["rope"] += 1
                        bA, bB = 2 * r_i, 2 * r_i + 1
                        for kc in range(8):
                            MM(PS(bA), wt[:, wi, kc, :], hT[:, kc, t0:t0 + n], kc == 0, kc == 7, [(kwt, wi)] + hk, [PSK(bA)])
                        qr, k_qr = qraws[r_i]
                        ACT(qr, PS(bA), AF.Copy, [PSK(bA)], [k_qr])
                        MM(PS(bB), permT, qr, True, True, [(k_perm, 0), (k_perm, 1), k_qr], [PSK(bB)])
                        ra, k_ra = ropa[r_i]
                        rb_, k_rb_ = ropb[r_i]
                        c0 = t0 - CTX
                        TT("dve", ra, PS(bA), cosT[:, c0:c0 + n], ALU.mult, [PSK(bA), k_cos], [k_ra])
                        TT("dve", rb_, PS(bB), sinT[:, c0:c0 + n], ALU.mult, [PSK(bB), k_sin], [k_rb_])
                        if isq:
                            TT("pool", qT[0:64, 0, t0:t0 + n], ra[0:64, :], rb_[0:64, :], ALU.add, [k_ra, k_rb_, (k_qT, "z0")], [(k_qT, 0, bi)])
                            TT("pool", qT[64:128, 1, t0:t0 + n], ra[64:128, :], rb_[64:128, :], ALU.add, [k_ra, k_rb_, (k_qT, "z1")], [(k_qT, 1, bi)])
                        else:
                            TT("pool", dst[:, t0:t0 + n], ra, rb_, ALU.add, [k_ra, k_rb_], [(kdst, bi)])
                for t4 in range(0, NT, 4):
                    nt4 = min(4, NT - t4)
                    b = 4 + ((t4 // 4) % 2)
                    for ti in range(nt4):
                        t = t4 + ti
                        for kc in range(8):
                            MM(PS(b, 128, off=ti * 128), hT[:, kc, t * 128:(t + 1) * 128], wt[:, 2, kc, :], kc == 0, kc == 7,
                               [(k_hT, t), (kwt, 2)], [PSK(b)])
                    CP("dve", v_aug[:, t4:t4 + nt4, 0:128], PS(b, nt4 * 128).rearrange("p (t e) -> p t e", e=128), [PSK(b)],
                       [(k_v, t) for t in range(t4, t4 + nt4)])
                flush_pend()
                if h + 2 < 8:
                    load_watt(h + 2)
                qblocks = ([(0, 256, [0, 1], [0])] if with_ctx else []) + [(256 + 512 * j, 512, list(range(NT)), [1 + j]) for j in range(4)]
                LOOK = 3

                def acc_of(nq, comp, j, n=129):
                    idx = comp * nq + j
                    return PS(4 + idx // 3, n, off=(idx % 3) * 132), PSK(4 + idx // 3), idx

                def emit_S(qb, i):
                    q0, qn, ktiles, qbi = qb
                    ki, comp = i // 2, i % 2
                    kt = ktiles[ki]
                    sb = att["sb"] = (att["sb"] + 1) % 4
                    att["sbs"][(q0, i)] = sb
                    MM(PS(sb, qn), kT[:, kt * 128:(kt + 1) * 128], qT[:, comp, q0:q0 + qn], True, True,
                       [(k_kT, 0 if kt < 2 else 1 + (kt - 2) // 4), (k_qT, "z%d" % (1 - comp))] + [(k_qT, comp, b_) for b_ in qbi], [PSK(sb)])

                def emit_EP(qb, i):
                    q0, qn, ktiles, qbi = qb
                    nq = qn // 128
                    ki, comp = i // 2, i % 2
                    kt = ktiles[ki]
                    sb = att["sbs"].pop((q0, i))
                    pT_, k_pT = pTs[sb]
                    ACT(pT_[:, 0:qn], PS(sb, qn), AF.Exp, [PSK(sb)], [k_pT], scale=ATT_SCALE)
                    for j in range(nq):
                        a_, ka_, idx = acc_of(nq, comp, j)
                        MM(a_, pT_[:, j * 128:(j + 1) * 128], v_aug[:, kt, 0:129], ki == 0 and idx % 3 == 0, ki == len(ktiles) - 1,
                           [k_pT, (k_v, kt), (k_v, "ones")], [ka_], skip=True)

                def post1(qb):
                    q0, qn, ktiles, qbi = qb
                    nq = qn // 128
                    A0 = [acc_of(nq, 0, j) for j in range(nq)]
                    A1 = [acc_of(nq, 1, j) for j in range(nq)]
                    J = range(nq)
                    for j in J:
                        RECIP(rs4[:, j, 0:1], A0[j][0][:, 128:129], [A0[j][1]], [(k_rs4, j, 0)])
                        RECIP(rs4[:, j, 1:2], A1[j][0][:, 128:129], [A1[j][1]], [(k_rs4, j, 1)])
                    for j in J:
                        TT("dve", rs4[:, j, 2:3], rs4[:, j, 1:2], neglam, ALU.mult, [(k_rs4, j, 1), k_neglam], [(k_rs4, j, 2)])
                    for j in J:
                        TS("dve", o4[:, j, :], A0[j][0][:, 0:128], rs4[:, j, 0:1], None, ALU.mult, None, [A0[j][1], (k_rs4, j, 0)], [(k_o4, j)])
                    for j in J:
                        STT("dve", o4[:, j, :], A1[j][0][:, 0:128], rs4[:, j, 2:3], o4[:, j, :], ALU.mult, ALU.add,
                            [A1[j][1], (k_rs4, j, 2), (k_o4, j)], [(k_o4, j)])

                def post2(qb):
                    J = range(qb[1] // 128)
                    for j in J:
                        ACT(on4[:, j, :], o4[:, j, :], AF.Square, [(k_o4, j)], [(k_on4, j), (k_rs4, j, 3)], accum_out=rs4[:, j, 3:4])
                    for j in J:
                        ACT(rs4[:, j, 4:5], rs4[:, j, 3:4], AF.Ln, [(k_rs4, j, 3)], [(k_rs4, j, 4)], scale=1.0 / 128, bias=EPS)
                    for j in J:
                        ACT(rs4[:, j, 5:6], rs4[:, j, 4:5], AF.Exp, [(k_rs4, j, 4)], [(k_rs4, j, 5)], scale=-0.5)
                    for j in J:
                        TS("dve", on4[:, j, :], o4[:, j, :], rs4[:, j, 5:6], None, ALU.mult, None, [(k_o4, j), (k_rs4, j, 5)], [(k_on4, j)])

                def post3(qb, h):
                    q0 = qb[0]
                    J = range(qb[1] // 128)
                    pt = PS(7).bitcast(BF16)
                    for j in J:
                        TR(pt[:, j * 128:(j + 1) * 128], on4[:, j, :], [(k_on4, j)], [PSK(7)])
                    for j in J:
                        tq = q0 // 128 + j
                        ACT(yaT[:, h, tq * 128:(tq + 1) * 128], pt[:, j * 128:(j + 1) * 128], AF.Identity, [PSK(7), k_subln],
                            [(k_yaT, tq, h)], scale=subln[:, 0:1])

                for bi_, qb in enumerate(qblocks):
                    nst = 2 * len(qb[2])
                    if bi_ == 0:
                        for i in range(min(LOOK, nst)):
                            emit_S(qb, i)
                    for i in range(nst):
                        if i + LOOK < nst:
                            emit_S(qb, i + LOOK)
                        emit_EP(qb, i)
                        if pend and pend[1] == 2 and i >= 3:
                            post2(pend[0])
                            pend[1] = 3
                        if pend and pend[1] == 3 and i >= 8:
                            post3(pend[0], pend[2])
                            pend.clear()
                    flush_pend()
                    if bi_ + 1 < len(qblocks):
                        nb = qblocks[bi_ + 1]
                        for i in range(min(LOOK, 2 * len(nb[2]))):
                            emit_S(nb, i)
                    post1(qb)
                    pend.extend([qb, 2, h])
            flush_pend()
            T_.free_all()
            yaT_all = [(k_yaT, t, h) for t in tiles_q for h in range(8)]
            if want("yaT%d" % l):
                dump(yaT.rearrange("p a b -> p (a b)"), 8 * NTOK, yaT_all)
                break

            mT, k_mT = R3.alloc("mT", [128, 8, NTOK], BF16)
            wms = [R3.alloc("wmrg%d" % i, [128, 4, 8, 128], BF16) for i in range(2)]
            sgs = [R3.alloc("sg%d" % i, [128, 2, 512], F32) for i in range(2)]
            tms = [R3.alloc("tm%d" % i, [128, 2, 512], F32) for i in range(2)]

            def load_wm(j):
                wt, kwt = wms[j % 2]
                WLOAD(wt[:, 0], wsrc(w_brs, l, j * 128, 128), (), [(kwt, 0)])
                WLOAD(wt[:, 1], wsrc(w_bra, l, j * 128, 128), (), [(kwt, 1)])
                WLOAD(wt[:, 2], wsrc(w_in, l, OFF_G + j * 128, 128), (), [(kwt, 2)])
                WLOAD(wt[:, 3], wsrc(w_in, l, OFF_G + D + j * 128, 128), (), [(kwt, 3)])

            load_wm(0)
            load_wm(1)
            it = 0
            for j in range(8):
                wt, kwt = wms[j % 2]
                for (t0, n) in blocks_q:
                    tl = list(range(t0 // 128, (t0 + n) // 128))
                    b0 = 4 * (it % 2)
                    sg, k_sg = sgs[it % 2]
                    tm, k_tm = tms[it % 2]
                    it += 1
                    srcs = ((ysT, [(k_ysT, t) for t in tl]), (yaT, [(k_yaT, t, h_) for t in tl for h_ in range(8)]),
                            (hT, [(k_hT, t) for t in tl]), (hT, [(k_hT, t) for t in tl]))
                    for wi in range(4):
                        src, sk = srcs[wi]
                        for kc in range(8):
                            MM(PS(b0 + wi, n), wt[:, wi, kc, :], src[:, kc, t0:t0 + n], kc == 0, kc == 7, [(kwt, wi)] + sk, [PSK(b0 + wi)])
                    for wi in range(2):
                        ACT(sg[:, wi, 0:n], PS(b0 + 2 + wi, n), AF.Sigmoid, [PSK(b0 + 2 + wi)], [(k_sg, wi)])
                    for wi in range(2):
                        TT("dve", tm[:, wi, 0:n], PS(b0 + wi, n), sg[:, wi, 0:n], ALU.mult, [PSK(b0 + wi), (k_sg, wi)], [(k_tm, wi)])
                    TT("pool", mT[:, j, t0:t0 + n], tm[:, 0, 0:n], tm[:, 1, 0:n], ALU.add, [(k_tm, 0), (k_tm, 1)], [(k_mT, t, j) for t in tl])
                if j + 2 < 8:
                    load_wm(j + 2)
            for k in R3.keys[1:]:
                A.free(k)
            del R3.keys[1:]
            R0.free_all()
            R1.free_all()
            R2.free_all()

            h2T, k_h2T = R0.alloc("h2T", [128, 8, NTOK], BF16)
            wo, k_wo = R1.alloc("wo", [128, 8, D], BF16)
            for hf in range(2):
                WLOAD(wo[:, :, hf * 512:(hf + 1) * 512], wsrc(w_out, l, hf * 512, 512), (), [(k_wo, hf)])
            xts = [R2.alloc("xt%d" % i, [128, D], F32) for i in range(2)]
            xns = [R2.alloc("xnew%d" % i, [128, D], F32) for i in range(2)]
            nbufs = norm_bufs(R2)
            def op_front(i):
                t = tiles_q[i]
                w_ = 1 if t < 2 else 0
                xt, kx = xts[i % 2]
                xnw, kxn_ = xns[i % 2]
                P.dma("sp", xt, xsrc(t), [("xres", t)], [kx])
                for hf in range(2):
                    b = hf + 2 * (i % 2)
                    for kc in range(8):
                        MM(PS(b), mT[:, kc, t * 128:(t + 1) * 128], wo[:, kc, hf * 512:(hf + 1) * 512], kc == 0, kc == 7,
                           [(k_mT, t, kc), (k_wo, hf)], [PSK(b)])
                    TT("dve", xnw[:, hf * 512:(hf + 1) * 512], PS(b), g1bc[:, w_, hf * 512:(hf + 1) * 512], ALU.mult,
                       [PSK(b), (k_g1bc, w_, hf)], [(kxn_, hf)])
                kxh = [(kxn_, 0), (kxn_, 1)]
                TT("pool", xnw, xnw, xt, ALU.add, kxh + [kx], kxh)
                P.dma("sp", xres[t * 128:(t + 1) * 128, :], xnw, kxh, [("xres", t)])
                norm_front(xnw, kxh, D, nbufs, i)

            op_front(0)
            for i, t in enumerate(tiles_q):
                if i + 1 < len(tiles_q):
                    op_front(i + 1)
                w_ = 1 if t < 2 else 0
                norm_back(t, scl2[:, :, w_], modT[:, 24:32, w_], [k_scl2] + [(k_modT, j) for j in range(24, 32)], h2T, k_h2T, nbufs, i)
            R1.free_all()
            R2.free_all()
            R3.free_all()
            if want("xm%d" % l):
                P.dma("sp", dbg_out.rearrange("p (t f) -> p t f", t=NT), xres.rearrange("(t p) f -> p t f", p=128),
                      [("xres", t) for t in range(NT)], ["dbg"], is_output=True)
                break
            if want("h2T%d" % l):
                dump(h2T.rearrange("p a b -> p (a b)"), 8 * NTOK, [(k_h2T, t) for t in tiles_q])
                break

            GT0 = BASE + SZ
            RG = Region(A, GT0, GT0 + NFF * NTOK * 2)
            RF = Region(A, GT0 + NFF * NTOK * 2, CAP)
            gT, k_gT = RG.alloc("gT", [128, NFF, NTOK], BF16)
            ffncw, k_fcw = RF.alloc("ffncw", [128, 2 * NFF, 3], F32)
            ffncb, k_fcb = RF.alloc("ffncb", [128, 2 * NFF], F32)
            P.dma("sp", ffncw, ffncw_in[l].rearrange("p (c k) -> p c k", k=3), (), [k_fcw])
            P.dma("sp", ffncb, ffncb_in[l], (), [k_fcb])
            Upads = [RF.alloc("Upad%d" % i, [128, NTOK + 4], F32) for i in range(2)]
            acc_, k_acc = RF.alloc("acc", [128, NTOK], F32)
            sa, k_sa = RF.alloc("sa", [128, NTOK], BF16)
            NWU = 2
            wus = [RF.alloc("wup%d" % i, [128, 8, 256], BF16) for i in range(NWU)]
            for Upad, k_U in Upads:
                MSET("pool", Upad[:, 0:1], 0.0, [(k_U, "p0")])
                MSET("pool", Upad[:, 257:259], 0.0, [(k_U, "p1")])
                MSET("pool", Upad[:, 2307:2308], 0.0, [(k_U, "p2")])

            def load_wu(j):
                wt, kwt = wus[j % NWU]
                WLOAD(wt[:, :, 0:128], wsrc(w_up, l, j * 128, 128), (), [(kwt, 0)])
                WLOAD(wt[:, :, 128:256], wsrc(w_up, l, D_FF + j * 128, 128), (), [(kwt, 1)])

            for j in range(NWU):
                load_wu(j)
            ranges = ([(0, 256, 0)] if with_ctx else []) + [(256, 2048, 258)]
            bi_q = list(range(0 if with_ctx else 1, 5))
            it = 0
            for j in range(NFF):
                wt, kwt = wus[j % NWU]
                for part in range(2):
                    cidx = j + part * NFF
                    Upad, k_U = Upads[part]
                    for bi in bi_q:
                        t0, n = BLOCKS[bi]
                        b = it % 6
                        it += 1
                        for kc in range(8):
                            MM(PS(b, n), wt[:, kc, part * 128:(part + 1) * 128], h2T[:, kc, t0:t0 + n], kc == 0, kc == 7,
                               [(kwt, part)] + [(k_h2T, t) for t in range(t0 // 128, (t0 + n) // 128)], [PSK(b)])
                        off = t0 + 1 if t0 == 0 else t0 + 3
                        ACT(Upad[:, off:off + n], PS(b, n), AF.Copy, [PSK(b)], [(k_U, bi)])
                    ukeys = [(k_U, bi) for bi in bi_q] + [(k_U, "p0"), (k_U, "p1"), (k_U, "p2")]
                    for (o0, n, u0) in ranges:
                        ka = (k_acc, o0)
                        TS("dve", acc_[:, o0:o0 + n], Upad[:, u0:u0 + n], ffncw[:, cidx, 0:1], None, ALU.mult, None, ukeys + [k_fcw], [ka])
                        STT("dve", acc_[:, o0:o0 + n], Upad[:, u0 + 1:u0 + 1 + n], ffncw[:, cidx, 1:2], acc_[:, o0:o0 + n], ALU.mult, ALU.add,
                            ukeys + [k_fcw, ka], [ka])
                        STT("dve", acc_[:, o0:o0 + n], Upad[:, u0 + 2:u0 + 2 + n], ffncw[:, cidx, 2:3], acc_[:, o0:o0 + n], ALU.mult, ALU.add,
                            ukeys + [k_fcw, ka], [ka])
                        if part == 0:
                            ACT(sa[:, o0:o0 + n], acc_[:, o0:o0 + n], AF.Silu, [ka, k_fcb], [(k_sa, o0)], bias=ffncb[:, cidx:cidx + 1])
                        else:
                            STT("dve", gT[:, j, o0:o0 + n], acc_[:, o0:o0 + n], ffncb[:, cidx:cidx + 1], sa[:, o0:o0 + n], ALU.add, ALU.mult,
                                [ka, k_fcb, (k_sa, o0)], [(k_gT, j, o0)])
                if j + NWU < NFF:
                    load_wu(j + NWU)
            RF.free_all()
            R0.free_all()
            if want("gT%d" % l):
                dump(gT.rearrange("p a b -> p (a b)"), NFF * NTOK, [(k_gT, j, o0) for j in range(NFF) for (o0, _, _) in ranges])
                break

            last = l == DEPTH - 1
            wds = [R0.alloc("wd0", [128, NFF, 512], BF16), RF.alloc("wd1", [128, NFF, 512], BF16)]
            for hf in range(2):
                for j0 in range(0, NFF, 6):
                    j1 = min(NFF, j0 + 6)
                    WLOAD(wds[hf][0][:, j0:j1, :], w_down[l].rearrange("(j p) n -> p j n", p=128)[:, j0:j1, hf * 512:(hf + 1) * 512],
                          (), [(wds[hf][1], j0)])
            xts = [RF.alloc("xt%d" % i, [128, D], F32) for i in range(2)]
            xns = [RF.alloc("xnew%d" % i, [128, D], F32) for i in range(2)]
            if last:
                fnw, k_fnw = R0.alloc("fnw", [128, D], F32)
                fjunk, k_fjunk = R0.alloc("fjunk", [128, 2, D], BF16)
                fss, k_fss = R0.alloc("fss", [128, 2, 4], F32)
                P.dma("sp", fnw, fnw_in.partition_broadcast(128), (), [k_fnw])
            for i, t in enumerate(tiles_q):
                w_ = 1 if t < 2 else 0
                xt, kx = xts[i % 2]
                xnw, kxn_ = xns[i % 2]
                P.dma("sp", xt, xres[t * 128:(t + 1) * 128, :], [("xres", t)], [kx])
                gk = [(k_gT, j, 0 if t < 2 else 256) for j in range(NFF)]
                for hf in range(2):
                    b = hf + 2 * (i % 2)
                    wd, kwd = wds[hf]
                    for j in range(NFF):
                        MM(PS(b), gT[:, j, t * 128:(t + 1) * 128], wd[:, j, :], j == 0, j == NFF - 1,
                           [(k_gT, j, 0 if t < 2 else 256), (kwd, (j // 6) * 6)], [PSK(b)])
                    TT("dve", xnw[:, hf * 512:(hf + 1) * 512], PS(b), g2bc[:, w_, hf * 512:(hf + 1) * 512], ALU.mult,
                       [PSK(b), (k_g2bc, w_, hf)], [(kxn_, hf)])
                kxh = [(kxn_, 0), (kxn_, 1)]
                TT("pool", xnw, xnw, xt, ALU.add, kxh + [kx], kxh)
                if not last:
                    P.dma("sp", xres[t * 128:(t + 1) * 128, :], xnw, kxh, [("xres", t)])
                else:
                    pi = i % 2
                    ACT(fjunk[:, pi, :], xnw, AF.Square, kxh, [(k_fjunk, pi), (k_fss, pi, 0)], accum_out=fss[:, pi, 0:1])
                    ACT(fss[:, pi, 1:2], fss[:, pi, 0:1], AF.Ln, [(k_fss, pi, 0)], [(k_fss, pi, 1)], scale=1.0 / D, bias=EPS)
                    ACT(fss[:, pi, 2:3], fss[:, pi, 1:2], AF.Exp, [(k_fss, pi, 1)], [(k_fss, pi, 2)], scale=-0.5)
                    STT("dve", xt, xnw, fss[:, pi, 2:3], fnw, ALU.mult, ALU.mult, kxh + [(k_fss, pi, 2), k_fnw, kx], [kx])
                    P.dma("sp", y_out[(t - 2) * 128:(t - 1) * 128, :], xt, [kx], [("y", t)], is_output=True)
            RF.free_all()
            R0.free_all()
            RG.free_all()
            if want("xf%d" % l):
                P.dma("sp", dbg_out.rearrange("p (t f) -> p t f", t=NT), xres.rearrange("(t p) f -> p t f", p=128),
                      [("xres", t) for t in range(NT)], ["dbg"], is_output=True)
                break

        P.finish()
        P.emit()
        print("ops", P.nops, "arena peak", A.peak)
    return nc


def rope_tables():
    inv_freq = (np.float32(10000.0) ** (-np.arange(16, dtype=np.float32) / np.float32(16))).astype(np.float32)
    t = np.arange(SEQ)
    row = (t // 64).astype(np.float32)
    col = (t % 64).astype(np.float32)
    cosT = np.zeros((128, SEQ), np.float32)
    sinT = np.zeros((128, SEQ), np.float32)
    for p in range(128):
        d = p % 64
        axis, half, f = d // 32, (d % 32) // 16, d % 16
        ang = ((row if axis == 0 else col) * inv_freq[f]).astype(np.float32)
        cosT[p] = np.cos(ang)
        sinT[p] = np.sin(ang) * (-1.0 if half == 0 else 1.0)
    return cosT, sinT


def make_in_maps(inp):
    f = lambda a: np.ascontiguousarray(np.asarray(a, dtype=np.float32))
    cosT, sinT = rope_tables()
    shared = {
        "w_mod": f(inp["w_mod"]), "b_mod": f(inp["b_mod"]),
        "bmodT": f(np.asarray(inp["b_mod"]).reshape(DEPTH, 48, 128).transpose(0, 2, 1)),
        "n1wT": f(np.asarray(inp["norm1_w"]).reshape(DEPTH, 8, 128).transpose(0, 2, 1)),
        "n2wT": f(np.asarray(inp["norm2_w"]).reshape(DEPTH, 8, 128).transpose(0, 2, 1)),
        "ssdnwT": f(np.asarray(inp["ssd_norm_w"]).reshape(DEPTH, 8, 128).transpose(0, 2, 1)),
        "sublnT": f(np.asarray(inp["att_subln_w"]).reshape(DEPTH, 128, 1)),
        "w_in": f(inp["w_in"]),
        "ssdcw": f(np.asarray(inp["ssd_conv_w"]).transpose(0, 2, 1).reshape(DEPTH, 16, 128, 3).transpose(0, 2, 1, 3).reshape(DEPTH, 128, 48)),
        "ssdcb": f(np.asarray(inp["ssd_conv_b"]).reshape(DEPTH, 16, 128).transpose(0, 2, 1)),
        "alog": f(np.asarray(inp["ssd_a_log"]).reshape(DEPTH, 32)),
        "dtb": f(np.asarray(inp["ssd_dt_bias"]).reshape(DEPTH, 32)),
        "dskip": f(inp["ssd_d"]),
        "lamv": f(np.asarray(inp["diff_lambda"]).reshape(DEPTH, 256)),
        "w_br_ssd": f(inp["w_br_ssd"]), "w_br_att": f(inp["w_br_att"]), "w_out": f(inp["w_out"]),
        "w_up": f(inp["w_up"]),
        "ffncw": f(np.asarray(inp["ffn_conv_w"]).transpose(0, 2, 1).reshape(DEPTH, 2 * NFF, 128, 3).transpose(0, 2, 1, 3).reshape(DEPTH, 128, 2 * NFF * 3)),
        "ffncb": f(np.asarray(inp["ffn_conv_b"]).reshape(DEPTH, 2 * NFF, 128).transpose(0, 2, 1)),
        "w_down": f(inp["w_down"]),
        "fnw": f(inp["final_norm_w"]),
        "ropecos": cosT, "ropesin": sinT,
    }
    x = np.asarray(inp["x"], dtype=np.float32)
    ctx = np.asarray(inp["ctx"], dtype=np.float32)
    c = np.asarray(inp["c"], dtype=np.float32)
    c_ctx = np.asarray(inp["c_ctx"], dtype=np.float32)
    maps = []
    for b in range(8):
        cT = np.stack([c[b].reshape(8, 128).T, c_ctx.reshape(8, 128).T], axis=-1).reshape(128, 16)
        m = dict(shared)
        m["x_b"] = f(x[b])
        m["ctx_b"] = f(ctx[b])
        m["cT"] = f(cT)
        maps.append(m)
    return maps


def kernel(**inputs):
    nc = build_program()
    res = run_bass_kernel_spmd(nc, make_in_maps(inputs), core_ids=list(range(8)))
    return np.stack([np.asarray(r["y"], dtype=np.float32) for r in res.results], axis=0)
```
